# Optimizing a Trainium2 kernel written in Bass

```python
import math
import jax, jax.numpy as jnp
from jax import lax
import numpy as np

D_MODEL = 1024
BATCH = 4
SEQ = 4096
DEPTH = 1

GDN_HEADS = 4
GDN_HEAD_DIM = 128
MLSTM_HEADS = 4
MLSTM_HEAD_DIM = 128
GDN_W = GDN_HEADS * GDN_HEAD_DIM
MLSTM_W = MLSTM_HEADS * MLSTM_HEAD_DIM
CONV_WIDTH = 4
GDN_CHUNK = 64
MLSTM_CHUNK = 64
PROJ_SIZES = (3 * GDN_W, GDN_W, GDN_HEADS, GDN_HEADS,
              2 * MLSTM_W, MLSTM_W, MLSTM_W, MLSTM_HEADS, MLSTM_HEADS)
PROJ_DIM = sum(PROJ_SIZES)
MEM_TOKENS = 256
XA_HEADS = 4
XA_HEAD_DIM = D_MODEL // XA_HEADS
PEER_HEADS = 8
PEER_N_KEYS = 128
PEER_N_EXPERTS = PEER_N_KEYS * PEER_N_KEYS
PEER_TOPK = 16
PEER_HALF = 128
PEER_QUERY_DIM = 2 * PEER_HALF
PEER_BLOCK = 128
NORM_EPS = 1e-6

kernel_name = 'hymba_gdn_mlstm_peer_block'


def rmsnorm(x, w):
    xf = x.astype(jnp.float32)
    y = xf * lax.rsqrt(jnp.mean(xf * xf, axis=-1, keepdims=True) + NORM_EPS)
    return (y * w.astype(jnp.float32)).astype(x.dtype)


def l2norm(x):
    return x * lax.rsqrt(jnp.sum(x * x, axis=-1, keepdims=True) + NORM_EPS)


def causal_dwconv(x, w):
    kw = w.shape[0]
    return lax.conv_general_dilated(x, w[:, None, :].astype(x.dtype), window_strides=(1,),
                                    padding=[(kw - 1, 0)],
                                    dimension_numbers=('NWC', 'WIO', 'NWC'),
                                    feature_group_count=x.shape[-1])


def to_chunks(t, nc, lc):
    b, h = t.shape[0], t.shape[1]
    return jnp.moveaxis(t.reshape(b, h, nc, lc, *t.shape[3:]), 2, 0)


def gated_delta_rule(q, k, v, log_alpha, beta):
    b, nh, s, dk = q.shape
    dv = v.shape[-1]
    lc = GDN_CHUNK
    nc = s // lc
    q = q * dk ** -0.5
    q, k, v, log_alpha, beta = (to_chunks(t, nc, lc) for t in (q, k, v, log_alpha, beta))
    gc = jnp.cumsum(log_alpha, axis=-1)
    idx = jnp.arange(lc)
    causal = idx[:, None] >= idx[None, :]
    strict = idx[:, None] > idx[None, :]
    decay = jnp.exp(jnp.where(causal, gc[..., :, None] - gc[..., None, :], -jnp.inf))
    kb = k * beta[..., None]
    a = jnp.where(strict, jnp.einsum('nbhid,nbhjd->nbhij', kb, k) * decay, 0.0)
    eye = jnp.eye(lc, dtype=q.dtype)
    t_inv = lax.linalg.triangular_solve(a + eye, jnp.broadcast_to(eye, a.shape), left_side=True,
                                        lower=True, unit_diagonal=True)
    u = t_inv @ (v * beta[..., None])
    w = t_inv @ (kb * jnp.exp(gc)[..., None])
    qk = jnp.einsum('nbhid,nbhjd->nbhij', q, k) * decay
    q_dec = q * jnp.exp(gc)[..., None]
    k_dec = k * jnp.exp(gc[..., -1:] - gc)[..., None]
    g_last = jnp.exp(gc[..., -1])

    def step(state, xs):
        u_c, w_c, qk_c, qd_c, kd_c, gl_c = xs
        v_new = u_c - w_c @ state
        o = qd_c @ state + qk_c @ v_new
        state = state * gl_c[..., None, None] + jnp.einsum('bhld,bhle->bhde', kd_c, v_new)
        return state, o

    s0 = jnp.zeros((b, nh, dk, dv), q.dtype)
    _, o = lax.scan(step, s0, (u, w, qk, q_dec, k_dec, g_last))
    return jnp.moveaxis(o, 0, 2).reshape(b, nh, s, dv)


def mlstm_chunkwise(q, k, v, log_i, log_f):
    b, nh, s, dk = q.shape
    dv = v.shape[-1]
    lc = MLSTM_CHUNK
    nc = s // lc
    k = k * dk ** -0.5
    q, k, v, log_i, log_f = (to_chunks(t, nc, lc) for t in (q, k, v, log_i, log_f))
    bcum = jnp.cumsum(log_f, axis=-1)
    idx = jnp.arange(lc)
    causal = idx[:, None] >= idx[None, :]
    d_intra = jnp.where(causal, bcum[..., :, None] - bcum[..., None, :] + log_i[..., None, :], -jnp.inf)
    d_last = bcum[..., -1:] - bcum + log_i
    qk = jnp.einsum('nbhid,nbhjd->nbhij', q, k)

    def step(carry, xs):
        c, n, m = carry
        q_c, k_c, v_c, b_c, d_c, qk_c, dl_c = xs
        inter = b_c + m[..., None]
        m_t = jnp.maximum(inter, jnp.max(d_c, axis=-1))
        p = qk_c * jnp.exp(d_c - m_t[..., None])
        sc = jnp.exp(inter - m_t)
        num = sc[..., None] * (q_c @ c) + p @ v_c
        den = sc * jnp.einsum('bhld,bhd->bhl', q_c, n) + jnp.sum(p, axis=-1)
        h = num / jnp.maximum(jnp.abs(den), jnp.exp(-m_t))[..., None]
        b_last = b_c[..., -1]
        m_new = jnp.maximum(b_last + m, jnp.max(dl_c, axis=-1))
        wgt = jnp.exp(dl_c - m_new[..., None])
        dec = jnp.exp(b_last + m - m_new)
        c = dec[..., None, None] * c + jnp.einsum('bhld,bhle->bhde', k_c * wgt[..., None], v_c)
        n = dec[..., None] * n + jnp.einsum('bhld,bhl->bhd', k_c, wgt)
        return (c, n, m_new), h

    init = (jnp.zeros((b, nh, dk, dv), q.dtype), jnp.zeros((b, nh, dk), q.dtype),
            jnp.zeros((b, nh), q.dtype))
    _, hs = lax.scan(step, init, (q, k, v, bcum, d_intra, qk, d_last))
    return jnp.moveaxis(hs, 0, 2).reshape(b, nh, s, dv)


def hybrid_mixer(h, w_in, gdn_conv_w, gdn_a_log, gdn_dt_bias, gdn_norm_w,
                 mlstm_conv_w, mlstm_i_bias, mlstm_f_bias, mlstm_norm_w, w_out):
    b, s, _ = h.shape
    f32 = jnp.float32
    proj = (h @ w_in).astype(f32)
    cuts = np.cumsum(PROJ_SIZES)[:-1].tolist()
    g_qkv, g_z, g_a, g_b, m_qk, m_v, m_o, m_i, m_f = jnp.split(proj, cuts, axis=-1)

    def heads(t, nh):
        return t.reshape(b, s, nh, -1).transpose(0, 2, 1, 3)

    g_qkv = jax.nn.silu(causal_dwconv(g_qkv, gdn_conv_w.astype(f32)))
    gq, gk, gv = (heads(t, GDN_HEADS) for t in jnp.split(g_qkv, 3, axis=-1))
    gq, gk = l2norm(gq), l2norm(gk)
    log_alpha = -jnp.exp(gdn_a_log.astype(f32)) * jax.nn.softplus(g_a + gdn_dt_bias.astype(f32))
    beta = jax.nn.sigmoid(g_b)
    go = gated_delta_rule(gq, gk, gv, log_alpha.transpose(0, 2, 1), beta.transpose(0, 2, 1))
    go = rmsnorm(go.transpose(0, 2, 1, 3), gdn_norm_w) * jax.nn.silu(
        g_z.reshape(b, s, GDN_HEADS, GDN_HEAD_DIM))

    m_qk = jax.nn.silu(causal_dwconv(m_qk, mlstm_conv_w.astype(f32)))
    mq, mk = (heads(t, MLSTM_HEADS) for t in jnp.split(m_qk, 2, axis=-1))
    mv = heads(m_v, MLSTM_HEADS)
    log_i = (m_i + mlstm_i_bias.astype(f32)).transpose(0, 2, 1)
    log_f = jax.nn.log_sigmoid(m_f + mlstm_f_bias.astype(f32)).transpose(0, 2, 1)
    mh = mlstm_chunkwise(mq, mk, mv, log_i, log_f)
    mh = rmsnorm(mh.transpose(0, 2, 1, 3), mlstm_norm_w.reshape(MLSTM_HEADS, MLSTM_HEAD_DIM)) * \
        jax.nn.sigmoid(m_o.reshape(b, s, MLSTM_HEADS, MLSTM_HEAD_DIM))

    mixed = jnp.concatenate([go.reshape(b, s, GDN_W), mh.reshape(b, s, MLSTM_W)], axis=-1)
    return mixed.astype(h.dtype) @ w_out


def memory_cross_attention(h, m, wq, wkv, wo):
    b, s, _ = h.shape
    q = (h @ wq).reshape(b, s, XA_HEADS, XA_HEAD_DIM)
    k, v = jnp.split(m @ wkv, 2, axis=-1)
    k = k.reshape(b, -1, XA_HEADS, XA_HEAD_DIM)
    v = v.reshape(b, -1, XA_HEADS, XA_HEAD_DIM)
    scores = jnp.einsum('bshd,bmhd->bhsm', q, k).astype(jnp.float32) * XA_HEAD_DIM ** -0.5
    p = jax.nn.softmax(scores, axis=-1).astype(v.dtype)
    o = jnp.einsum('bhsm,bmhd->bshd', p, v).reshape(b, s, D_MODEL)
    return o @ wo


def peer_ffn(h, wq, sub_keys, u_table, v_table):
    b, s, d = h.shape
    t = b * s
    hf = h.reshape(t, d)
    q = (hf @ wq).reshape(t, PEER_HEADS, 2, PEER_HALF)
    sc = jnp.einsum('thpd,hpnd->thpn', q, sub_keys).astype(jnp.float32)
    top_v, top_i = lax.top_k(sc, PEER_TOPK)
    kk = PEER_TOPK * PEER_TOPK
    cand = (top_v[:, :, 0, :, None] + top_v[:, :, 1, None, :]).reshape(t, PEER_HEADS, kk)
    cand_id = (top_i[:, :, 0, :, None] * PEER_N_KEYS + top_i[:, :, 1, None, :]).reshape(t, PEER_HEADS, kk)
    best_v, best_pos = lax.top_k(cand, PEER_TOPK)
    expert_id = jnp.take_along_axis(cand_id, best_pos, axis=-1)
    gate = jax.nn.softmax(best_v, axis=-1)
    nb = t // PEER_BLOCK

    def block(args):
        xb, eb, gb = args
        u = jnp.take(u_table, eb, axis=0)
        act = jax.nn.gelu(jnp.einsum('phkd,pd->phk', u, xb).astype(jnp.float32), approximate=False)
        wgt = (gb * act).astype(v_table.dtype)
        return jnp.einsum('phk,phkd->pd', wgt, jnp.take(v_table, eb, axis=0))

    out = lax.map(block, (hf.reshape(nb, PEER_BLOCK, d),
                          expert_id.reshape(nb, PEER_BLOCK, PEER_HEADS, PEER_TOPK),
                          gate.reshape(nb, PEER_BLOCK, PEER_HEADS, PEER_TOPK)))
    return out.reshape(b, s, d).astype(h.dtype)


def setup_inputs(seed: int = 0) -> dict:
    key = jax.random.key(seed)
    ks = jax.random.split(key, 24)
    f32 = jnp.float32
    L, D = DEPTH, D_MODEL

    def nrm(k, shape, scale):
        return jax.random.normal(k, shape, f32) * scale

    def gain(k, shape):
        return 1.0 + 0.01 * jax.random.normal(k, shape, f32)

    dt = jnp.exp(jax.random.uniform(ks[6], (L, GDN_HEADS), f32, math.log(1e-3), math.log(1e-1)))
    return {
        'x': nrm(ks[0], (BATCH, SEQ, D), 1.0),
        'mem': nrm(ks[1], (BATCH, MEM_TOKENS, D), 1.0),
        'norm_mix_w': gain(ks[2], (L, D)),
        'w_in': nrm(ks[3], (L, D, PROJ_DIM), D ** -0.5),
        'gdn_conv_w': nrm(ks[4], (L, CONV_WIDTH, 3 * GDN_W), CONV_WIDTH ** -0.5),
        'gdn_a_log': jnp.log(jax.random.uniform(ks[5], (L, GDN_HEADS), f32, 1.0, 16.0)),
        'gdn_dt_bias': dt + jnp.log(-jnp.expm1(-dt)),
        'gdn_norm_w': gain(ks[7], (L, GDN_HEAD_DIM)),
        'mlstm_conv_w': nrm(ks[8], (L, CONV_WIDTH, 2 * MLSTM_W), CONV_WIDTH ** -0.5),
        'mlstm_i_bias': nrm(ks[9], (L, MLSTM_HEADS), 0.5),
        'mlstm_f_bias': jax.random.uniform(ks[10], (L, MLSTM_HEADS), f32, 3.0, 6.0),
        'mlstm_norm_w': gain(ks[11], (L, MLSTM_W)),
        'w_out': nrm(ks[12], (L, GDN_W + MLSTM_W, D), (GDN_W + MLSTM_W) ** -0.5),
        'norm_xa_w': gain(ks[13], (L, D)),
        'norm_mem_w': gain(ks[14], (L, D)),
        'xa_wq': nrm(ks[15], (L, D, D), D ** -0.5),
        'xa_wkv': nrm(ks[16], (L, D, 2 * D), D ** -0.5),
        'xa_wo': nrm(ks[17], (L, D, D), D ** -0.5),
        'norm_ffn_w': gain(ks[18], (L, D)),
        'peer_wq': nrm(ks[19], (L, D, PEER_HEADS * PEER_QUERY_DIM), D ** -0.5),
        'peer_sub_keys': nrm(ks[20], (L, PEER_HEADS, 2, PEER_N_KEYS, PEER_HALF), PEER_HALF ** -0.5),
        'peer_u': nrm(ks[21], (L, PEER_N_EXPERTS, D), D ** -0.5),
        'peer_v': nrm(ks[22], (L, PEER_N_EXPERTS, D), (PEER_HEADS * PEER_TOPK) ** -0.5),
        'norm_final_w': gain(ks[23], (D,)),
    }


def reference(x, mem, norm_mix_w, w_in, gdn_conv_w, gdn_a_log, gdn_dt_bias, gdn_norm_w,
              mlstm_conv_w, mlstm_i_bias, mlstm_f_bias, mlstm_norm_w, w_out,
              norm_xa_w, norm_mem_w, xa_wq, xa_wkv, xa_wo,
              norm_ffn_w, peer_wq, peer_sub_keys, peer_u, peer_v, norm_final_w):
    for l in range(DEPTH):
        x = x + hybrid_mixer(rmsnorm(x, norm_mix_w[l]), w_in[l], gdn_conv_w[l], gdn_a_log[l],
                             gdn_dt_bias[l], gdn_norm_w[l], mlstm_conv_w[l], mlstm_i_bias[l],
                             mlstm_f_bias[l], mlstm_norm_w[l], w_out[l])
        x = x + memory_cross_attention(rmsnorm(x, norm_xa_w[l]), rmsnorm(mem, norm_mem_w[l]),
                                       xa_wq[l], xa_wkv[l], xa_wo[l])
        x = x + peer_ffn(rmsnorm(x, norm_ffn_w[l]), peer_wq[l], peer_sub_keys[l], peer_u[l], peer_v[l])
    return rmsnorm(x, norm_final_w)
```

```python
import numpy as np
from contextlib import ExitStack
import concourse.bass as bass
import concourse.mybir as mybir
from concourse.bass_utils import run_bass_kernel_spmd

F32 = mybir.dt.float32
BF = mybir.dt.bfloat16
U32 = mybir.dt.uint32
I32 = mybir.dt.int32
AF = mybir.ActivationFunctionType
ALU = mybir.AluOpType
AX = mybir.AxisListType

ENGS = ("sp", "act", "dve", "pool", "pe")
D = 1024
NEG = -30000.0
EPS = 1e-6


class Sch:
    def __init__(self):
        self.ops = {e: [] for e in ENGS}
        self.ccount = {e: 0 for e in ENGS}
        self.dcount = {}
        self.seen = {e: {} for e in ENGS}
        self.waited = {}
        self.lastw = {}
        self.readers = {}
        self.cap = None

    def replay(self, items):
        for it in items:
            self.add(*it)

    def add(self, eng, fn, r=(), w=(), dq=None, chain=False, extra=()):
        if self.cap is not None:
            self.cap.append((eng, fn, list(r), list(w), dq, chain, list(extra)))
            return
        r = list(r)
        w = list(w)
        if dq and chain:
            w.append("__chain_" + dq)
        if dq:
            idx = self.dcount.get(dq, 0)
            self.dcount[dq] = idx + 1
            q = dq
        else:
            idx = self.ccount[eng]
            if fn is not None:
                self.ccount[eng] += 1
            q = eng
        deps = {}

        def need(d):
            if d is None:
                return
            if deps.get(d[0], -1) < d[1]:
                deps[d[0]] = d[1]

        for x in r:
            need(self.lastw.get(x))
            if isinstance(x, str) and x.startswith("ps"):
                for d in self.readers.get(x, ()):
                    if d[0] != q:
                        need(d)
        for x in w:
            need(self.lastw.get(x))
            for d in self.readers.get(x, ()):
                need(d)
        for d in extra:
            need(d)
        waits = []
        for dqn, di in deps.items():
            if dqn == "pe" and eng == "pe" and not dq:
                continue
            if self.seen[eng].get(dqn, -1) >= di:
                continue
            self.seen[eng][dqn] = di
            waits.append((dqn, di))
            self.waited.setdefault(dqn, set()).add(di)
        self.ops[eng].append((fn, waits, q, idx, bool(dq)))
        if fn is None:
            return
        me = (q, idx)
        for x in w:
            self.lastw[x] = me
            self.readers[x] = []
        for x in r:
            self.readers.setdefault(x, []).append(me)

    def barrier(self):
        ext = []
        for e in ENGS:
            if self.ccount[e] > 0:
                ext.append((e, self.ccount[e] - 1))
        for q, n in self.dcount.items():
            ext.append((q, n - 1))
        for e in ENGS:
            self.add(e, None, extra=ext)

    def emit(self, nc, es):
        queues = set(self.waited.keys()) | set(self.dcount.keys())
        sem = {q: es.enter_context(nc.semaphore("s_" + q)) for q in sorted(queues)}
        val = {}
        for q, s in self.waited.items():
            if q in self.dcount:
                continue
            for rank, i in enumerate(sorted(s)):
                val[(q, i)] = rank + 1

        def value(q, i):
            if q in self.dcount:
                return 16 * (i + 1)
            return val[(q, i)]

        def runner(en):
            def f(e):
                for fn, waits, q, idx, isdma in self.ops[en]:
                    for (wq, wi) in waits:
                        e.wait_ge(sem[wq], value(wq, wi))
                    if fn is None:
                        continue
                    ins = fn(e)
                    if isdma:
                        ins.then_inc(sem[q], 16)
                    elif (q, idx) in val:
                        ins.then_inc(sem[q], 1)
            return f

        with nc.Block() as block:
            block.sync(runner("sp"))
            block.scalar(runner("act"))
            block.vector(runner("dve"))
            block.gpsimd(runner("pool"))
            block.tensor(runner("pe"))


def bc(ap, shape):
    return ap.to_broadcast(list(shape))


def build(NPREV, NOWN, phases=("mix", "mixout", "xa", "peer", "final")):
    NT = NPREV + NOWN
    nc = bass.Bass("TRN2", target_bir_lowering=False)
    dr = lambda n, s, dt, k="ExternalInput": nc.dram_tensor(n, list(s), dt, kind=k).ap()
    xs = dr("xs", [NT * 128, D], F32)
    memd = dr("mem", [256, D], F32)
    w_in_d = dr("w_in", [128, 8, 4112], F32)
    w_out_d = dr("w_out", [128, 8, 1024], F32)
    wq_d = dr("xa_wq", [128, 8, 1024], F32)
    wkv_d = dr("xa_wkv", [128, 8, 2048], F32)
    wo_d = dr("xa_wo", [128, 8, 1024], F32)
    pwq_d = dr("peer_wq", [128, 8, 2048], F32)
    subk_d = dr("subk", [128, 16, 128], F32)
    convw_d = dr("convw", [128, 80], F32)
    pvec_d = dr("pvec", [128, 40], F32)
    rvec_d = dr("rvec", [128, 2064], F32)
    cst_d = dr("cst", [128, 1040], F32)
    puv_d = dr("peer_uv", [16384, 2 * D], F32)
    outd = dr("out", [NOWN * 128, D], F32, "ExternalOutput")
    uvb_d = nc.dram_tensor("uvb", [16384, 2 * D], BF, kind="Internal").ap()

    S = Sch()
    es = ExitStack()
    with es:
        sb = lambda n, s, dt: es.enter_context(nc.sbuf_tensor("sb_" + n, list(s), dt))
        ARW = 49152
        AR = sb("arena", [128, ARW], F32)
        ARB = AR.bitcast(BF)
        ARI = AR.bitcast(I32)
        ARU = AR.bitcast(U32)
        cst = sb("cst", [128, 1040], F32)
        pvec = sb("pvec", [128, 40], F32)
        rvec = sb("rvec", [128, 2064], F32)
        identb = sb("identb", [128, 128], BF)
        small = sb("small", [128, 256], F32)
        PS = [es.enter_context(nc.psum_tensor("ps%d" % i, [128, 512], F32)) for i in range(8)]
        PSB = [p.bitcast(BF) for p in PS]

        class Bump:
            def __init__(self, lo, hi):
                self.lo, self.hi, self.p = lo, hi, lo

            def f32(self, shape):
                n = int(np.prod(shape))
                o = self.p
                self.p += n
                assert self.p <= self.hi, ("arena overflow", self.p, self.hi)
                v = AR[:, o:o + n]
                return self._shape(v, shape)

            def _shape(self, v, shape):
                if len(shape) == 1:
                    return v
                if len(shape) == 2:
                    return v.rearrange("p (a b) -> p a b", a=shape[0], b=shape[1])
                return v.rearrange("p (a b c) -> p a b c", a=shape[0], b=shape[1], c=shape[2])

            def bf(self, shape):
                n = int(np.prod(shape))
                nw = (n + 1) // 2
                o = self.p
                self.p += nw
                assert self.p <= self.hi, ("arena overflow", self.p, self.hi)
                return self._shape(ARB[:, 2 * o:2 * o + n], shape)

            def i32(self, shape, u=False):
                n = int(np.prod(shape))
                o = self.p
                self.p += n
                assert self.p <= self.hi
                return self._shape((ARU if u else ARI)[:, o:o + n], shape)

        ident = cst[:, 0:128]
        tri = cst[:, 128:256]
        rev = cst[:, 256:384]
        ci = [cst[:, 384:512], cst[:, 512:640]]
        negones = cst[:, 640:768]
        maskL = cst[:, 768:896]
        maskU = cst[:, 896:1024]
        iota16 = cst[:, 1024:1040]

        rot = {"i": 0, "banks": [0, 1, 2, 3, 4]}

        def nb():
            b = rot["banks"][rot["i"] % len(rot["banks"])]
            rot["i"] += 1
            return b

        def psn(b):
            return "ps%d" % b

        def dve(fn, r, w):
            S.add("dve", fn, r, w)

        def act(fn, r, w):
            S.add("act", fn, r, w)

        def pe(fn, r, w):
            S.add("pe", fn, r, w)

        def mm(out, lhsT, rhs, start, stop, r, w):
            pe(lambda e: e.matmul(out, lhsT=lhsT, rhs=rhs, start=start, stop=stop), r, w)

        def tr(out, in_, r, w):
            pe(lambda e: e.transpose(out=out, in_=in_, identity=identb[:, :]), list(r) + ["identb"], w)

        def tr2(out, in_, r, w):
            mm(out, in_, identb[:, :], True, True, list(r) + ["identb"], w)

        def tt(out, in0, in1, op, r, w):
            dve(lambda e: e.tensor_tensor(out=out, in0=in0, in1=in1, op=op), r, w)

        def ts(out, in0, s1, op0, r, w, s2=None, op1=None):
            if op1 is None:
                dve(lambda e: e.tensor_scalar(out=out, in0=in0, scalar1=s1, scalar2=None, op0=op0), r, w)
            else:
                dve(lambda e: e.tensor_scalar(out=out, in0=in0, scalar1=s1, scalar2=s2, op0=op0, op1=op1), r, w)

        def stt(out, in0, scalar, in1, op0, op1, r, w, accum=None):
            if accum is None:
                dve(lambda e: e.scalar_tensor_tensor(out=out, in0=in0, scalar=scalar, in1=in1, op0=op0, op1=op1), r, w)
            else:
                dve(lambda e: e.scalar_tensor_tensor(out=out, in0=in0, scalar=scalar, in1=in1, op0=op0, op1=op1,
                                                     accum_out=accum), r, w)

        def cp(out, in_, r, w, eng="dve"):
            if eng == "dve":
                dve(lambda e: e.tensor_copy(out=out, in_=in_), r, w)
            else:
                act(lambda e: e.copy(out=out, in_=in_), r, w)

        def actf(out, in_, func, r, w, bias=None, scale=None, accum=None):
            kw = {}
            if bias is not None:
                kw["bias"] = bias
            if scale is not None:
                kw["scale"] = scale
            if accum is not None:
                kw["accum_out"] = accum
            act(lambda e: e.activation(out=out, in_=in_, func=func, **kw), r, w)

        def memset(ap, v, w):
            dve(lambda e: e.memset(ap, v), [], w)

        def dma(eng, q, out, in_, r, w, chain=False):
            S.add(eng, lambda e: e.dma_start(out=out, in_=in_), r, w, dq=q, chain=chain)

        def load_w(q, dst3, src3, name, ncols):
            step = 1024
            for kc in range(8):
                for c0 in range(0, ncols, step):
                    c1 = min(ncols, c0 + step)
                    dma("pool", q, dst3[:, kc, c0:c1], src3[:, kc, c0:c1], [], [name])

        dma("sp", "ld_c", cst[:, :], cst_d[:, :], [], ["cst"])
        dma("sp", "ld_pv", pvec[:, :], pvec_d[:, :], [], ["pvec"])
        dma("sp", "ld_rv", rvec[:, :], rvec_d[:, :], [], ["rvec"])
        cp(identb[:, :], ident, ["cst"], ["identb"])
        negA = small[:, 0:4]
        ibp = small[:, 4:8]
        actf(negA, rvec[:, 0:4], AF.Exp, ["rvec"], ["negA"])
        ts(negA, negA, -1.0, ALU.mult, ["negA"], ["negA"])
        ts(ibp, rvec[:, 8:12], float(np.log(128.0 ** -0.5)), ALU.add, ["rvec"], ["ibp"])

        def cast_tables(dep=()):
            if "peer" in phases:
                for i in range(64):
                    dma("pool", "cvt", uvb_d[i * 256:(i + 1) * 256, :], puv_d[i * 256:(i + 1) * 256, :], list(dep), ["uvb"])

        def rms_T(xap, xname, pvcol, xn, hT, tagp, nbf=None):
            ss = small[:, 16:17]
            rstd = small[:, 17:18]
            junk = xn
            actf(junk, xap, AF.Square, [xname], ["xn", "ss"], scale=1.0 / 32.0, accum=ss)
            actf(ss, ss, AF.Sqrt, ["ss"], ["ss"], bias=EPS, scale=1.0)
            dve(lambda e: e.reciprocal(out=rstd, in_=ss), ["ss"], ["rstd"])
            ts(xn, xap, rstd, ALU.mult, [xname, "rstd"], ["xn"])
            b = (nbf or nb)()
            for kc in range(8):
                tr(PSB[b][:, kc * 128:(kc + 1) * 128], xn[:, kc * 128:(kc + 1) * 128], ["xn"], [psn(b)])
            tt(hT, PSB[b][:, :].rearrange("p (a b) -> p a b", a=8, b=128),
               bc(pvec[:, pvcol:pvcol + 8].unsqueeze(2), [128, 8, 128]), ALU.mult, [psn(b), "pvec"], ["hT"])

        MIXT_OFF = 24512
        mixTs = ARB[:, 2 * MIXT_OFF:2 * MIXT_OFF + 16 * 1024].rearrange("p (t a b) -> p t a b", t=16, a=8, b=128)
        if "mix" in phases:
            WIN_OFF = MIXT_OFF + 8192
            w_in = ARB[:, 2 * WIN_OFF:2 * WIN_OFF + 8 * 4112].rearrange("p (a b) -> p a b", a=8, b=4112)
            load_w("ld_win", w_in, w_in_d, "w_in", 4112)
            B0 = Bump(0, MIXT_OFF)
            B1 = B0
            convw = B1.f32([80])
            dma("sp", "ld_cw", convw, convw_d[:, :], [], ["convw"])
            diag = B1.bf([80, 128])
            for i in range(80):
                ts(diag[:, i, :], ident, convw[:, i:i + 1], ALU.mult, ["cst", "convw"], ["diag"])
            xin = [B0.f32([D])]
            xn = B0.bf([D])
            hT = B0.bf([8, 128])
            pre = B0.bf([20, 131])
            cvA = B0.f32([8, 128])
            cvV = B0.f32([4, 128])
            sq = B0.f32([8, 128])
            R1 = sq[:, 0:4, :]
            R2 = sq[:, 4:8, :]
            tmpM = B0.f32([4, 128])
            E1 = B0.f32([4, 128])
            wT = B0.bf([4, 128])
            qkT = B0.bf([4, 128])
            E2g = B0.f32([4, 128])
            E2m = E2g
            qkn = B0.bf([8, 128])
            kbe = B0.bf([4, 128])
            vb = B0.bf([4, 128])
            kdm = B0.bf([2, 4, 128])
            qd = B0.bf([4, 128])
            kqT = B0.bf([8, 128])
            knT = kqT[:, 0:4, :]
            qnT = kqT[:, 4:8, :]
            qdm = B0.bf([2, 4, 128])
            Pb = [B0.bf([4, 128]), B0.bf([4, 128])]
            PTb = [B0.bf([4, 128]), B0.bf([4, 128])]
            R32 = B0.f32([4, 128])
            Rbf = B0.bf([4, 128])
            u32 = B0.f32([4, 128])
            vnew = B0.bf([4, 128])
            vnew2 = [B0.bf([4, 128]), B0.bf([4, 128])]
            S32 = B0.f32([4, 128])
            Sbf2 = [B0.bf([4, 128]), B0.bf([4, 128])]
            mqk_bf = qkn
            mqF = qd
            kwm = B0.bf([2, 4, 128])
            vaug = B0.bf([4, 130])
            mqkT = kqT
            qFm = B0.bf([2, 4, 128])
            PTm = B0.bf([4, 128])
            Cm32 = B0.f32([4, 129])
            Cmbf2 = [B0.bf([4, 130]), B0.bf([4, 130])]
            hm = R32
            o32 = u32
            zs = B1.f32([4, 128])
            so = B1.f32([4, 128])
            mixed = B1.bf([8, 128])
            g16 = B1.f32([16])
            T12 = B1.f32([12])
            E12 = B1.f32([12])
            L8 = B1.f32([8])
            LA = B1.f32([8])
            beta = B1.f32([4])
            nbeta = B1.f32([4])
            bEgc = B1.f32([4])
            lip = B1.f32([4])
            gcs = B1.f32([32])
            EGC = B1.f32([8])
            EREV = B1.f32([8])
            EGL2 = [B1.f32([16]), B1.f32([16])]
            ssq = B1.f32([8])
            rinv = B1.f32([8])
            den = B1.f32([4])
            ms4 = B1.f32([4])

            for (ap, nm) in ((pre, "pre"), (kdm, "kdm"), (qdm, "qdm"), (kwm, "kwm"), (qFm, "qFm"), (vnew, "vnew"), (vnew2[0], "vnew0"), (vnew2[1], "vnew1"),
                             (S32, "S32"), (Sbf2[0], "Sbf0"), (Sbf2[1], "Sbf1"), (Cm32, "Cm32"), (Cmbf2[0], "Cmbf0"), (Cmbf2[1], "Cmbf1")):
                memset(ap, 0.0, [nm])
            memset(vaug, 1.0, ["vaug"])

            OB, HX, HY = 5, 6, 7

            def hview(b):
                return PS[b][:, :].rearrange("p (a b) -> p a b", a=4, b=128)

            def hview2(b):
                return PS[b][:, :].rearrange("p (a b) -> p a b", a=2, b=256)

            roth = {"i": 0}

            def nbh():
                b = roth["i"] % 2
                roth["i"] += 1
                return b

            rot["banks"] = [2, 3, 4]

            def Mmat(lo, dest_ps):
                cp(R1, bc(LA[:, lo:lo + 4].unsqueeze(2), [128, 4, 128]), ["LA"], ["R1"])
                tt(R2, bc(tri.unsqueeze(1), [128, 4, 128]), bc(LA[:, lo:lo + 4].unsqueeze(2), [128, 4, 128]),
                   ALU.mult, ["cst", "LA"], ["R2"])
                mm(PS[dest_ps][:, :], tri, R1[:, :, :].rearrange("p a b -> p (a b)"), True, False, ["cst", "R1"], [psn(dest_ps)])
                mm(PS[dest_ps][:, :], negones, R2[:, :, :].rearrange("p a b -> p (a b)"), False, True, ["cst", "R2"], [psn(dest_ps)])

            def head(t):
                xt = xin[0]
                xnm = "xin0"
                dma("sp", "ld_x0", xt, xs[t * 128:(t + 1) * 128, :], [], [xnm])
                rms_T(xt, xnm, 0, xn, hT, "m", nbf=nbh)
                if t > 0:
                    cp(pre[:, :, 0:3], pre[:, :, 128:131], ["pre"], ["pre"])
                groups = [0, 1, 2, 3, 4] if t >= NPREV - 1 else [1, 2, 4]
                for g in groups:
                    b = nbh()
                    for h in range(4):
                        blk = g * 4 + h
                        for kc in range(8):
                            mm(PS[b][:, h * 128:(h + 1) * 128], w_in[:, kc, blk * 128:(blk + 1) * 128], hT[:, kc, :],
                               kc == 0, kc == 7, ["w_in", "hT"], [psn(b)])
                    cp(pre[:, g * 4:(g + 1) * 4, 3:131], hview(b), [psn(b)], ["pre"], eng="act")
                full = t >= NPREV
                bg = nbh()
                for kc in range(8):
                    mm(PS[bg][:, 0:16], hT[:, kc, :], w_in[:, kc, 4096:4112], kc == 0, kc == 7, ["w_in", "hT"], [psn(bg)])
                cp(g16, PS[bg][:, 0:16], [psn(bg)], ["g16"])
                tt(T12[:, 0:4], g16[:, 0:4], rvec[:, 4:8], ALU.add, ["g16", "rvec"], ["T12"])
                stt(T12[:, 4:8], g16[:, 12:16], 1.0, rvec[:, 12:16], ALU.mult, ALU.add, ["g16", "rvec"], ["T12"])
                ts(T12[:, 4:8], T12[:, 4:8], -1.0, ALU.mult, ["T12"], ["T12"])
                ts(T12[:, 8:12], g16[:, 4:8], -1.0, ALU.mult, ["g16"], ["T12"])
                actf(E12, T12, AF.Exp, ["T12"], ["E12"])
                actf(L8, E12[:, 0:8], AF.Ln, ["E12"], ["L8"], bias=1.0)
                tt(LA[:, 0:4], L8[:, 0:4], negA, ALU.mult, ["L8", "negA"], ["LA"])
                ts(LA[:, 4:8], L8[:, 4:8], -1.0, ALU.mult, ["L8"], ["LA"])
                ts(beta, E12[:, 8:12], 1.0, ALU.add, ["E12"], ["beta"])
                dve(lambda e: e.reciprocal(out=beta, in_=beta), ["beta"], ["beta"])
                ts(nbeta, beta, -1.0, ALU.mult, ["beta"], ["nbeta"])
                tt(lip, g16[:, 8:12], ibp, ALU.add, ["g16", "ibp"], ["lip"])
                bq = nbh()
                mm(PS[bq][:, 0:8], tri, LA, True, True, ["cst", "LA"], [psn(bq)])
                mm(PS[bq][:, 8:16], rev, LA, True, True, ["cst", "LA"], [psn(bq)])
                mm(PS[bq][:, 16:24], ci[0], LA, True, True, ["cst", "LA"], [psn(bq)])
                mm(PS[bq][:, 24:32], ci[1], LA, True, True, ["cst", "LA"], [psn(bq)])
                cp(gcs, PS[bq][:, 0:32], [psn(bq)], ["gcs"])
                tt(gcs[:, 12:16], gcs[:, 12:16], lip, ALU.add, ["gcs", "lip"], ["gcs"])
                actf(EGC, gcs[:, 0:8], AF.Exp, ["gcs"], ["EGC"])
                actf(EREV, gcs[:, 8:16], AF.Exp, ["gcs"], ["EREV"])
                actf(EGL2[t % 2], gcs[:, 16:32], AF.Exp, ["gcs"], ["EGL%d" % (t % 2)])
                tt(bEgc, beta, EGC[:, 0:4], ALU.mult, ["beta", "EGC"], ["bEgc"])

                bM = nbh()
                Mmat(0, bM)
                tt(tmpM, hview(bM), bc(maskL.unsqueeze(1), [128, 4, 128]), ALU.add, [psn(bM), "cst"], ["tmpM"])
                actf(E1, tmpM, AF.Exp, ["tmpM"], ["E1"])
                if full:
                    stt(tmpM, hview(bM), -1.0, bc(maskU.unsqueeze(1), [128, 4, 128]), ALU.mult, ALU.add, [psn(bM), "cst"], ["tmpM"])
                    actf(E2g, tmpM, AF.Exp, ["tmpM"], ["E2g"])


            def tail(t, full):
                for c in range(2):
                    Sbf, Sn = Sbf2[c], "Sbf%d" % c
                    Sbo, Son = Sbf2[1 - c], "Sbf%d" % (1 - c)
                    Cbo, Con = Cmbf2[1 - c], "Cmbf%d" % (1 - c)
                    bws = nb()
                    for h in range(4):
                        mm(PS[bws][:, h * 128:(h + 1) * 128], wT[:, h, :], Sbf[:, h, :], True, True, ["E1", Sn], [psn(bws)])
                    if full and c == 0:
                        for h in range(4):
                            mm(PS[OB][:, h * 128:(h + 1) * 128], qdm[:, 0, h, :], Sbf[:, h, :], True, True, ["qdm", Sn], [psn(OB)])
                        for h in range(4):
                            hb = HX if h < 2 else HY
                            mm(hview2(hb)[:, h % 2, 0:129], qFm[:, 0, h, :], Cmbf2[0][:, h, 0:129], True, True, ["qFm", "Cmbf0"], [psn(hb)])
                    vc = vnew2[c]
                    tt(vc, u32, hview(bws), ALU.subtract, ["u32", psn(bws)], ["vnew%d" % c])
                    bds = nb()
                    for h in range(4):
                        mm(PS[bds][:, h * 128:(h + 1) * 128], kdm[:, c, h, :], vc[:, h, :], True, True, ["kdm", "vnew%d" % c], [psn(bds)])
                    tt(S32, S32, bc(EGL2[t % 2][:, c * 8:c * 8 + 4].unsqueeze(2), [128, 4, 128]), ALU.mult, ["S32", "EGL%d" % (t % 2)], ["S32"])
                    tt(S32, S32, hview(bds), ALU.add, ["S32", psn(bds)], ["S32"])
                    cp(Sbo, S32, ["S32"], [Son], eng="act")
                    bc1, bc2 = nb(), nb()
                    for h in range(4):
                        hb = bc1 if h < 2 else bc2
                        mm(hview2(hb)[:, h % 2, 0:129], kwm[:, c, h, :], vaug[:, h, 0:129], True, True, ["kwm", "vaug"], [psn(hb)])
                    tt(Cm32, Cm32, bc(EGL2[t % 2][:, c * 8 + 4:c * 8 + 8].unsqueeze(2), [128, 4, 129]), ALU.mult, ["Cm32", "EGL%d" % (t % 2)], ["Cm32"])
                    tt(Cm32[:, 0:2, :], Cm32[:, 0:2, :], hview2(bc1)[:, :, 0:129], ALU.add, ["Cm32", psn(bc1)], ["Cm32"])
                    tt(Cm32[:, 2:4, :], Cm32[:, 2:4, :], hview2(bc2)[:, :, 0:129], ALU.add, ["Cm32", psn(bc2)], ["Cm32"])
                    cp(Cbo[:, :, 0:129], Cm32, ["Cm32"], [Con], eng="act")
                if not full:
                    return
                ts(vnew, vnew2[0], ci[0][:, 0:1], ALU.mult, ["vnew0", "cst"], ["vnew"])
                stt(vnew, vnew2[1], ci[1][:, 0:1], vnew, ALU.mult, ALU.add, ["vnew1", "cst", "vnew"], ["vnew"])
                bo2 = nb()
                for h in range(4):
                    mm(PS[bo2][:, h * 128:(h + 1) * 128], qdm[:, 1, h, :], Sbf2[1][:, h, :], True, False, ["qdm", "Sbf1"], [psn(bo2)])
                    mm(PS[bo2][:, h * 128:(h + 1) * 128], qkT[:, h, :], vnew[:, h, :], False, True, ["E1", "vnew"], [psn(bo2)])
                bh2 = [nb(), nb()]
                for h in range(4):
                    hb = bh2[h // 2]
                    mm(hview2(hb)[:, h % 2, 0:129], qFm[:, 1, h, :], Cmbf2[1][:, h, 0:129], True, False, ["qFm", "Cmbf1"], [psn(hb)])
                    mm(hview2(hb)[:, h % 2, 0:129], PTm[:, h, :], vaug[:, h, 0:129], False, True, ["PTm", "vaug"], [psn(hb)])
                cp(o32, hview(OB), [psn(OB)], ["u32"], eng="act")
                tt(o32, o32, hview(bo2), ALU.add, ["u32", psn(bo2)], ["u32"])
                tt(cvA[:, 0:4, :], o32, o32, ALU.mult, ["u32"], ["cvA"])
                dve(lambda e: e.tensor_reduce(out=ms4, in_=cvA[:, 0:4, :], axis=AX.X, op=ALU.add), ["cvA"], ["ms4"])
                actf(ms4, ms4, AF.Sqrt, ["ms4"], ["ms4"], bias=EPS, scale=1.0 / 128.0)
                dve(lambda e: e.reciprocal(out=ms4, in_=ms4), ["ms4"], ["ms4"])
                tt(o32, o32, bc(ms4.unsqueeze(2), [128, 4, 128]), ALU.mult, ["u32", "ms4"], ["u32"])
                tt(mixed[:, 0:4, :], o32, zs, ALU.mult, ["u32", "zs"], ["mixed"])
                for hb, hb2, h0 in ((HX, bh2[0], 0), (HY, bh2[1], 2)):
                    cp(hm[:, h0:h0 + 2, :], hview2(hb)[:, :, 0:128], [psn(hb)], ["R32"], eng="act")
                    cp(den[:, h0:h0 + 2], hview2(hb)[:, :, 128], [psn(hb)], ["den"], eng="act")
                    tt(hm[:, h0:h0 + 2, :], hm[:, h0:h0 + 2, :], hview2(hb2)[:, :, 0:128], ALU.add, ["R32", psn(hb2)], ["R32"])
                    tt(den[:, h0:h0 + 2], den[:, h0:h0 + 2], hview2(hb2)[:, :, 128], ALU.add, ["den", psn(hb2)], ["den"])
                stt(den, den, -1.0, den, ALU.mult, ALU.max, ["den"], ["den"])
                ts(den, den, 1.0, ALU.max, ["den"], ["den"])
                dve(lambda e: e.reciprocal(out=den, in_=den), ["den"], ["den"])
                tt(hm, hm, bc(den.unsqueeze(2), [128, 4, 128]), ALU.mult, ["R32", "den"], ["R32"])
                tt(cvA[:, 0:4, :], hm, hm, ALU.mult, ["R32"], ["cvA"])
                dve(lambda e: e.tensor_reduce(out=ms4, in_=cvA[:, 0:4, :], axis=AX.X, op=ALU.add), ["cvA"], ["ms4"])
                actf(ms4, ms4, AF.Sqrt, ["ms4"], ["ms4"], bias=EPS, scale=1.0 / 128.0)
                dve(lambda e: e.reciprocal(out=ms4, in_=ms4), ["ms4"], ["ms4"])
                tt(hm, hm, bc(ms4.unsqueeze(2), [128, 4, 128]), ALU.mult, ["R32", "ms4"], ["R32"])
                tt(mixed[:, 4:8, :], hm, so, ALU.mult, ["R32", "so"], ["mixed"])
                bmx = [nb(), nb()]
                for kc in range(8):
                    tr2(PS[bmx[kc // 4]][:, (kc % 4) * 128:(kc % 4 + 1) * 128], mixed[:, kc, :], ["mixed"], [psn(bmx[kc // 4])])
                to = t - NPREV
                for g2 in range(2):
                    tt(mixTs[:, to, g2 * 4:(g2 + 1) * 4, :], hview(bmx[g2]),
                       bc(pvec[:, 8 + g2 * 4:12 + g2 * 4].unsqueeze(2), [128, 4, 128]), ALU.mult, [psn(bmx[g2]), "pvec"], ["mixT%d" % to])

            def merge(a, b):
                out, i, j = [], 0, 0
                while i < len(a) or j < len(b):
                    if j >= len(b) or (i < len(a) and i * len(b) <= j * len(a)):
                        out.append(a[i]); i += 1
                    else:
                        out.append(b[j]); j += 1
                return out

            head(0)
            for t in range(NT):
                full = t >= NPREV
                bz = bmo = None
                if full:
                    bz = nb()
                    for kc in range(8):
                        mm(PS[bz][:, :], hT[:, kc, :], w_in[:, kc, 2560:3072], kc == 0, kc == 7, ["w_in", "hT"], [psn(bz)])
                    actf(zs, hview(bz), AF.Silu, [psn(bz)], ["zs"])
                    bmo = nb()
                    for kc in range(8):
                        mm(PS[bmo][:, :], hT[:, kc, :], w_in[:, kc, 3584:4096], kc == 0, kc == 7, ["w_in", "hT"], [psn(bmo)])
                    actf(so, hview(bmo), AF.Sigmoid, [psn(bmo)], ["so"])
                bmv = nb()
                for kc in range(8):
                    mm(PS[bmv][:, :], hT[:, kc, :], w_in[:, kc, 3072:3584], kc == 0, kc == 7, ["w_in", "hT"], [psn(bmv)])
                cp(vaug[:, :, 0:128], hview(bmv), [psn(bmv)], ["vaug"], eng="act")
                def conv_group(g, dst, dname):
                    b = nb()
                    for h in range(4):
                        blk = g * 4 + h
                        for tap in range(4):
                            mm(PS[b][:, h * 128:(h + 1) * 128], pre[:, blk, tap:tap + 128], diag[:, blk * 4 + tap, :],
                               tap == 0, tap == 3, ["pre", "diag"], [psn(b)])
                    actf(dst, hview(b), AF.Silu, [psn(b)], [dname])

                if full:
                    conv_group(0, cvA[:, 0:4, :], "cvA")
                conv_group(1, cvA[:, 4:8, :], "cvA")
                conv_group(2, cvV, "cvV")
                lo = 0 if full else 4
                tt(sq[:, lo:8, :], cvA[:, lo:8, :], cvA[:, lo:8, :], ALU.mult, ["cvA"], ["R1", "R2"])
                dve(lambda e, lo=lo: e.tensor_reduce(out=ssq[:, lo:8], in_=sq[:, lo:8, :], axis=AX.X, op=ALU.add), ["R1", "R2"], ["ssq"])
                actf(ssq[:, lo:8], ssq[:, lo:8], AF.Sqrt, ["ssq"], ["ssq"], bias=EPS, scale=1.0)
                dve(lambda e, lo=lo: e.reciprocal(out=rinv[:, lo:8], in_=ssq[:, lo:8]), ["ssq"], ["rinv"])
                if full:
                    ts(rinv[:, 0:4], rinv[:, 0:4], float(128.0 ** -0.5), ALU.mult, ["rinv"], ["rinv"])
                tt(qkn[:, lo:8, :], cvA[:, lo:8, :], bc(rinv[:, lo:8].unsqueeze(2), [128, 8 - lo, 128]), ALU.mult,
                   ["cvA", "rinv"], ["qkn"])
                kn = qkn[:, 4:8, :]
                tt(kbe, kn, bc(bEgc.unsqueeze(2), [128, 4, 128]), ALU.mult, ["qkn", "bEgc"], ["kbe"])
                tt(vb, cvV, bc(beta.unsqueeze(2), [128, 4, 128]), ALU.mult, ["cvV", "beta"], ["vb"])
                for c in range(2):
                    stt(kdm[:, c, :, :], kn, ci[c][:, 0:1], bc(EREV[:, 0:4].unsqueeze(2), [128, 4, 128]), ALU.mult, ALU.mult,
                        ["qkn", "EREV", "cst"], ["kdm"])
                bT = nb()
                bTq = nb()
                for h in range(4):
                    tr2(PS[bT][:, h * 128:(h + 1) * 128], kn[:, h, :], ["qkn"], [psn(bT)])
                if full:
                    tt(qd, qkn[:, 0:4, :], bc(EGC[:, 0:4].unsqueeze(2), [128, 4, 128]), ALU.mult, ["qkn", "EGC"], ["qd"])
                if full:
                    for h in range(4):
                        tr2(PS[bTq][:, h * 128:(h + 1) * 128], qkn[:, h, :], ["qkn"], [psn(bTq)])
                cp(knT, hview(bT), [psn(bT)], ["knT"], eng="act")
                if full:
                    cp(qnT, hview(bTq), [psn(bTq)], ["qnT"], eng="act")
                    bT2 = nb()
                    for h in range(4):
                        tr2(PS[bT2][:, h * 128:(h + 1) * 128], qd[:, h, :], ["qd"], [psn(bT2)])
                    for c in range(2):
                        cp(qdm[:, c, :, c * 64:(c + 1) * 64], hview(bT2)[:, :, c * 64:(c + 1) * 64],
                           [psn(bT2)], ["qdm"], eng="act")
                bG = nb()
                for h in range(4):
                    mm(PS[bG][:, h * 128:(h + 1) * 128], knT[:, h, :], knT[:, h, :], True, True, ["knT"], [psn(bG)])
                tt(tmpM, hview(bG), E1, ALU.mult, [psn(bG), "E1"], ["tmpM"])
                tt(Pb[0], tmpM, bc(nbeta.unsqueeze(2), [128, 4, 128]), ALU.mult, ["tmpM", "nbeta"], ["P0"])
                bP = nb()
                for h in range(4):
                    tr2(PS[bP][:, h * 128:(h + 1) * 128], Pb[0][:, h, :], ["P0"], [psn(bP)])
                pT_ps = hview(bP)
                tt(R32, pT_ps, bc(ident.unsqueeze(1), [128, 4, 128]), ALU.add, [psn(bP), "cst"], ["R32"])
                cp(PTb[0], pT_ps, [psn(bP)], ["PT0"])
                cp(Rbf, R32, ["R32"], ["Rbf"], eng="act")
                for k in range(1, 6):
                    pc, pp = k % 2, (k - 1) % 2
                    b1 = nb()
                    for h in range(4):
                        mm(PS[b1][:, h * 128:(h + 1) * 128], PTb[pp][:, h, :], Pb[pp][:, h, :], True, True,
                           ["P%d" % pp, "PT%d" % pp], [psn(b1)])
                    cp(Pb[pc], hview(b1), [psn(b1)], ["P%d" % pc], eng="act")
                    if k < 5:
                        b2 = nb()
                        for h in range(4):
                            mm(PS[b2][:, h * 128:(h + 1) * 128], Pb[pp][:, h, :], PTb[pp][:, h, :], True, True,
                               ["P%d" % pp, "PT%d" % pp], [psn(b2)])
                        cp(PTb[pc], hview(b2), [psn(b2)], ["PT%d" % pc])
                    b3 = nb()
                    for h in range(4):
                        mm(PS[b3][:, h * 128:(h + 1) * 128], Pb[pc][:, h, :], Rbf[:, h, :], True, True,
                           ["P%d" % pc, "Rbf"], [psn(b3)])
                    tt(R32, R32, hview(b3), ALU.add, ["R32", psn(b3)], ["R32"])
                    cp(Rbf, R32, ["R32"], ["Rbf"], eng="act")
                bu = nb()
                for h in range(4):
                    mm(PS[bu][:, h * 128:(h + 1) * 128], Rbf[:, h, :], vb[:, h, :], True, True, ["Rbf", "vb"], [psn(bu)])
                cp(u32, hview(bu), [psn(bu)], ["u32"], eng="act")
                bw = nb()
                for h in range(4):
                    mm(PS[bw][:, h * 128:(h + 1) * 128], kbe[:, h, :], Rbf[:, h, :], True, True, ["Rbf", "kbe"], [psn(bw)])
                cp(wT, hview(bw), [psn(bw)], ["E1"])
                if full:
                    bqk = nb()
                    for h in range(4):
                        mm(PS[bqk][:, h * 128:(h + 1) * 128], knT[:, h, :], qnT[:, h, :], True, True, ["knT", "qnT"], [psn(bqk)])
                    tt(qkT, hview(bqk), E2g, ALU.mult, [psn(bqk), "E2g"], ["E1"])
                if full:
                    conv_group(3, cvA[:, 0:4, :], "cvA")
                conv_group(4, cvA[:, 4:8, :], "cvA")
                mk = cvA[:, 4:8, :]
                for c in range(2):
                    stt(kwm[:, c, :, :], mk, ci[c][:, 0:1], bc(EREV[:, 4:8].unsqueeze(2), [128, 4, 128]), ALU.mult, ALU.mult,
                        ["cvA", "EREV", "cst"], ["kwm"])
                if full:
                    cp(mqk_bf, cvA, ["cvA"], ["qkn"])
                    tt(mqF, cvA[:, 0:4, :], bc(EGC[:, 4:8].unsqueeze(2), [128, 4, 128]), ALU.mult, ["cvA", "EGC"], ["qd"])
                    bT3 = [nb(), nb()]
                    for j in range(8):
                        tr2(PS[bT3[j // 4]][:, (j % 4) * 128:(j % 4 + 1) * 128], mqk_bf[:, j, :], ["qkn"], [psn(bT3[j // 4])])
                    for g2 in range(2):
                        cp(mqkT[:, g2 * 4:(g2 + 1) * 4, :], hview(bT3[g2]), [psn(bT3[g2])], ["knT", "qnT"], eng="act")
                    bT4 = nb()
                    for h in range(4):
                        tr2(PS[bT4][:, h * 128:(h + 1) * 128], mqF[:, h, :], ["qd"], [psn(bT4)])
                    for c in range(2):
                        cp(qFm[:, c, :, c * 64:(c + 1) * 64], hview(bT4)[:, :, c * 64:(c + 1) * 64],
                           [psn(bT4)], ["qFm"], eng="act")
                    bM2 = nb()
                    Mmat(4, bM2)
                    stt(tmpM, hview(bM2), -1.0, bc(maskU.unsqueeze(1), [128, 4, 128]), ALU.mult, ALU.add, [psn(bM2), "cst"], ["tmpM"])
                    tt(tmpM, tmpM, bc(lip.unsqueeze(2), [128, 4, 128]), ALU.add, ["tmpM", "lip"], ["tmpM"])
                    actf(E2m, tmpM, AF.Exp, ["tmpM"], ["E2g"])
                    bs = nb()
                    for h in range(4):
                        mm(PS[bs][:, h * 128:(h + 1) * 128], mqkT[:, 4 + h, :], mqkT[:, h, :], True, True, ["knT", "qnT"], [psn(bs)])
                    tt(PTm, hview(bs), E2m, ALU.mult, [psn(bs), "E2g"], ["PTm"])
                tl, hd = [], []
                S.cap = tl
                tail(t, full)
                S.cap = None
                if t + 1 < NT:
                    S.cap = hd
                    head(t + 1)
                    S.cap = None
                S.replay(merge(tl, hd))
                if t == 1:
                    cast_tables(["S32"])
            S.barrier()

        x_own = AR[:, 0:NOWN * D].rearrange("p (t d) -> p t d", t=NOWN, d=D)
        rot["banks"] = [0, 1, 2, 3, 4, 5, 6, 7]
        if "mixout" in phases:
            WOUT_OFF = 16384
            w_out = ARB[:, 2 * WOUT_OFF:2 * WOUT_OFF + 8192].rearrange("p (a b) -> p a b", a=8, b=1024)
            load_w("ld_wout", w_out, w_out_d, "w_out", 1024)
            for t in range(NOWN):
                dma("sp", "ld_xo", x_own[:, t, :], xs[(NPREV + t) * 128:(NPREV + t + 1) * 128, :], [], ["xo%d" % t], chain=True)
                if "mix" not in phases:
                    continue
                for n in range(2):
                    b = nb()
                    for kc in range(8):
                        mm(PS[b][:, :], mixTs[:, t, kc, :], w_out[:, kc, n * 512:(n + 1) * 512], kc == 0, kc == 7,
                           ["mixT%d" % t, "w_out"], [psn(b)])
                    tt(x_own[:, t, n * 512:(n + 1) * 512], x_own[:, t, n * 512:(n + 1) * 512], PS[b][:, :], ALU.add,
                       ["xo%d" % t, psn(b)], ["xo%d" % t])
            S.barrier()

        XEND = NOWN * D
        PWQ_OFF = ARW - 8192
        pwq = ARB[:, 2 * PWQ_OFF:2 * PWQ_OFF + 16384].rearrange("p (a b) -> p a b", a=8, b=2048)
        if "mix" not in phases:
            cast_tables()

        if "xa" in phases:
            Bx = Bump(16384, PWQ_OFF)
            wq = Bx.bf([8, 1024])
            wkv = Bx.bf([8, 2048])
            wo = Bx.bf([8, 1024])
            load_w("ld_wq", wq, wq_d, "wq", 1024)
            load_w("ld_wkv", wkv, wkv_d, "wkv", 2048)
            load_w("ld_wo", wo, wo_d, "wo", 1024)
            if "peer" in phases:
                load_w("ld_pwq", pwq, pwq_d, "pwq", 2048)
            xn = Bx.bf([D])
            hT = Bx.bf([8, 128])
            mtile = Bx.f32([D])
            mT = Bx.bf([8, 256])
            kT = Bx.bf([8, 256])
            vbf = Bx.bf([2, 1024])
            qT2 = [Bx.bf([8, 128]), Bx.bf([8, 128])]
            pexp = mtile.rearrange("p (a b) -> p a b", a=4, b=256)
            pn = Bx.bf([4, 256])
            pT = Bx.bf([8, 128])
            oT = Bx.bf([8, 128])
            mx4 = Bx.f32([4])
            sm4 = Bx.f32([4])
            for mt in range(2):
                dma("sp", "ld_mem", mtile, memd[mt * 128:(mt + 1) * 128, :], [], ["mtile"], chain=True)
                rms_T(mtile, "mtile", 24, xn, hT, "mem")
                cp(mT[:, :, mt * 128:(mt + 1) * 128], hT, ["hT"], ["mT"])
            for blk in range(8):
                b = nb()
                for kc in range(8):
                    mm(PS[b][:, 0:256], wkv[:, kc, blk * 128:(blk + 1) * 128], mT[:, kc, :], kc == 0, kc == 7, ["wkv", "mT"], [psn(b)])
                cp(kT[:, blk, :], PS[b][:, 0:256], [psn(b)], ["kT"], eng="act")
            for mt in range(2):
                for n in range(2):
                    b = nb()
                    for kc in range(8):
                        mm(PS[b][:, :], mT[:, kc, mt * 128:(mt + 1) * 128], wkv[:, kc, 1024 + n * 512:1024 + (n + 1) * 512],
                           kc == 0, kc == 7, ["wkv", "mT"], [psn(b)])
                    cp(vbf[:, mt, n * 512:(n + 1) * 512], PS[b][:, :], [psn(b)], ["vbf"], eng="act")
            rotx = {"i": 0}

            def nbx():
                b = rotx["i"] % 2
                rotx["i"] += 1
                return b

            rot["banks"] = [2, 3, 4, 5, 6, 7]

            def xhead(t):
                xnm = "xo%d" % t
                rms_T(x_own[:, t, :], xnm, 16, xn, hT, "xa", nbf=nbx)
                for g in range(2):
                    b = nbx()
                    for j in range(4):
                        blk = g * 4 + j
                        for kc in range(8):
                            mm(PS[b][:, j * 128:(j + 1) * 128], wq[:, kc, blk * 128:(blk + 1) * 128], hT[:, kc, :], kc == 0, kc == 7,
                               ["wq", "hT"], [psn(b)])
                    cp(qT2[t % 2][:, g * 4:(g + 1) * 4, :], PS[b][:, :].rearrange("p (a b) -> p a b", a=4, b=128), [psn(b)], ["qT%d" % (t % 2)], eng="act")

            def xtail(t):
                xnm = "xo%d" % t
                sb_ = [nb(), nb()]
                for h in range(4):
                    b = sb_[h // 2]
                    for dc in range(2):
                        mm(PS[b][:, (h % 2) * 256:(h % 2 + 1) * 256], qT2[t % 2][:, h * 2 + dc, :], kT[:, h * 2 + dc, :], dc == 0, dc == 1,
                           ["qT%d" % (t % 2), "kT"], [psn(b)])
                for g in range(2):
                    b = sb_[g]
                    dve(lambda e, b=b, g=g: e.tensor_reduce(out=mx4[:, g * 2:g * 2 + 2],
                                                             in_=PS[b][:, :].rearrange("p (a b) -> p a b", a=2, b=256),
                                                             axis=AX.X, op=ALU.max), [psn(b)], ["mx4"])
                ts(mx4, mx4, -1.0 / 16.0, ALU.mult, ["mx4"], ["mx4"])
                for h in range(4):
                    b = sb_[h // 2]
                    actf(pexp[:, h, :], PS[b][:, (h % 2) * 256:(h % 2 + 1) * 256], AF.Exp, [psn(b), "mx4"], ["pexp", "sm4"],
                         bias=mx4[:, h:h + 1], scale=1.0 / 16.0, accum=sm4[:, h:h + 1])
                dve(lambda e: e.reciprocal(out=sm4, in_=sm4), ["sm4"], ["sm4"])
                tt(pn, pexp, bc(sm4.unsqueeze(2), [128, 4, 256]), ALU.mult, ["pexp", "sm4"], ["pn"])
                b = nb()
                for h in range(4):
                    for mt in range(2):
                        tr(PSB[b][:, (h * 2 + mt) * 128:(h * 2 + mt + 1) * 128], pn[:, h, mt * 128:(mt + 1) * 128], ["pn"], [psn(b)])
                cp(pT, PSB[b][:, :].rearrange("p (a b) -> p a b", a=8, b=128), [psn(b)], ["pT"], eng="act")
                for g in range(2):
                    b = nb()
                    for j in range(4):
                        blk = g * 4 + j
                        h, dc = blk // 2, blk % 2
                        for mt in range(2):
                            mm(PS[b][:, j * 128:(j + 1) * 128], vbf[:, mt, h * 256 + dc * 128:h * 256 + (dc + 1) * 128],
                               pT[:, h * 2 + mt, :], mt == 0, mt == 1, ["vbf", "pT"], [psn(b)])
                    cp(oT[:, g * 4:(g + 1) * 4, :], PS[b][:, :].rearrange("p (a b) -> p a b", a=4, b=128), [psn(b)], ["oT"], eng="act")
                for n in range(2):
                    b = nb()
                    for kc in range(8):
                        mm(PS[b][:, :], oT[:, kc, :], wo[:, kc, n * 512:(n + 1) * 512], kc == 0, kc == 7, ["oT", "wo"], [psn(b)])
                    tt(x_own[:, t, n * 512:(n + 1) * 512], x_own[:, t, n * 512:(n + 1) * 512], PS[b][:, :], ALU.add,
                       [xnm, psn(b)], [xnm])

            def xmerge(a, b):
                out, i, j = [], 0, 0
                while i < len(a) or j < len(b):
                    if j >= len(b) or (i < len(a) and i * len(b) <= j * len(a)):
                        out.append(a[i]); i += 1
                    else:
                        out.append(b[j]); j += 1
                return out

            xhead(0)
            for t in range(NOWN):
                tl, hd = [], []
                S.cap = tl
                xtail(t)
                S.cap = None
                if t + 1 < NOWN:
                    S.cap = hd
                    xhead(t + 1)
                    S.cap = None
                S.replay(xmerge(tl, hd))
            S.barrier()

        if "peer" in phases:
            Bp = Bump(16384, PWQ_OFF)
            subk = Bp.bf([16, 128])
            if "xa" not in phases:
                load_w("ld_pwq", pwq, pwq_d, "pwq", 2048)
            dma("pool", "ld_sk", subk[:, 0:8, :], subk_d[:, 0:8, :], [], ["subk"])
            dma("pool", "ld_sk", subk[:, 8:16, :], subk_d[:, 8:16, :], [], ["subk"])
            xn = Bp.bf([D])
            hT = Bp.bf([8, 128])
            htok2 = [Bp.f32([D]), Bp.f32([D])]
            qT = Bp.bf([16, 128])
            sc = Bp.f32([16, 128])
            t2k = Bp.f32([2048])
            sc2 = t2k.rearrange("p (a b) -> p a b", a=16, b=128)
            topv = Bp.f32([16, 16])
            topi = Bp.i32([16, 16], u=True)
            topif = Bp.f32([16, 16])
            cand = sc[:, :, :].rearrange("p a b -> p (a b)").rearrange("p (a b) -> p a b", a=8, b=256)
            cand2 = t2k.rearrange("p (a b) -> p a b", a=8, b=256)
            bestv = Bp.f32([8, 16])
            pos = Bp.i32([8, 16], u=True)
            pa = Bp.i32([8, 16], u=True)
            pb_ = Bp.i32([8, 16], u=True)
            paf = Bp.f32([8, 16])
            pbf = Bp.f32([8, 16])
            oh = t2k.rearrange("p (a b c) -> p a b c", a=8, b=16, c=16)
            isel = Bp.f32([8, 16])
            jsel = Bp.f32([8, 16])
            eidf = Bp.f32([128])
            eid2 = [Bp.i32([128]), Bp.i32([128])]
            gate2 = [Bp.f32([8, 16]), Bp.f32([8, 16])]
            gsum = Bp.f32([8])
            actv = Bp.f32([128])
            wgt = Bp.f32([128])
            junk = Bp.bf([D])
            NG = 8
            dgs = [Bp.bf([128]) for _ in range(4)]
            rot["banks"] = [0, 1, 2, 3, 4, 5]
            gbuf = [Bp.bf([2 * D]) for _ in range(NG)]
            def prologue(t):
                par = t % 2
                eid, gate, htok = eid2[par], gate2[par], htok2[par]
                EIDN, GATEN, HTOKN = "eid%d" % par, "gate%d" % par, "htok%d" % par
                xnm = "xo%d" % t
                rms_T(x_own[:, t, :], xnm, 32, xn, hT, "pf")
                stt(htok, x_own[:, t, :], small[:, 17:18], rvec[:, 16:1040], ALU.mult, ALU.mult, [xnm, "rstd", "rvec"], [HTOKN])
                for g in range(4):
                    b = nb()
                    for j in range(4):
                        blk = g * 4 + j
                        for kc in range(8):
                            mm(PS[b][:, j * 128:(j + 1) * 128], pwq[:, kc, blk * 128:(blk + 1) * 128], hT[:, kc, :], kc == 0, kc == 7,
                               ["pwq", "hT"], [psn(b)])
                    cp(qT[:, g * 4:(g + 1) * 4, :], PS[b][:, :].rearrange("p (a b) -> p a b", a=4, b=128), [psn(b)], ["qT"], eng="act")
                for g in range(4):
                    b = nb()
                    for j in range(4):
                        blk = g * 4 + j
                        mm(PS[b][:, j * 128:(j + 1) * 128], qT[:, blk, :], subk[:, blk, :], True, True, ["qT", "subk"], [psn(b)])
                    cp(sc[:, g * 4:(g + 1) * 4, :], PS[b][:, :].rearrange("p (a b) -> p a b", a=4, b=128), [psn(b)], ["sc"], eng="act")
                for blk in range(16):
                    dve(lambda e, blk=blk: e.max(out=topv[:, blk, 0:8], in_=sc[:, blk, :]), ["sc"], ["topv"])
                    dve(lambda e, blk=blk: e.match_replace(out=sc2[:, blk, :], in_to_replace=topv[:, blk, 0:8],
                                                             in_values=sc[:, blk, :], imm_value=-1e30), ["sc", "topv"], ["t2k"])
                    dve(lambda e, blk=blk: e.max(out=topv[:, blk, 8:16], in_=sc2[:, blk, :]), ["t2k"], ["topv"])
                    dve(lambda e, blk=blk: e.max_index(out=topi[:, blk, 0:8], in_max=topv[:, blk, 0:8], in_values=sc[:, blk, :]),
                        ["sc", "topv"], ["topi"])
                    dve(lambda e, blk=blk: e.max_index(out=topi[:, blk, 8:16], in_max=topv[:, blk, 8:16], in_values=sc[:, blk, :]),
                        ["sc", "topv"], ["topi"])
                cp(topif, topi, ["topi"], ["topif"])
                tv = topv[:, :, :].rearrange("p (h two) k -> p h two k", h=8, two=2)
                tif = topif[:, :, :].rearrange("p (h two) k -> p h two k", h=8, two=2)
                candv = cand[:, :, :].rearrange("p h (a b) -> p h a b", a=16, b=16)
                tt(candv, bc(tv[:, :, 0, :].unsqueeze(3), [128, 8, 16, 16]), bc(tv[:, :, 1, :].unsqueeze(2), [128, 8, 16, 16]),
                   ALU.add, ["topv"], ["sc"])
                for h in range(8):
                    dve(lambda e, h=h: e.max(out=bestv[:, h, 0:8], in_=cand[:, h, :]), ["sc"], ["bestv"])
                    dve(lambda e, h=h: e.match_replace(out=cand2[:, h, :], in_to_replace=bestv[:, h, 0:8],
                                                         in_values=cand[:, h, :], imm_value=-1e30), ["sc", "bestv"], ["t2k"])
                    dve(lambda e, h=h: e.max(out=bestv[:, h, 8:16], in_=cand2[:, h, :]), ["t2k"], ["bestv"])
                    dve(lambda e, h=h: e.max_index(out=pos[:, h, 0:8], in_max=bestv[:, h, 0:8], in_values=cand[:, h, :]),
                        ["sc", "bestv"], ["pos"])
                    dve(lambda e, h=h: e.max_index(out=pos[:, h, 8:16], in_max=bestv[:, h, 8:16], in_values=cand[:, h, :]),
                        ["sc", "bestv"], ["pos"])
                dve(lambda e: e.tensor_single_scalar(out=pa, in_=pos, scalar=4, op=ALU.arith_shift_right), ["pos"], ["pa"])
                dve(lambda e: e.tensor_single_scalar(out=pb_, in_=pos, scalar=15, op=ALU.bitwise_and), ["pos"], ["pb"])
                cp(paf, pa, ["pa"], ["paf"])
                cp(pbf, pb_, ["pb"], ["pbf"])
                io4 = bc(iota16.unsqueeze(1).unsqueeze(1), [128, 8, 16, 16])
                tt(oh, io4, bc(paf.unsqueeze(3), [128, 8, 16, 16]), ALU.is_equal, ["cst", "paf"], ["t2k"])
                tt(oh, oh, bc(tif[:, :, 0, :].unsqueeze(2), [128, 8, 16, 16]), ALU.mult, ["t2k", "topif"], ["t2k"])
                dve(lambda e: e.tensor_reduce(out=isel, in_=oh, axis=AX.X, op=ALU.add), ["t2k"], ["isel"])
                tt(oh, io4, bc(pbf.unsqueeze(3), [128, 8, 16, 16]), ALU.is_equal, ["cst", "pbf"], ["t2k"])
                tt(oh, oh, bc(tif[:, :, 1, :].unsqueeze(2), [128, 8, 16, 16]), ALU.mult, ["t2k", "topif"], ["t2k"])
                dve(lambda e: e.tensor_reduce(out=jsel, in_=oh, axis=AX.X, op=ALU.add), ["t2k"], ["jsel"])
                stt(eidf.rearrange("p (h k) -> p h k", h=8, k=16), isel, 128.0, jsel, ALU.mult, ALU.add, ["isel", "jsel"], ["eidf"])
                cp(eid, eidf, ["eidf"], [EIDN])
                tt(gate, bestv, bc(bestv[:, :, 0:1], [128, 8, 16]), ALU.subtract, ["bestv"], [GATEN])
                actf(gate, gate, AF.Exp, [GATEN], [GATEN])
                dve(lambda e: e.tensor_reduce(out=gsum, in_=gate, axis=AX.X, op=ALU.add), [GATEN], ["gsum"])
                dve(lambda e: e.reciprocal(out=gsum, in_=gsum), ["gsum"], ["gsum"])
                tt(gate, gate, bc(gsum.unsqueeze(2), [128, 8, 16]), ALU.mult, [GATEN, "gsum"], [GATEN])

            def slotloop(t, inj):
                par = t % 2
                xnm = "xo%d" % t
                eid, gate, htok = eid2[par], gate2[par], htok2[par]
                EIDN, GATEN, HTOKN = "eid%d" % par, "gate%d" % par, "htok%d" % par
                gflat = gate[:, :, :].rearrange("p h k -> p (h k)")

                def fin(s_):
                    gb_ = gbuf[s_ % NG]
                    gn_ = "gb%d" % (s_ % NG)
                    dg = dgs[s_ % 4]
                    dn = "dg%d" % (s_ % 4)
                    actf(wgt[:, s_:s_ + 1], wgt[:, s_:s_ + 1], AF.Copy, ["wgt%d" % s_, GATEN], ["wgt%d" % s_], scale=gflat[:, s_:s_ + 1])
                    actf(dg, ident, AF.Copy, ["cst", "wgt%d" % s_], [dn], scale=wgt[:, s_:s_ + 1])
                    for n_ in range(2):
                        mm(PS[6 + n_][:, :], dg, gb_[:, D + n_ * 512:D + (n_ + 1) * 512], s_ == 0, s_ == 127, [dn, gn_], [psn(6 + n_)])

                for s in range(128):
                    gb = gbuf[s % NG]
                    gn = "gb%d" % (s % NG)
                    S.add("pool", lambda e, gb=gb, s=s: e.indirect_dma_start(
                        out=gb, out_offset=None, in_=uvb_d[:, :],
                        in_offset=bass.IndirectOffsetOnAxis(ap=eid[:, s:s + 1], axis=0)), [EIDN, "uvb"], [gn], dq="g%d" % (s % NG))
                    stt(junk, gb[:, 0:D], 1.0, htok, ALU.mult, ALU.mult, [gn, HTOKN], ["junk", "actv%d" % s], accum=actv[:, s:s + 1])
                    actf(wgt[:, s:s + 1], actv[:, s:s + 1], AF.Gelu, ["actv%d" % s], ["wgt%d" % s])
                    if s >= 1:
                        fin(s - 1)
                    if inj and s < 120:
                        S.replay(inj[len(inj) * s // 120:len(inj) * (s + 1) // 120])
                fin(127)
                for n_ in range(2):
                    tt(x_own[:, t, n_ * 512:(n_ + 1) * 512], x_own[:, t, n_ * 512:(n_ + 1) * 512], PS[6 + n_][:, :], ALU.add,
                       [xnm, psn(6 + n_)], [xnm])

            prologue(0)
            for t in range(NOWN):
                inj = []
                if t + 1 < NOWN:
                    S.cap = inj
                    prologue(t + 1)
                    S.cap = None
                slotloop(t, inj)
            S.barrier()

        Bf = Bump(16384, ARW)
        obuf = [Bf.f32([D]), Bf.f32([D])]
        junkb = Bf.bf([D])
        outs = []
        for t in range(NOWN):
            xnm = "xo%d" % t
            ob = obuf[t % 2]
            on = "ob%d" % (t % 2)
            if "final" in phases:
                ss = small[:, 16:17]
                rstd = small[:, 17:18]
                actf(junkb, x_own[:, t, :], AF.Square, [xnm], ["junkb", "ss"], scale=1.0 / 32.0, accum=ss)
                actf(ss, ss, AF.Sqrt, ["ss"], ["ss"], bias=EPS, scale=1.0)
                dve(lambda e: e.reciprocal(out=rstd, in_=ss), ["ss"], ["rstd"])
                stt(ob, x_own[:, t, :], rstd, rvec[:, 1040:2064], ALU.mult, ALU.mult, [xnm, "rstd", "rvec"], [on])
            else:
                cp(ob, x_own[:, t, :], [xnm], [on])
            dma("sp", "st%d" % (t % 2), outd[t * 128:(t + 1) * 128, :], ob, [on], ["out%d" % t])
            outs.append("out%d" % t)
        S.add("sp", None, r=outs)
        S.add("sp", None, extra=[(q, n - 1) for q, n in S.dcount.items() if q.startswith("st")])
        S.emit(nc, es)
    return nc


def _consts():
    p = np.arange(128)[:, None]
    f = np.arange(128)[None, :]
    same = (p // 64) == (f // 64)
    c = np.zeros((128, 1040), np.float32)
    c[:, 0:128] = np.eye(128)
    c[:, 128:256] = ((p <= f) & same)
    c[:, 256:384] = ((p > f) & same)
    c[:, 384:512] = (p // 64 == 0) * np.ones((1, 128))
    c[:, 512:640] = (p // 64 == 1) * np.ones((1, 128))
    c[:, 640:768] = -1.0
    c[:, 768:896] = np.where((p > f) & same, 0.0, NEG)
    c[:, 896:1024] = np.where((f >= p) & same, 0.0, NEG)
    c[:, 1024:1040] = np.arange(16)[None, :]
    return c


def _kc(w):
    return np.ascontiguousarray(w.reshape(8, 128, -1).transpose(1, 0, 2))


def _pv(v):
    return v.reshape(8, 128).T


_NC_CACHE = {}


def run(inputs, npre, nown, phases=("mix", "mixout", "xa", "peer", "final")):
    f = lambda k: np.asarray(inputs[k], dtype=np.float32)
    x = f("x")
    B, SEQ, _ = x.shape
    half = SEQ // 2
    assert half == nown * 128 and npre == nown
    perm = np.r_[0:1536, 2056:3080, 1536:2048, 3080:3592, 3592:4104, 2048:2052, 2052:2056, 4104:4108, 4108:4112]
    w_in = _kc(f("w_in")[0][:, perm])
    convw = np.concatenate([f("gdn_conv_w")[0], f("mlstm_conv_w")[0]], axis=1)
    convw = np.ascontiguousarray(convw.reshape(4, 20, 128).transpose(2, 1, 0).reshape(128, 80))
    pvec = np.zeros((128, 40), np.float32)
    pvec[:, 0:8] = _pv(f("norm_mix_w")[0])
    pvec[:, 8:12] = f("gdn_norm_w")[0][:, None]
    pvec[:, 12:16] = f("mlstm_norm_w")[0].reshape(4, 128).T
    pvec[:, 16:24] = _pv(f("norm_xa_w")[0])
    pvec[:, 24:32] = _pv(f("norm_mem_w")[0])
    pvec[:, 32:40] = _pv(f("norm_ffn_w")[0])
    rv = np.concatenate([f("gdn_a_log")[0], f("gdn_dt_bias")[0], f("mlstm_i_bias")[0], f("mlstm_f_bias")[0],
                         f("norm_ffn_w")[0], f("norm_final_w")])
    rvec = np.ascontiguousarray(np.broadcast_to(rv[None, :], (128, 2064)))
    subk = f("peer_sub_keys")[0].reshape(16, 128, 128)
    subk = np.ascontiguousarray(subk.transpose(2, 0, 1))
    common = {
        "w_in": w_in, "w_out": _kc(f("w_out")[0]), "xa_wq": _kc(f("xa_wq")[0]), "xa_wkv": _kc(f("xa_wkv")[0]),
        "xa_wo": _kc(f("xa_wo")[0]), "peer_wq": _kc(f("peer_wq")[0]), "subk": subk, "convw": convw, "pvec": pvec,
        "rvec": rvec, "cst": _consts(),
        "peer_uv": np.ascontiguousarray(np.concatenate([f("peer_u")[0], f("peer_v")[0]], axis=1)),
    }
    mem = f("mem")
    in_maps = []
    ncores = 2 * B
    for c in range(ncores):
        b, hf = c // 2, c % 2
        own = x[b, hf * half:(hf + 1) * half]
        prev = x[b, 0:half] if hf == 1 else np.zeros_like(own)
        m = dict(common)
        m["xs"] = np.ascontiguousarray(np.concatenate([prev, own], axis=0))
        m["mem"] = np.ascontiguousarray(mem[b])
        in_maps.append(m)
    key = (npre, nown, tuple(phases))
    if key not in _NC_CACHE:
        _NC_CACHE[key] = build(npre, nown, phases)
    nc = _NC_CACHE[key]
    res = run_bass_kernel_spmd(nc, in_maps, core_ids=list(range(ncores)))
    out = np.zeros((B, SEQ, D), np.float32)
    for c in range(ncores):
        b, hf = c // 2, c % 2
        out[b, hf * half:(hf + 1) * half] = res.results[c]["out"]
    return out


def kernel(**inputs):
    return run(inputs, 16, 16)
```

```python
import numpy as np
from contextlib import ExitStack
import concourse.bass as bass
import concourse.mybir as mybir
from concourse.bass_utils import run_bass_kernel_spmd

F32 = mybir.dt.float32
BF = mybir.dt.bfloat16
U32 = mybir.dt.uint32
I32 = mybir.dt.int32
AF = mybir.ActivationFunctionType
ALU = mybir.AluOpType
AX = mybir.AxisListType

ENGS = ("sp", "act", "dve", "pool", "pe")
D = 1024
NEG = -30000.0
EPS = 1e-6


class Sch:
    def __init__(self):
        self.ops = {e: [] for e in ENGS}
        self.ccount = {e: 0 for e in ENGS}
        self.dcount = {}
        self.seen = {e: {} for e in ENGS}
        self.waited = {}
        self.lastw = {}
        self.readers = {}
        self.cap = None

    def replay(self, items):
        for it in items:
            self.add(*it)

    def add(self, eng, fn, r=(), w=(), dq=None, chain=False, extra=()):
        if self.cap is not None:
            self.cap.append((eng, fn, list(r), list(w), dq, chain, list(extra)))
            return
        r = list(r)
        w = list(w)
        if dq and chain:
            w.append("__chain_" + dq)
        if dq:
            idx = self.dcount.get(dq, 0)
            self.dcount[dq] = idx + 1
            q = dq
        else:
            idx = self.ccount[eng]
            if fn is not None:
                self.ccount[eng] += 1
            q = eng
        deps = {}

        def need(d):
            if d is None:
                return
            if deps.get(d[0], -1) < d[1]:
                deps[d[0]] = d[1]

        for x in r:
            need(self.lastw.get(x))
            if isinstance(x, str) and x.startswith("ps"):
                for d in self.readers.get(x, ()):
                    if d[0] != q:
                        need(d)
        for x in w:
            need(self.lastw.get(x))
            for d in self.readers.get(x, ()):
                need(d)
        for d in extra:
            need(d)
        waits = []
        for dqn, di in deps.items():
            if dqn == "pe" and eng == "pe" and not dq:
                continue
            if self.seen[eng].get(dqn, -1) >= di:
                continue
            self.seen[eng][dqn] = di
            waits.append((dqn, di))
            self.waited.setdefault(dqn, set()).add(di)
        self.ops[eng].append((fn, waits, q, idx, bool(dq)))
        if fn is None:
            return
        me = (q, idx)
        for x in w:
            self.lastw[x] = me
            self.readers[x] = []
        for x in r:
            self.readers.setdefault(x, []).append(me)

    def barrier(self):
        ext = []
        for e in ENGS:
            if self.ccount[e] > 0:
                ext.append((e, self.ccount[e] - 1))
        for q, n in self.dcount.items():
            ext.append((q, n - 1))
        for e in ENGS:
            self.add(e, None, extra=ext)

    def emit(self, nc, es):
        queues = set(self.waited.keys()) | set(self.dcount.keys())
        sem = {q: es.enter_context(nc.semaphore("s_" + q)) for q in sorted(queues)}
        val = {}
        for q, s in self.waited.items():
            if q in self.dcount:
                continue
            for rank, i in enumerate(sorted(s)):
                val[(q, i)] = rank + 1

        def value(q, i):
            if q in self.dcount:
                return 16 * (i + 1)
            return val[(q, i)]

        def runner(en):
            def f(e):
                for fn, waits, q, idx, isdma in self.ops[en]:
                    for (wq, wi) in waits:
                        e.wait_ge(sem[wq], value(wq, wi))
                    if fn is None:
                        continue
                    ins = fn(e)
                    if isdma:
                        ins.then_inc(sem[q], 16)
                    elif (q, idx) in val:
                        ins.then_inc(sem[q], 1)
            return f

        with nc.Block() as block:
            block.sync(runner("sp"))
            block.scalar(runner("act"))
            block.vector(runner("dve"))
            block.gpsimd(runner("pool"))
            block.tensor(runner("pe"))


def bc(ap, shape):
    return ap.to_broadcast(list(shape))


def build(NPREV, NOWN, phases=("mix", "mixout", "xa", "peer", "final")):
    NT = NPREV + NOWN
    nc = bass.Bass("TRN2", target_bir_lowering=False)
    dr = lambda n, s, dt, k="ExternalInput": nc.dram_tensor(n, list(s), dt, kind=k).ap()
    xs = dr("xs", [NT * 128, D], F32)
    memd = dr("mem", [256, D], F32)
    w_in_d = dr("w_in", [128, 8, 4112], F32)
    w_out_d = dr("w_out", [128, 8, 1024], F32)
    wq_d = dr("xa_wq", [128, 8, 1024], F32)
    wkv_d = dr("xa_wkv", [128, 8, 2048], F32)
    wo_d = dr("xa_wo", [128, 8, 1024], F32)
    pwq_d = dr("peer_wq", [128, 8, 2048], F32)
    subk_d = dr("subk", [128, 16, 128], F32)
    convw_d = dr("convw", [128, 80], F32)
    pvec_d = dr("pvec", [128, 40], F32)
    rvec_d = dr("rvec", [128, 2064], F32)
    cst_d = dr("cst", [128, 1040], F32)
    puv_d = dr("peer_uv", [16384, 2 * D], F32)
    outd = dr("out", [NOWN * 128, D], F32, "ExternalOutput")
    uvb_d = nc.dram_tensor("uvb", [16384, 2 * D], BF, kind="Internal").ap()

    S = Sch()
    es = ExitStack()
    with es:
        sb = lambda n, s, dt: es.enter_context(nc.sbuf_tensor("sb_" + n, list(s), dt))
        ARW = 49152
        AR = sb("arena", [128, ARW], F32)
        ARB = AR.bitcast(BF)
        ARI = AR.bitcast(I32)
        ARU = AR.bitcast(U32)
        cst = sb("cst", [128, 1040], F32)
        pvec = sb("pvec", [128, 40], F32)
        rvec = sb("rvec", [128, 2064], F32)
        identb = sb("identb", [128, 128], BF)
        small = sb("small", [128, 256], F32)
        PS = [es.enter_context(nc.psum_tensor("ps%d" % i, [128, 512], F32)) for i in range(8)]
        PSB = [p.bitcast(BF) for p in PS]

        class Bump:
            def __init__(self, lo, hi):
                self.lo, self.hi, self.p = lo, hi, lo

            def f32(self, shape):
                n = int(np.prod(shape))
                o = self.p
                self.p += n
                assert self.p <= self.hi, ("arena overflow", self.p, self.hi)
                v = AR[:, o:o + n]
                return self._shape(v, shape)

            def _shape(self, v, shape):
                if len(shape) == 1:
                    return v
                if len(shape) == 2:
                    return v.rearrange("p (a b) -> p a b", a=shape[0], b=shape[1])
                return v.rearrange("p (a b c) -> p a b c", a=shape[0], b=shape[1], c=shape[2])

            def bf(self, shape):
                n = int(np.prod(shape))
                nw = (n + 1) // 2
                o = self.p
                self.p += nw
                assert self.p <= self.hi, ("arena overflow", self.p, self.hi)
                return self._shape(ARB[:, 2 * o:2 * o + n], shape)

            def i32(self, shape, u=False):
                n = int(np.prod(shape))
                o = self.p
                self.p += n
                assert self.p <= self.hi
                return self._shape((ARU if u else ARI)[:, o:o + n], shape)

        ident = cst[:, 0:128]
        tri = cst[:, 128:256]
        rev = cst[:, 256:384]
        ci = [cst[:, 384:512], cst[:, 512:640]]
        negones = cst[:, 640:768]
        maskL = cst[:, 768:896]
        maskU = cst[:, 896:1024]
        iota16 = cst[:, 1024:1040]

        rot = {"i": 0, "banks": [0, 1, 2, 3, 4]}

        def nb():
            b = rot["banks"][rot["i"] % len(rot["banks"])]
            rot["i"] += 1
            return b

        def psn(b):
            return "ps%d" % b

        def dve(fn, r, w):
            S.add("dve", fn, r, w)

        def act(fn, r, w):
            S.add("act", fn, r, w)

        def pe(fn, r, w):
            S.add("pe", fn, r, w)

        def mm(out, lhsT, rhs, start, stop, r, w):
            pe(lambda e: e.matmul(out, lhsT=lhsT, rhs=rhs, start=start, stop=stop), r, w)

        def tr(out, in_, r, w):
            pe(lambda e: e.transpose(out=out, in_=in_, identity=identb[:, :]), list(r) + ["identb"], w)

        def tr2(out, in_, r, w):
            mm(out, in_, identb[:, :], True, True, list(r) + ["identb"], w)

        def tt(out, in0, in1, op, r, w):
            dve(lambda e: e.tensor_tensor(out=out, in0=in0, in1=in1, op=op), r, w)

        def ts(out, in0, s1, op0, r, w, s2=None, op1=None):
            if op1 is None:
                dve(lambda e: e.tensor_scalar(out=out, in0=in0, scalar1=s1, scalar2=None, op0=op0), r, w)
            else:
                dve(lambda e: e.tensor_scalar(out=out, in0=in0, scalar1=s1, scalar2=s2, op0=op0, op1=op1), r, w)

        def stt(out, in0, scalar, in1, op0, op1, r, w, accum=None):
            if accum is None:
                dve(lambda e: e.scalar_tensor_tensor(out=out, in0=in0, scalar=scalar, in1=in1, op0=op0, op1=op1), r, w)
            else:
                dve(lambda e: e.scalar_tensor_tensor(out=out, in0=in0, scalar=scalar, in1=in1, op0=op0, op1=op1,
                                                     accum_out=accum), r, w)

        def cp(out, in_, r, w, eng="dve"):
            if eng == "dve":
                dve(lambda e: e.tensor_copy(out=out, in_=in_), r, w)
            else:
                act(lambda e: e.copy(out=out, in_=in_), r, w)

        def actf(out, in_, func, r, w, bias=None, scale=None, accum=None):
            kw = {}
            if bias is not None:
                kw["bias"] = bias
            if scale is not None:
                kw["scale"] = scale
            if accum is not None:
                kw["accum_out"] = accum
            act(lambda e: e.activation(out=out, in_=in_, func=func, **kw), r, w)

        def memset(ap, v, w):
            dve(lambda e: e.memset(ap, v), [], w)

        def dma(eng, q, out, in_, r, w, chain=False):
            S.add(eng, lambda e: e.dma_start(out=out, in_=in_), r, w, dq=q, chain=chain)

        def load_w(q, dst3, src3, name, ncols):
            step = 1024
            for kc in range(8):
                for c0 in range(0, ncols, step):
                    c1 = min(ncols, c0 + step)
                    dma("pool", q, dst3[:, kc, c0:c1], src3[:, kc, c0:c1], [], [name])

        dma("sp", "ld_c", cst[:, :], cst_d[:, :], [], ["cst"])
        dma("sp", "ld_pv", pvec[:, :], pvec_d[:, :], [], ["pvec"])
        dma("sp", "ld_rv", rvec[:, :], rvec_d[:, :], [], ["rvec"])
        cp(identb[:, :], ident, ["cst"], ["identb"])
        negA = small[:, 0:4]
        ibp = small[:, 4:8]
        actf(negA, rvec[:, 0:4], AF.Exp, ["rvec"], ["negA"])
        ts(negA, negA, -1.0, ALU.mult, ["negA"], ["negA"])
        ts(ibp, rvec[:, 8:12], float(np.log(128.0 ** -0.5)), ALU.add, ["rvec"], ["ibp"])

        def cast_tables(dep=()):
            if "peer" in phases:
                for i in range(64):
                    dma("pool", "cvt", uvb_d[i * 256:(i + 1) * 256, :], puv_d[i * 256:(i + 1) * 256, :], list(dep), ["uvb"])

        def rms_T(xap, xname, pvcol, xn, hT, tagp, nbf=None):
            ss = small[:, 16:17]
            rstd = small[:, 17:18]
            junk = xn
            actf(junk, xap, AF.Square, [xname], ["xn", "ss"], scale=1.0 / 32.0, accum=ss)
            actf(ss, ss, AF.Sqrt, ["ss"], ["ss"], bias=EPS, scale=1.0)
            dve(lambda e: e.reciprocal(out=rstd, in_=ss), ["ss"], ["rstd"])
            ts(xn, xap, rstd, ALU.mult, [xname, "rstd"], ["xn"])
            b = (nbf or nb)()
            for kc in range(8):
                tr(PSB[b][:, kc * 128:(kc + 1) * 128], xn[:, kc * 128:(kc + 1) * 128], ["xn"], [psn(b)])
            tt(hT, PSB[b][:, :].rearrange("p (a b) -> p a b", a=8, b=128),
               bc(pvec[:, pvcol:pvcol + 8].unsqueeze(2), [128, 8, 128]), ALU.mult, [psn(b), "pvec"], ["hT"])

        MIXT_OFF = 24512
        mixTs = ARB[:, 2 * MIXT_OFF:2 * MIXT_OFF + 16 * 1024].rearrange("p (t a b) -> p t a b", t=16, a=8, b=128)
        if "mix" in phases:
            WIN_OFF = MIXT_OFF + 8192
            w_in = ARB[:, 2 * WIN_OFF:2 * WIN_OFF + 8 * 4112].rearrange("p (a b) -> p a b", a=8, b=4112)
            load_w("ld_win", w_in, w_in_d, "w_in", 4112)
            cast_tables()
            B0 = Bump(0, MIXT_OFF)
            B1 = B0
            convw = B1.f32([80])
            dma("sp", "ld_cw", convw, convw_d[:, :], [], ["convw"])
            diag = B1.bf([80, 128])
            for i in range(80):
                ts(diag[:, i, :], ident, convw[:, i:i + 1], ALU.mult, ["cst", "convw"], ["diag"])
            xin = [B0.f32([D])]
            xn = B0.bf([D])
            hT = B0.bf([8, 128])
            pre = B0.bf([20, 131])
            cvA = B0.f32([8, 128])
            cvV = B0.f32([4, 128])
            sq = B0.f32([8, 128])
            R1 = sq[:, 0:4, :]
            R2 = sq[:, 4:8, :]
            tmpM = B0.f32([4, 128])
            E1 = B0.f32([4, 128])
            wT = B0.bf([4, 128])
            qkT = B0.bf([4, 128])
            E2g = B0.f32([4, 128])
            E2m = E2g
            qkn = B0.bf([8, 128])
            kbe = B0.bf([4, 128])
            vb = B0.bf([4, 128])
            kdm = B0.bf([2, 4, 128])
            qd = B0.bf([4, 128])
            kqT = B0.bf([8, 128])
            knT = kqT[:, 0:4, :]
            qnT = kqT[:, 4:8, :]
            qdm = B0.bf([2, 4, 128])
            Pb = [B0.bf([4, 128]), B0.bf([4, 128])]
            PTb = [B0.bf([4, 128]), B0.bf([4, 128])]
            R32 = B0.f32([4, 128])
            Rbf = B0.bf([4, 128])
            u32 = B0.f32([4, 128])
            vnew = B0.bf([4, 128])
            vnew2 = [B0.bf([4, 128]), B0.bf([4, 128])]
            S32 = B0.f32([4, 128])
            Sbf2 = [B0.bf([4, 128]), B0.bf([4, 128])]
            mqk_bf = qkn
            mqF = qd
            kwm = B0.bf([2, 4, 128])
            vaug = B0.bf([4, 130])
            mqkT = kqT
            qFm = B0.bf([2, 4, 128])
            PTm = B0.bf([4, 128])
            Cm32 = B0.f32([4, 129])
            Cmbf2 = [B0.bf([4, 130]), B0.bf([4, 130])]
            hm = R32
            o32 = u32
            zs = B1.f32([4, 128])
            so = B1.f32([4, 128])
            mixed = B1.bf([8, 128])
            g16 = B1.f32([16])
            T12 = B1.f32([12])
            E12 = B1.f32([12])
            L8 = B1.f32([8])
            LA = B1.f32([8])
            beta = B1.f32([4])
            nbeta = B1.f32([4])
            bEgc = B1.f32([4])
            lip = B1.f32([4])
            gcs = B1.f32([32])
            EGC = B1.f32([8])
            EREV = B1.f32([8])
            EGL2 = [B1.f32([16]), B1.f32([16])]
            ssq = B1.f32([8])
            rinv = B1.f32([8])
            den = B1.f32([4])
            ms4 = B1.f32([4])

            for (ap, nm) in ((pre, "pre"), (kdm, "kdm"), (qdm, "qdm"), (kwm, "kwm"), (qFm, "qFm"), (vnew, "vnew"), (vnew2[0], "vnew0"), (vnew2[1], "vnew1"),
                             (S32, "S32"), (Sbf2[0], "Sbf0"), (Sbf2[1], "Sbf1"), (Cm32, "Cm32"), (Cmbf2[0], "Cmbf0"), (Cmbf2[1], "Cmbf1")):
                memset(ap, 0.0, [nm])
            memset(vaug, 1.0, ["vaug"])

            OB, HX, HY = 5, 6, 7

            def hview(b):
                return PS[b][:, :].rearrange("p (a b) -> p a b", a=4, b=128)

            def hview2(b):
                return PS[b][:, :].rearrange("p (a b) -> p a b", a=2, b=256)

            roth = {"i": 0}

            def nbh():
                b = roth["i"] % 2
                roth["i"] += 1
                return b

            rot["banks"] = [2, 3, 4]

            def Mmat(lo, dest_ps):
                cp(R1, bc(LA[:, lo:lo + 4].unsqueeze(2), [128, 4, 128]), ["LA"], ["R1"])
                tt(R2, bc(tri.unsqueeze(1), [128, 4, 128]), bc(LA[:, lo:lo + 4].unsqueeze(2), [128, 4, 128]),
                   ALU.mult, ["cst", "LA"], ["R2"])
                mm(PS[dest_ps][:, :], tri, R1[:, :, :].rearrange("p a b -> p (a b)"), True, False, ["cst", "R1"], [psn(dest_ps)])
                mm(PS[dest_ps][:, :], negones, R2[:, :, :].rearrange("p a b -> p (a b)"), False, True, ["cst", "R2"], [psn(dest_ps)])

            def head(t):
                xt = xin[0]
                xnm = "xin0"
                dma("sp", "ld_x0", xt, xs[t * 128:(t + 1) * 128, :], [], [xnm])
                rms_T(xt, xnm, 0, xn, hT, "m", nbf=nbh)
                if t > 0:
                    cp(pre[:, :, 0:3], pre[:, :, 128:131], ["pre"], ["pre"])
                groups = [0, 1, 2, 3, 4] if t >= NPREV - 1 else [1, 2, 4]
                for g in groups:
                    b = nbh()
                    for h in range(4):
                        blk = g * 4 + h
                        for kc in range(8):
                            mm(PS[b][:, h * 128:(h + 1) * 128], w_in[:, kc, blk * 128:(blk + 1) * 128], hT[:, kc, :],
                               kc == 0, kc == 7, ["w_in", "hT"], [psn(b)])
                    cp(pre[:, g * 4:(g + 1) * 4, 3:131], hview(b), [psn(b)], ["pre"], eng="act")
                full = t >= NPREV
                bg = nbh()
                for kc in range(8):
                    mm(PS[bg][:, 0:16], hT[:, kc, :], w_in[:, kc, 4096:4112], kc == 0, kc == 7, ["w_in", "hT"], [psn(bg)])
                cp(g16, PS[bg][:, 0:16], [psn(bg)], ["g16"])
                tt(T12[:, 0:4], g16[:, 0:4], rvec[:, 4:8], ALU.add, ["g16", "rvec"], ["T12"])
                stt(T12[:, 4:8], g16[:, 12:16], 1.0, rvec[:, 12:16], ALU.mult, ALU.add, ["g16", "rvec"], ["T12"])
                ts(T12[:, 4:8], T12[:, 4:8], -1.0, ALU.mult, ["T12"], ["T12"])
                ts(T12[:, 8:12], g16[:, 4:8], -1.0, ALU.mult, ["g16"], ["T12"])
                actf(E12, T12, AF.Exp, ["T12"], ["E12"])
                actf(L8, E12[:, 0:8], AF.Ln, ["E12"], ["L8"], bias=1.0)
                tt(LA[:, 0:4], L8[:, 0:4], negA, ALU.mult, ["L8", "negA"], ["LA"])
                ts(LA[:, 4:8], L8[:, 4:8], -1.0, ALU.mult, ["L8"], ["LA"])
                ts(beta, E12[:, 8:12], 1.0, ALU.add, ["E12"], ["beta"])
                dve(lambda e: e.reciprocal(out=beta, in_=beta), ["beta"], ["beta"])
                ts(nbeta, beta, -1.0, ALU.mult, ["beta"], ["nbeta"])
                tt(lip, g16[:, 8:12], ibp, ALU.add, ["g16", "ibp"], ["lip"])
                bq = nbh()
                mm(PS[bq][:, 0:8], tri, LA, True, True, ["cst", "LA"], [psn(bq)])
                mm(PS[bq][:, 8:16], rev, LA, True, True, ["cst", "LA"], [psn(bq)])
                mm(PS[bq][:, 16:24], ci[0], LA, True, True, ["cst", "LA"], [psn(bq)])
                mm(PS[bq][:, 24:32], ci[1], LA, True, True, ["cst", "LA"], [psn(bq)])
                cp(gcs, PS[bq][:, 0:32], [psn(bq)], ["gcs"])
                tt(gcs[:, 12:16], gcs[:, 12:16], lip, ALU.add, ["gcs", "lip"], ["gcs"])
                actf(EGC, gcs[:, 0:8], AF.Exp, ["gcs"], ["EGC"])
                actf(EREV, gcs[:, 8:16], AF.Exp, ["gcs"], ["EREV"])
                actf(EGL2[t % 2], gcs[:, 16:32], AF.Exp, ["gcs"], ["EGL%d" % (t % 2)])
                tt(bEgc, beta, EGC[:, 0:4], ALU.mult, ["beta", "EGC"], ["bEgc"])

                bM = nbh()
                Mmat(0, bM)
                tt(tmpM, hview(bM), bc(maskL.unsqueeze(1), [128, 4, 128]), ALU.add, [psn(bM), "cst"], ["tmpM"])
                actf(E1, tmpM, AF.Exp, ["tmpM"], ["E1"])
                if full:
                    stt(tmpM, hview(bM), -1.0, bc(maskU.unsqueeze(1), [128, 4, 128]), ALU.mult, ALU.add, [psn(bM), "cst"], ["tmpM"])
                    actf(E2g, tmpM, AF.Exp, ["tmpM"], ["E2g"])


            def tail(t, full):
                for c in range(2):
                    Sbf, Sn = Sbf2[c], "Sbf%d" % c
                    Sbo, Son = Sbf2[1 - c], "Sbf%d" % (1 - c)
                    Cbo, Con = Cmbf2[1 - c], "Cmbf%d" % (1 - c)
                    bws = nb()
                    for h in range(4):
                        mm(PS[bws][:, h * 128:(h + 1) * 128], wT[:, h, :], Sbf[:, h, :], True, True, ["E1", Sn], [psn(bws)])
                    if full and c == 0:
                        for h in range(4):
                            mm(PS[OB][:, h * 128:(h + 1) * 128], qdm[:, 0, h, :], Sbf[:, h, :], True, True, ["qdm", Sn], [psn(OB)])
                        for h in range(4):
                            hb = HX if h < 2 else HY
                            mm(hview2(hb)[:, h % 2, 0:129], qFm[:, 0, h, :], Cmbf2[0][:, h, 0:129], True, True, ["qFm", "Cmbf0"], [psn(hb)])
                    vc = vnew2[c]
                    tt(vc, u32, hview(bws), ALU.subtract, ["u32", psn(bws)], ["vnew%d" % c])
                    bds = nb()
                    for h in range(4):
                        mm(PS[bds][:, h * 128:(h + 1) * 128], kdm[:, c, h, :], vc[:, h, :], True, True, ["kdm", "vnew%d" % c], [psn(bds)])
                    tt(S32, S32, bc(EGL2[t % 2][:, c * 8:c * 8 + 4].unsqueeze(2), [128, 4, 128]), ALU.mult, ["S32", "EGL%d" % (t % 2)], ["S32"])
                    tt(S32, S32, hview(bds), ALU.add, ["S32", psn(bds)], ["S32"])
                    cp(Sbo, S32, ["S32"], [Son], eng="act")
                    bc1, bc2 = nb(), nb()
                    for h in range(4):
                        hb = bc1 if h < 2 else bc2
                        mm(hview2(hb)[:, h % 2, 0:129], kwm[:, c, h, :], vaug[:, h, 0:129], True, True, ["kwm", "vaug"], [psn(hb)])
                    tt(Cm32, Cm32, bc(EGL2[t % 2][:, c * 8 + 4:c * 8 + 8].unsqueeze(2), [128, 4, 129]), ALU.mult, ["Cm32", "EGL%d" % (t % 2)], ["Cm32"])
                    tt(Cm32[:, 0:2, :], Cm32[:, 0:2, :], hview2(bc1)[:, :, 0:129], ALU.add, ["Cm32", psn(bc1)], ["Cm32"])
                    tt(Cm32[:, 2:4, :], Cm32[:, 2:4, :], hview2(bc2)[:, :, 0:129], ALU.add, ["Cm32", psn(bc2)], ["Cm32"])
                    cp(Cbo[:, :, 0:129], Cm32, ["Cm32"], [Con], eng="act")
                if not full:
                    return
                ts(vnew, vnew2[0], ci[0][:, 0:1], ALU.mult, ["vnew0", "cst"], ["vnew"])
                stt(vnew, vnew2[1], ci[1][:, 0:1], vnew, ALU.mult, ALU.add, ["vnew1", "cst", "vnew"], ["vnew"])
                bo2 = nb()
                for h in range(4):
                    mm(PS[bo2][:, h * 128:(h + 1) * 128], qdm[:, 1, h, :], Sbf2[1][:, h, :], True, False, ["qdm", "Sbf1"], [psn(bo2)])
                    mm(PS[bo2][:, h * 128:(h + 1) * 128], qkT[:, h, :], vnew[:, h, :], False, True, ["E1", "vnew"], [psn(bo2)])
                bh2 = [nb(), nb()]
                for h in range(4):
                    hb = bh2[h // 2]
                    mm(hview2(hb)[:, h % 2, 0:129], qFm[:, 1, h, :], Cmbf2[1][:, h, 0:129], True, False, ["qFm", "Cmbf1"], [psn(hb)])
                    mm(hview2(hb)[:, h % 2, 0:129], PTm[:, h, :], vaug[:, h, 0:129], False, True, ["PTm", "vaug"], [psn(hb)])
                cp(o32, hview(OB), [psn(OB)], ["u32"], eng="act")
                tt(o32, o32, hview(bo2), ALU.add, ["u32", psn(bo2)], ["u32"])
                tt(cvA[:, 0:4, :], o32, o32, ALU.mult, ["u32"], ["cvA"])
                dve(lambda e: e.tensor_reduce(out=ms4, in_=cvA[:, 0:4, :], axis=AX.X, op=ALU.add), ["cvA"], ["ms4"])
                actf(ms4, ms4, AF.Sqrt, ["ms4"], ["ms4"], bias=EPS, scale=1.0 / 128.0)
                dve(lambda e: e.reciprocal(out=ms4, in_=ms4), ["ms4"], ["ms4"])
                tt(o32, o32, bc(ms4.unsqueeze(2), [128, 4, 128]), ALU.mult, ["u32", "ms4"], ["u32"])
                tt(mixed[:, 0:4, :], o32, zs, ALU.mult, ["u32", "zs"], ["mixed"])
                for hb, hb2, h0 in ((HX, bh2[0], 0), (HY, bh2[1], 2)):
                    cp(hm[:, h0:h0 + 2, :], hview2(hb)[:, :, 0:128], [psn(hb)], ["R32"], eng="act")
                    cp(den[:, h0:h0 + 2], hview2(hb)[:, :, 128], [psn(hb)], ["den"], eng="act")
                    tt(hm[:, h0:h0 + 2, :], hm[:, h0:h0 + 2, :], hview2(hb2)[:, :, 0:128], ALU.add, ["R32", psn(hb2)], ["R32"])
                    tt(den[:, h0:h0 + 2], den[:, h0:h0 + 2], hview2(hb2)[:, :, 128], ALU.add, ["den", psn(hb2)], ["den"])
                stt(den, den, -1.0, den, ALU.mult, ALU.max, ["den"], ["den"])
                ts(den, den, 1.0, ALU.max, ["den"], ["den"])
                dve(lambda e: e.reciprocal(out=den, in_=den), ["den"], ["den"])
                tt(hm, hm, bc(den.unsqueeze(2), [128, 4, 128]), ALU.mult, ["R32", "den"], ["R32"])
                tt(cvA[:, 0:4, :], hm, hm, ALU.mult, ["R32"], ["cvA"])
                dve(lambda e: e.tensor_reduce(out=ms4, in_=cvA[:, 0:4, :], axis=AX.X, op=ALU.add), ["cvA"], ["ms4"])
                actf(ms4, ms4, AF.Sqrt, ["ms4"], ["ms4"], bias=EPS, scale=1.0 / 128.0)
                dve(lambda e: e.reciprocal(out=ms4, in_=ms4), ["ms4"], ["ms4"])
                tt(hm, hm, bc(ms4.unsqueeze(2), [128, 4, 128]), ALU.mult, ["R32", "ms4"], ["R32"])
                tt(mixed[:, 4:8, :], hm, so, ALU.mult, ["R32", "so"], ["mixed"])
                bmx = [nb(), nb()]
                for kc in range(8):
                    tr2(PS[bmx[kc // 4]][:, (kc % 4) * 128:(kc % 4 + 1) * 128], mixed[:, kc, :], ["mixed"], [psn(bmx[kc // 4])])
                to = t - NPREV
                for g2 in range(2):
                    tt(mixTs[:, to, g2 * 4:(g2 + 1) * 4, :], hview(bmx[g2]),
                       bc(pvec[:, 8 + g2 * 4:12 + g2 * 4].unsqueeze(2), [128, 4, 128]), ALU.mult, [psn(bmx[g2]), "pvec"], ["mixT%d" % to])

            def merge(a, b):
                out, i, j = [], 0, 0
                while i < len(a) or j < len(b):
                    if j >= len(b) or (i < len(a) and i * len(b) <= j * len(a)):
                        out.append(a[i]); i += 1
                    else:
                        out.append(b[j]); j += 1
                return out

            head(0)
            for t in range(NT):
                full = t >= NPREV
                bz = bmo = None
                if full:
                    bz = nb()
                    for kc in range(8):
                        mm(PS[bz][:, :], hT[:, kc, :], w_in[:, kc, 2560:3072], kc == 0, kc == 7, ["w_in", "hT"], [psn(bz)])
                    actf(zs, hview(bz), AF.Silu, [psn(bz)], ["zs"])
                    bmo = nb()
                    for kc in range(8):
                        mm(PS[bmo][:, :], hT[:, kc, :], w_in[:, kc, 3584:4096], kc == 0, kc == 7, ["w_in", "hT"], [psn(bmo)])
                    actf(so, hview(bmo), AF.Sigmoid, [psn(bmo)], ["so"])
                bmv = nb()
                for kc in range(8):
                    mm(PS[bmv][:, :], hT[:, kc, :], w_in[:, kc, 3072:3584], kc == 0, kc == 7, ["w_in", "hT"], [psn(bmv)])
                cp(vaug[:, :, 0:128], hview(bmv), [psn(bmv)], ["vaug"], eng="act")
                def conv_group(g, dst, dname):
                    b = nb()
                    for h in range(4):
                        blk = g * 4 + h
                        for tap in range(4):
                            mm(PS[b][:, h * 128:(h + 1) * 128], pre[:, blk, tap:tap + 128], diag[:, blk * 4 + tap, :],
                               tap == 0, tap == 3, ["pre", "diag"], [psn(b)])
                    actf(dst, hview(b), AF.Silu, [psn(b)], [dname])

                if full:
                    conv_group(0, cvA[:, 0:4, :], "cvA")
                conv_group(1, cvA[:, 4:8, :], "cvA")
                conv_group(2, cvV, "cvV")
                lo = 0 if full else 4
                tt(sq[:, lo:8, :], cvA[:, lo:8, :], cvA[:, lo:8, :], ALU.mult, ["cvA"], ["R1", "R2"])
                dve(lambda e, lo=lo: e.tensor_reduce(out=ssq[:, lo:8], in_=sq[:, lo:8, :], axis=AX.X, op=ALU.add), ["R1", "R2"], ["ssq"])
                actf(ssq[:, lo:8], ssq[:, lo:8], AF.Sqrt, ["ssq"], ["ssq"], bias=EPS, scale=1.0)
                dve(lambda e, lo=lo: e.reciprocal(out=rinv[:, lo:8], in_=ssq[:, lo:8]), ["ssq"], ["rinv"])
                if full:
                    ts(rinv[:, 0:4], rinv[:, 0:4], float(128.0 ** -0.5), ALU.mult, ["rinv"], ["rinv"])
                tt(qkn[:, lo:8, :], cvA[:, lo:8, :], bc(rinv[:, lo:8].unsqueeze(2), [128, 8 - lo, 128]), ALU.mult,
                   ["cvA", "rinv"], ["qkn"])
                kn = qkn[:, 4:8, :]
                tt(kbe, kn, bc(bEgc.unsqueeze(2), [128, 4, 128]), ALU.mult, ["qkn", "bEgc"], ["kbe"])
                tt(vb, cvV, bc(beta.unsqueeze(2), [128, 4, 128]), ALU.mult, ["cvV", "beta"], ["vb"])
                for c in range(2):
                    stt(kdm[:, c, :, :], kn, ci[c][:, 0:1], bc(EREV[:, 0:4].unsqueeze(2), [128, 4, 128]), ALU.mult, ALU.mult,
                        ["qkn", "EREV", "cst"], ["kdm"])
                bT = nb()
                bTq = nb()
                for h in range(4):
                    tr2(PS[bT][:, h * 128:(h + 1) * 128], kn[:, h, :], ["qkn"], [psn(bT)])
                if full:
                    tt(qd, qkn[:, 0:4, :], bc(EGC[:, 0:4].unsqueeze(2), [128, 4, 128]), ALU.mult, ["qkn", "EGC"], ["qd"])
                if full:
                    for h in range(4):
                        tr2(PS[bTq][:, h * 128:(h + 1) * 128], qkn[:, h, :], ["qkn"], [psn(bTq)])
                cp(knT, hview(bT), [psn(bT)], ["knT"], eng="act")
                if full:
                    cp(qnT, hview(bTq), [psn(bTq)], ["qnT"], eng="act")
                    bT2 = nb()
                    for h in range(4):
                        tr2(PS[bT2][:, h * 128:(h + 1) * 128], qd[:, h, :], ["qd"], [psn(bT2)])
                    for c in range(2):
                        cp(qdm[:, c, :, c * 64:(c + 1) * 64], hview(bT2)[:, :, c * 64:(c + 1) * 64],
                           [psn(bT2)], ["qdm"], eng="act")
                bG = nb()
                for h in range(4):
                    mm(PS[bG][:, h * 128:(h + 1) * 128], knT[:, h, :], knT[:, h, :], True, True, ["knT"], [psn(bG)])
                tt(tmpM, hview(bG), E1, ALU.mult, [psn(bG), "E1"], ["tmpM"])
                tt(Pb[0], tmpM, bc(nbeta.unsqueeze(2), [128, 4, 128]), ALU.mult, ["tmpM", "nbeta"], ["P0"])
                bP = nb()
                for h in range(4):
                    tr2(PS[bP][:, h * 128:(h + 1) * 128], Pb[0][:, h, :], ["P0"], [psn(bP)])
                pT_ps = hview(bP)
                tt(R32, pT_ps, bc(ident.unsqueeze(1), [128, 4, 128]), ALU.add, [psn(bP), "cst"], ["R32"])
                cp(PTb[0], pT_ps, [psn(bP)], ["PT0"])
                cp(Rbf, R32, ["R32"], ["Rbf"], eng="act")
                for k in range(1, 6):
                    pc, pp = k % 2, (k - 1) % 2
                    b1 = nb()
                    for h in range(4):
                        mm(PS[b1][:, h * 128:(h + 1) * 128], PTb[pp][:, h, :], Pb[pp][:, h, :], True, True,
                           ["P%d" % pp, "PT%d" % pp], [psn(b1)])
                    cp(Pb[pc], hview(b1), [psn(b1)], ["P%d" % pc], eng="act")
                    if k < 5:
                        b2 = nb()
                        for h in range(4):
                            mm(PS[b2][:, h * 128:(h + 1) * 128], Pb[pp][:, h, :], PTb[pp][:, h, :], True, True,
                               ["P%d" % pp, "PT%d" % pp], [psn(b2)])
                        cp(PTb[pc], hview(b2), [psn(b2)], ["PT%d" % pc])
                    b3 = nb()
                    for h in range(4):
                        mm(PS[b3][:, h * 128:(h + 1) * 128], Pb[pc][:, h, :], Rbf[:, h, :], True, True,
                           ["P%d" % pc, "Rbf"], [psn(b3)])
                    tt(R32, R32, hview(b3), ALU.add, ["R32", psn(b3)], ["R32"])
                    cp(Rbf, R32, ["R32"], ["Rbf"], eng="act")
                bu = nb()
                for h in range(4):
                    mm(PS[bu][:, h * 128:(h + 1) * 128], Rbf[:, h, :], vb[:, h, :], True, True, ["Rbf", "vb"], [psn(bu)])
                cp(u32, hview(bu), [psn(bu)], ["u32"], eng="act")
                bw = nb()
                for h in range(4):
                    mm(PS[bw][:, h * 128:(h + 1) * 128], kbe[:, h, :], Rbf[:, h, :], True, True, ["Rbf", "kbe"], [psn(bw)])
                cp(wT, hview(bw), [psn(bw)], ["E1"])
                if full:
                    bqk = nb()
                    for h in range(4):
                        mm(PS[bqk][:, h * 128:(h + 1) * 128], knT[:, h, :], qnT[:, h, :], True, True, ["knT", "qnT"], [psn(bqk)])
                    tt(qkT, hview(bqk), E2g, ALU.mult, [psn(bqk), "E2g"], ["E1"])
                if full:
                    conv_group(3, cvA[:, 0:4, :], "cvA")
                conv_group(4, cvA[:, 4:8, :], "cvA")
                mk = cvA[:, 4:8, :]
                for c in range(2):
                    stt(kwm[:, c, :, :], mk, ci[c][:, 0:1], bc(EREV[:, 4:8].unsqueeze(2), [128, 4, 128]), ALU.mult, ALU.mult,
                        ["cvA", "EREV", "cst"], ["kwm"])
                if full:
                    cp(mqk_bf, cvA, ["cvA"], ["qkn"])
                    tt(mqF, cvA[:, 0:4, :], bc(EGC[:, 4:8].unsqueeze(2), [128, 4, 128]), ALU.mult, ["cvA", "EGC"], ["qd"])
                    bT3 = [nb(), nb()]
                    for j in range(8):
                        tr2(PS[bT3[j // 4]][:, (j % 4) * 128:(j % 4 + 1) * 128], mqk_bf[:, j, :], ["qkn"], [psn(bT3[j // 4])])
                    for g2 in range(2):
                        cp(mqkT[:, g2 * 4:(g2 + 1) * 4, :], hview(bT3[g2]), [psn(bT3[g2])], ["knT", "qnT"], eng="act")
                    bT4 = nb()
                    for h in range(4):
                        tr2(PS[bT4][:, h * 128:(h + 1) * 128], mqF[:, h, :], ["qd"], [psn(bT4)])
                    for c in range(2):
                        cp(qFm[:, c, :, c * 64:(c + 1) * 64], hview(bT4)[:, :, c * 64:(c + 1) * 64],
                           [psn(bT4)], ["qFm"], eng="act")
                    bM2 = nb()
                    Mmat(4, bM2)
                    stt(tmpM, hview(bM2), -1.0, bc(maskU.unsqueeze(1), [128, 4, 128]), ALU.mult, ALU.add, [psn(bM2), "cst"], ["tmpM"])
                    tt(tmpM, tmpM, bc(lip.unsqueeze(2), [128, 4, 128]), ALU.add, ["tmpM", "lip"], ["tmpM"])
                    actf(E2m, tmpM, AF.Exp, ["tmpM"], ["E2g"])
                    bs = nb()
                    for h in range(4):
                        mm(PS[bs][:, h * 128:(h + 1) * 128], mqkT[:, 4 + h, :], mqkT[:, h, :], True, True, ["knT", "qnT"], [psn(bs)])
                    tt(PTm, hview(bs), E2m, ALU.mult, [psn(bs), "E2g"], ["PTm"])
                tl, hd = [], []
                S.cap = tl
                tail(t, full)
                S.cap = None
                if t + 1 < NT:
                    S.cap = hd
                    head(t + 1)
                    S.cap = None
                S.replay(merge(tl, hd))
            S.barrier()

        x_own = AR[:, 0:NOWN * D].rearrange("p (t d) -> p t d", t=NOWN, d=D)
        rot["banks"] = [0, 1, 2, 3, 4, 5, 6, 7]
        if "mixout" in phases:
            WOUT_OFF = 16384
            w_out = ARB[:, 2 * WOUT_OFF:2 * WOUT_OFF + 8192].rearrange("p (a b) -> p a b", a=8, b=1024)
            load_w("ld_wout", w_out, w_out_d, "w_out", 1024)
            for t in range(NOWN):
                dma("sp", "ld_xo", x_own[:, t, :], xs[(NPREV + t) * 128:(NPREV + t + 1) * 128, :], [], ["xo%d" % t], chain=True)
                if "mix" not in phases:
                    continue
                for n in range(2):
                    b = nb()
                    for kc in range(8):
                        mm(PS[b][:, :], mixTs[:, t, kc, :], w_out[:, kc, n * 512:(n + 1) * 512], kc == 0, kc == 7,
                           ["mixT%d" % t, "w_out"], [psn(b)])
                    tt(x_own[:, t, n * 512:(n + 1) * 512], x_own[:, t, n * 512:(n + 1) * 512], PS[b][:, :], ALU.add,
                       ["xo%d" % t, psn(b)], ["xo%d" % t])
            S.barrier()

        XEND = NOWN * D
        PWQ_OFF = ARW - 8192
        pwq = ARB[:, 2 * PWQ_OFF:2 * PWQ_OFF + 16384].rearrange("p (a b) -> p a b", a=8, b=2048)
        if "mix" not in phases:
            cast_tables()

        if "xa" in phases:
            Bx = Bump(16384, PWQ_OFF)
            wq = Bx.bf([8, 1024])
            wkv = Bx.bf([8, 2048])
            wo = Bx.bf([8, 1024])
            load_w("ld_wq", wq, wq_d, "wq", 1024)
            load_w("ld_wkv", wkv, wkv_d, "wkv", 2048)
            load_w("ld_wo", wo, wo_d, "wo", 1024)
            if "peer" in phases:
                load_w("ld_pwq", pwq, pwq_d, "pwq", 2048)
            xn = Bx.bf([D])
            hT = Bx.bf([8, 128])
            mtile = Bx.f32([D])
            mT = Bx.bf([8, 256])
            kT = Bx.bf([8, 256])
            vbf = Bx.bf([2, 1024])
            qT2 = [Bx.bf([8, 128]), Bx.bf([8, 128])]
            pexp = mtile.rearrange("p (a b) -> p a b", a=4, b=256)
            pn = Bx.bf([4, 256])
            pT = Bx.bf([8, 128])
            oT = Bx.bf([8, 128])
            mx4 = Bx.f32([4])
            sm4 = Bx.f32([4])
            for mt in range(2):
                dma("sp", "ld_mem", mtile, memd[mt * 128:(mt + 1) * 128, :], [], ["mtile"], chain=True)
                rms_T(mtile, "mtile", 24, xn, hT, "mem")
                cp(mT[:, :, mt * 128:(mt + 1) * 128], hT, ["hT"], ["mT"])
            for blk in range(8):
                b = nb()
                for kc in range(8):
                    mm(PS[b][:, 0:256], wkv[:, kc, blk * 128:(blk + 1) * 128], mT[:, kc, :], kc == 0, kc == 7, ["wkv", "mT"], [psn(b)])
                cp(kT[:, blk, :], PS[b][:, 0:256], [psn(b)], ["kT"], eng="act")
            for mt in range(2):
                for n in range(2):
                    b = nb()
                    for kc in range(8):
                        mm(PS[b][:, :], mT[:, kc, mt * 128:(mt + 1) * 128], wkv[:, kc, 1024 + n * 512:1024 + (n + 1) * 512],
                           kc == 0, kc == 7, ["wkv", "mT"], [psn(b)])
                    cp(vbf[:, mt, n * 512:(n + 1) * 512], PS[b][:, :], [psn(b)], ["vbf"], eng="act")
            rotx = {"i": 0}

            def nbx():
                b = rotx["i"] % 2
                rotx["i"] += 1
                return b

            rot["banks"] = [2, 3, 4, 5, 6, 7]

            def xhead(t):
                xnm = "xo%d" % t
                rms_T(x_own[:, t, :], xnm, 16, xn, hT, "xa", nbf=nbx)
                for g in range(2):
                    b = nbx()
                    for j in range(4):
                        blk = g * 4 + j
                        for kc in range(8):
                            mm(PS[b][:, j * 128:(j + 1) * 128], wq[:, kc, blk * 128:(blk + 1) * 128], hT[:, kc, :], kc == 0, kc == 7,
                               ["wq", "hT"], [psn(b)])
                    cp(qT2[t % 2][:, g * 4:(g + 1) * 4, :], PS[b][:, :].rearrange("p (a b) -> p a b", a=4, b=128), [psn(b)], ["qT%d" % (t % 2)], eng="act")

            def xtail(t):
                xnm = "xo%d" % t
                sb_ = [nb(), nb()]
                for h in range(4):
                    b = sb_[h // 2]
                    for dc in range(2):
                        mm(PS[b][:, (h % 2) * 256:(h % 2 + 1) * 256], qT2[t % 2][:, h * 2 + dc, :], kT[:, h * 2 + dc, :], dc == 0, dc == 1,
                           ["qT%d" % (t % 2), "kT"], [psn(b)])
                for g in range(2):
                    b = sb_[g]
                    dve(lambda e, b=b, g=g: e.tensor_reduce(out=mx4[:, g * 2:g * 2 + 2],
                                                             in_=PS[b][:, :].rearrange("p (a b) -> p a b", a=2, b=256),
                                                             axis=AX.X, op=ALU.max), [psn(b)], ["mx4"])
                ts(mx4, mx4, -1.0 / 16.0, ALU.mult, ["mx4"], ["mx4"])
                for h in range(4):
                    b = sb_[h // 2]
                    actf(pexp[:, h, :], PS[b][:, (h % 2) * 256:(h % 2 + 1) * 256], AF.Exp, [psn(b), "mx4"], ["pexp", "sm4"],
                         bias=mx4[:, h:h + 1], scale=1.0 / 16.0, accum=sm4[:, h:h + 1])
                dve(lambda e: e.reciprocal(out=sm4, in_=sm4), ["sm4"], ["sm4"])
                tt(pn, pexp, bc(sm4.unsqueeze(2), [128, 4, 256]), ALU.mult, ["pexp", "sm4"], ["pn"])
                b = nb()
                for h in range(4):
                    for mt in range(2):
                        tr(PSB[b][:, (h * 2 + mt) * 128:(h * 2 + mt + 1) * 128], pn[:, h, mt * 128:(mt + 1) * 128], ["pn"], [psn(b)])
                cp(pT, PSB[b][:, :].rearrange("p (a b) -> p a b", a=8, b=128), [psn(b)], ["pT"], eng="act")
                for g in range(2):
                    b = nb()
                    for j in range(4):
                        blk = g * 4 + j
                        h, dc = blk // 2, blk % 2
                        for mt in range(2):
                            mm(PS[b][:, j * 128:(j + 1) * 128], vbf[:, mt, h * 256 + dc * 128:h * 256 + (dc + 1) * 128],
                               pT[:, h * 2 + mt, :], mt == 0, mt == 1, ["vbf", "pT"], [psn(b)])
                    cp(oT[:, g * 4:(g + 1) * 4, :], PS[b][:, :].rearrange("p (a b) -> p a b", a=4, b=128), [psn(b)], ["oT"], eng="act")
                for n in range(2):
                    b = nb()
                    for kc in range(8):
                        mm(PS[b][:, :], oT[:, kc, :], wo[:, kc, n * 512:(n + 1) * 512], kc == 0, kc == 7, ["oT", "wo"], [psn(b)])
                    tt(x_own[:, t, n * 512:(n + 1) * 512], x_own[:, t, n * 512:(n + 1) * 512], PS[b][:, :], ALU.add,
                       [xnm, psn(b)], [xnm])

            def xmerge(a, b):
                out, i, j = [], 0, 0
                while i < len(a) or j < len(b):
                    if j >= len(b) or (i < len(a) and i * len(b) <= j * len(a)):
                        out.append(a[i]); i += 1
                    else:
                        out.append(b[j]); j += 1
                return out

            xhead(0)
            for t in range(NOWN):
                tl, hd = [], []
                S.cap = tl
                xtail(t)
                S.cap = None
                if t + 1 < NOWN:
                    S.cap = hd
                    xhead(t + 1)
                    S.cap = None
                S.replay(xmerge(tl, hd))
            S.barrier()

        if "peer" in phases:
            Bp = Bump(16384, PWQ_OFF)
            subk = Bp.bf([16, 128])
            if "xa" not in phases:
                load_w("ld_pwq", pwq, pwq_d, "pwq", 2048)
            dma("pool", "ld_sk", subk[:, 0:8, :], subk_d[:, 0:8, :], [], ["subk"])
            dma("pool", "ld_sk", subk[:, 8:16, :], subk_d[:, 8:16, :], [], ["subk"])
            xn = Bp.bf([D])
            hT = Bp.bf([8, 128])
            htok2 = [Bp.f32([D]), Bp.f32([D])]
            qT = Bp.bf([16, 128])
            sc = Bp.f32([16, 128])
            t2k = Bp.f32([2048])
            sc2 = t2k.rearrange("p (a b) -> p a b", a=16, b=128)
            topv = Bp.f32([16, 16])
            topi = Bp.i32([16, 16], u=True)
            topif = Bp.f32([16, 16])
            cand = sc[:, :, :].rearrange("p a b -> p (a b)").rearrange("p (a b) -> p a b", a=8, b=256)
            cand2 = t2k.rearrange("p (a b) -> p a b", a=8, b=256)
            bestv = Bp.f32([8, 16])
            pos = Bp.i32([8, 16], u=True)
            pa = Bp.i32([8, 16], u=True)
            pb_ = Bp.i32([8, 16], u=True)
            paf = Bp.f32([8, 16])
            pbf = Bp.f32([8, 16])
            oh = t2k.rearrange("p (a b c) -> p a b c", a=8, b=16, c=16)
            isel = Bp.f32([8, 16])
            jsel = Bp.f32([8, 16])
            eidf = Bp.f32([128])
            eid2 = [Bp.i32([128]), Bp.i32([128])]
            gate2 = [Bp.f32([8, 16]), Bp.f32([8, 16])]
            gsum = Bp.f32([8])
            actv = Bp.f32([128])
            wgt = Bp.f32([128])
            junk = Bp.bf([D])
            NG = 8
            dgs = [Bp.bf([128]) for _ in range(4)]
            rot["banks"] = [0, 1, 2, 3, 4, 5]
            gbuf = [Bp.bf([2 * D]) for _ in range(NG)]
            def prologue(t):
                par = t % 2
                eid, gate, htok = eid2[par], gate2[par], htok2[par]
                EIDN, GATEN, HTOKN = "eid%d" % par, "gate%d" % par, "htok%d" % par
                xnm = "xo%d" % t
                rms_T(x_own[:, t, :], xnm, 32, xn, hT, "pf")
                stt(htok, x_own[:, t, :], small[:, 17:18], rvec[:, 16:1040], ALU.mult, ALU.mult, [xnm, "rstd", "rvec"], [HTOKN])
                for g in range(4):
                    b = nb()
                    for j in range(4):
                        blk = g * 4 + j
                        for kc in range(8):
                            mm(PS[b][:, j * 128:(j + 1) * 128], pwq[:, kc, blk * 128:(blk + 1) * 128], hT[:, kc, :], kc == 0, kc == 7,
                               ["pwq", "hT"], [psn(b)])
                    cp(qT[:, g * 4:(g + 1) * 4, :], PS[b][:, :].rearrange("p (a b) -> p a b", a=4, b=128), [psn(b)], ["qT"], eng="act")
                for g in range(4):
                    b = nb()
                    for j in range(4):
                        blk = g * 4 + j
                        mm(PS[b][:, j * 128:(j + 1) * 128], qT[:, blk, :], subk[:, blk, :], True, True, ["qT", "subk"], [psn(b)])
                    cp(sc[:, g * 4:(g + 1) * 4, :], PS[b][:, :].rearrange("p (a b) -> p a b", a=4, b=128), [psn(b)], ["sc"], eng="act")
                for blk in range(16):
                    dve(lambda e, blk=blk: e.max(out=topv[:, blk, 0:8], in_=sc[:, blk, :]), ["sc"], ["topv"])
                    dve(lambda e, blk=blk: e.match_replace(out=sc2[:, blk, :], in_to_replace=topv[:, blk, 0:8],
                                                             in_values=sc[:, blk, :], imm_value=-1e30), ["sc", "topv"], ["t2k"])
                    dve(lambda e, blk=blk: e.max(out=topv[:, blk, 8:16], in_=sc2[:, blk, :]), ["t2k"], ["topv"])
                    dve(lambda e, blk=blk: e.max_index(out=topi[:, blk, 0:8], in_max=topv[:, blk, 0:8], in_values=sc[:, blk, :]),
                        ["sc", "topv"], ["topi"])
                    dve(lambda e, blk=blk: e.max_index(out=topi[:, blk, 8:16], in_max=topv[:, blk, 8:16], in_values=sc[:, blk, :]),
                        ["sc", "topv"], ["topi"])
                cp(topif, topi, ["topi"], ["topif"])
                tv = topv[:, :, :].rearrange("p (h two) k -> p h two k", h=8, two=2)
                tif = topif[:, :, :].rearrange("p (h two) k -> p h two k", h=8, two=2)
                candv = cand[:, :, :].rearrange("p h (a b) -> p h a b", a=16, b=16)
                tt(candv, bc(tv[:, :, 0, :].unsqueeze(3), [128, 8, 16, 16]), bc(tv[:, :, 1, :].unsqueeze(2), [128, 8, 16, 16]),
                   ALU.add, ["topv"], ["sc"])
                for h in range(8):
                    dve(lambda e, h=h: e.max(out=bestv[:, h, 0:8], in_=cand[:, h, :]), ["sc"], ["bestv"])
                    dve(lambda e, h=h: e.match_replace(out=cand2[:, h, :], in_to_replace=bestv[:, h, 0:8],
                                                         in_values=cand[:, h, :], imm_value=-1e30), ["sc", "bestv"], ["t2k"])
                    dve(lambda e, h=h: e.max(out=bestv[:, h, 8:16], in_=cand2[:, h, :]), ["t2k"], ["bestv"])
                    dve(lambda e, h=h: e.max_index(out=pos[:, h, 0:8], in_max=bestv[:, h, 0:8], in_values=cand[:, h, :]),
                        ["sc", "bestv"], ["pos"])
                    dve(lambda e, h=h: e.max_index(out=pos[:, h, 8:16], in_max=bestv[:, h, 8:16], in_values=cand[:, h, :]),
                        ["sc", "bestv"], ["pos"])
                dve(lambda e: e.tensor_single_scalar(out=pa, in_=pos, scalar=4, op=ALU.arith_shift_right), ["pos"], ["pa"])
                dve(lambda e: e.tensor_single_scalar(out=pb_, in_=pos, scalar=15, op=ALU.bitwise_and), ["pos"], ["pb"])
                cp(paf, pa, ["pa"], ["paf"])
                cp(pbf, pb_, ["pb"], ["pbf"])
                io4 = bc(iota16.unsqueeze(1).unsqueeze(1), [128, 8, 16, 16])
                tt(oh, io4, bc(paf.unsqueeze(3), [128, 8, 16, 16]), ALU.is_equal, ["cst", "paf"], ["t2k"])
                tt(oh, oh, bc(tif[:, :, 0, :].unsqueeze(2), [128, 8, 16, 16]), ALU.mult, ["t2k", "topif"], ["t2k"])
                dve(lambda e: e.tensor_reduce(out=isel, in_=oh, axis=AX.X, op=ALU.add), ["t2k"], ["isel"])
                tt(oh, io4, bc(pbf.unsqueeze(3), [128, 8, 16, 16]), ALU.is_equal, ["cst", "pbf"], ["t2k"])
                tt(oh, oh, bc(tif[:, :, 1, :].unsqueeze(2), [128, 8, 16, 16]), ALU.mult, ["t2k", "topif"], ["t2k"])
                dve(lambda e: e.tensor_reduce(out=jsel, in_=oh, axis=AX.X, op=ALU.add), ["t2k"], ["jsel"])
                stt(eidf.rearrange("p (h k) -> p h k", h=8, k=16), isel, 128.0, jsel, ALU.mult, ALU.add, ["isel", "jsel"], ["eidf"])
                cp(eid, eidf, ["eidf"], [EIDN])
                tt(gate, bestv, bc(bestv[:, :, 0:1], [128, 8, 16]), ALU.subtract, ["bestv"], [GATEN])
                actf(gate, gate, AF.Exp, [GATEN], [GATEN])
                dve(lambda e: e.tensor_reduce(out=gsum, in_=gate, axis=AX.X, op=ALU.add), [GATEN], ["gsum"])
                dve(lambda e: e.reciprocal(out=gsum, in_=gsum), ["gsum"], ["gsum"])
                tt(gate, gate, bc(gsum.unsqueeze(2), [128, 8, 16]), ALU.mult, [GATEN, "gsum"], [GATEN])

            def slotloop(t, inj):
                par = t % 2
                xnm = "xo%d" % t
                eid, gate, htok = eid2[par], gate2[par], htok2[par]
                EIDN, GATEN, HTOKN = "eid%d" % par, "gate%d" % par, "htok%d" % par
                gflat = gate[:, :, :].rearrange("p h k -> p (h k)")

                def fin(s_):
                    gb_ = gbuf[s_ % NG]
                    gn_ = "gb%d" % (s_ % NG)
                    dg = dgs[s_ % 4]
                    dn = "dg%d" % (s_ % 4)
                    actf(wgt[:, s_:s_ + 1], wgt[:, s_:s_ + 1], AF.Copy, ["wgt%d" % s_, GATEN], ["wgt%d" % s_], scale=gflat[:, s_:s_ + 1])
                    actf(dg, ident, AF.Copy, ["cst", "wgt%d" % s_], [dn], scale=wgt[:, s_:s_ + 1])
                    for n_ in range(2):
                        mm(PS[6 + n_][:, :], dg, gb_[:, D + n_ * 512:D + (n_ + 1) * 512], s_ == 0, s_ == 127, [dn, gn_], [psn(6 + n_)])

                for s in range(128):
                    gb = gbuf[s % NG]
                    gn = "gb%d" % (s % NG)
                    S.add("pool", lambda e, gb=gb, s=s: e.indirect_dma_start(
                        out=gb, out_offset=None, in_=uvb_d[:, :],
                        in_offset=bass.IndirectOffsetOnAxis(ap=eid[:, s:s + 1], axis=0)), [EIDN, "uvb"], [gn], dq="g%d" % (s % NG))
                    stt(junk, gb[:, 0:D], 1.0, htok, ALU.mult, ALU.mult, [gn, HTOKN], ["junk", "actv%d" % s], accum=actv[:, s:s + 1])
                    actf(wgt[:, s:s + 1], actv[:, s:s + 1], AF.Gelu, ["actv%d" % s], ["wgt%d" % s])
                    if s >= 1:
                        fin(s - 1)
                    if inj and s < 120:
                        S.replay(inj[len(inj) * s // 120:len(inj) * (s + 1) // 120])
                fin(127)
                for n_ in range(2):
                    tt(x_own[:, t, n_ * 512:(n_ + 1) * 512], x_own[:, t, n_ * 512:(n_ + 1) * 512], PS[6 + n_][:, :], ALU.add,
                       [xnm, psn(6 + n_)], [xnm])

            prologue(0)
            for t in range(NOWN):
                inj = []
                if t + 1 < NOWN:
                    S.cap = inj
                    prologue(t + 1)
                    S.cap = None
                slotloop(t, inj)
            S.barrier()

        Bf = Bump(16384, ARW)
        obuf = [Bf.f32([D]), Bf.f32([D])]
        junkb = Bf.bf([D])
        outs = []
        for t in range(NOWN):
            xnm = "xo%d" % t
            ob = obuf[t % 2]
            on = "ob%d" % (t % 2)
            if "final" in phases:
                ss = small[:, 16:17]
                rstd = small[:, 17:18]
                actf(junkb, x_own[:, t, :], AF.Square, [xnm], ["junkb", "ss"], scale=1.0 / 32.0, accum=ss)
                actf(ss, ss, AF.Sqrt, ["ss"], ["ss"], bias=EPS, scale=1.0)
                dve(lambda e: e.reciprocal(out=rstd, in_=ss), ["ss"], ["rstd"])
                stt(ob, x_own[:, t, :], rstd, rvec[:, 1040:2064], ALU.mult, ALU.mult, [xnm, "rstd", "rvec"], [on])
            else:
                cp(ob, x_own[:, t, :], [xnm], [on])
            dma("sp", "st%d" % (t % 2), outd[t * 128:(t + 1) * 128, :], ob, [on], ["out%d" % t])
            outs.append("out%d" % t)
        S.add("sp", None, r=outs)
        S.add("sp", None, extra=[(q, n - 1) for q, n in S.dcount.items() if q.startswith("st")])
        S.emit(nc, es)
    return nc


def _consts():
    p = np.arange(128)[:, None]
    f = np.arange(128)[None, :]
    same = (p // 64) == (f // 64)
    c = np.zeros((128, 1040), np.float32)
    c[:, 0:128] = np.eye(128)
    c[:, 128:256] = ((p <= f) & same)
    c[:, 256:384] = ((p > f) & same)
    c[:, 384:512] = (p // 64 == 0) * np.ones((1, 128))
    c[:, 512:640] = (p // 64 == 1) * np.ones((1, 128))
    c[:, 640:768] = -1.0
    c[:, 768:896] = np.where((p > f) & same, 0.0, NEG)
    c[:, 896:1024] = np.where((f >= p) & same, 0.0, NEG)
    c[:, 1024:1040] = np.arange(16)[None, :]
    return c


def _kc(w):
    return np.ascontiguousarray(w.reshape(8, 128, -1).transpose(1, 0, 2))


def _pv(v):
    return v.reshape(8, 128).T


_NC_CACHE = {}


def run(inputs, npre, nown, phases=("mix", "mixout", "xa", "peer", "final")):
    f = lambda k: np.asarray(inputs[k], dtype=np.float32)
    x = f("x")
    B, SEQ, _ = x.shape
    half = SEQ // 2
    assert half == nown * 128 and npre == nown
    perm = np.r_[0:1536, 2056:3080, 1536:2048, 3080:3592, 3592:4104, 2048:2052, 2052:2056, 4104:4108, 4108:4112]
    w_in = _kc(f("w_in")[0][:, perm])
    convw = np.concatenate([f("gdn_conv_w")[0], f("mlstm_conv_w")[0]], axis=1)
    convw = np.ascontiguousarray(convw.reshape(4, 20, 128).transpose(2, 1, 0).reshape(128, 80))
    pvec = np.zeros((128, 40), np.float32)
    pvec[:, 0:8] = _pv(f("norm_mix_w")[0])
    pvec[:, 8:12] = f("gdn_norm_w")[0][:, None]
    pvec[:, 12:16] = f("mlstm_norm_w")[0].reshape(4, 128).T
    pvec[:, 16:24] = _pv(f("norm_xa_w")[0])
    pvec[:, 24:32] = _pv(f("norm_mem_w")[0])
    pvec[:, 32:40] = _pv(f("norm_ffn_w")[0])
    rv = np.concatenate([f("gdn_a_log")[0], f("gdn_dt_bias")[0], f("mlstm_i_bias")[0], f("mlstm_f_bias")[0],
                         f("norm_ffn_w")[0], f("norm_final_w")])
    rvec = np.ascontiguousarray(np.broadcast_to(rv[None, :], (128, 2064)))
    subk = f("peer_sub_keys")[0].reshape(16, 128, 128)
    subk = np.ascontiguousarray(subk.transpose(2, 0, 1))
    common = {
        "w_in": w_in, "w_out": _kc(f("w_out")[0]), "xa_wq": _kc(f("xa_wq")[0]), "xa_wkv": _kc(f("xa_wkv")[0]),
        "xa_wo": _kc(f("xa_wo")[0]), "peer_wq": _kc(f("peer_wq")[0]), "subk": subk, "convw": convw, "pvec": pvec,
        "rvec": rvec, "cst": _consts(),
        "peer_uv": np.ascontiguousarray(np.concatenate([f("peer_u")[0], f("peer_v")[0]], axis=1)),
    }
    mem = f("mem")
    in_maps = []
    ncores = 2 * B
    for c in range(ncores):
        b, hf = c // 2, c % 2
        own = x[b, hf * half:(hf + 1) * half]
        prev = x[b, 0:half] if hf == 1 else np.zeros_like(own)
        m = dict(common)
        m["xs"] = np.ascontiguousarray(np.concatenate([prev, own], axis=0))
        m["mem"] = np.ascontiguousarray(mem[b])
        in_maps.append(m)
    key = (npre, nown, tuple(phases))
    if key not in _NC_CACHE:
        _NC_CACHE[key] = build(npre, nown, phases)
    nc = _NC_CACHE[key]
    res = run_bass_kernel_spmd(nc, in_maps, core_ids=list(range(ncores)))
    out = np.zeros((B, SEQ, D), np.float32)
    for c in range(ncores):
        b, hf = c // 2, c % 2
        out[b, hf * half:(hf + 1) * half] = res.results[c]["out"]
    return out


def kernel(**inputs):
    return run(inputs, 16, 16)
```

```python
import numpy as np
from contextlib import ExitStack
import concourse.bass as bass
import concourse.mybir as mybir
from concourse.bass_utils import run_bass_kernel_spmd

F32 = mybir.dt.float32
BF = mybir.dt.bfloat16
U32 = mybir.dt.uint32
I32 = mybir.dt.int32
AF = mybir.ActivationFunctionType
ALU = mybir.AluOpType
AX = mybir.AxisListType

ENGS = ("sp", "act", "dve", "pool", "pe")
D = 1024
NEG = -30000.0
EPS = 1e-6


class Sch:
    def __init__(self):
        self.ops = {e: [] for e in ENGS}
        self.ccount = {e: 0 for e in ENGS}
        self.dcount = {}
        self.seen = {e: {} for e in ENGS}
        self.waited = {}
        self.lastw = {}
        self.readers = {}
        self.cap = None

    def replay(self, items):
        for it in items:
            self.add(*it)

    def add(self, eng, fn, r=(), w=(), dq=None, chain=False, extra=()):
        if self.cap is not None:
            self.cap.append((eng, fn, list(r), list(w), dq, chain, list(extra)))
            return
        r = list(r)
        w = list(w)
        if dq and chain:
            w.append("__chain_" + dq)
        if dq:
            idx = self.dcount.get(dq, 0)
            self.dcount[dq] = idx + 1
            q = dq
        else:
            idx = self.ccount[eng]
            if fn is not None:
                self.ccount[eng] += 1
            q = eng
        deps = {}

        def need(d):
            if d is None:
                return
            if deps.get(d[0], -1) < d[1]:
                deps[d[0]] = d[1]

        for x in r:
            need(self.lastw.get(x))
            if isinstance(x, str) and x.startswith("ps"):
                for d in self.readers.get(x, ()):
                    if d[0] != q:
                        need(d)
        for x in w:
            need(self.lastw.get(x))
            for d in self.readers.get(x, ()):
                need(d)
        for d in extra:
            need(d)
        waits = []
        for dqn, di in deps.items():
            if dqn == "pe" and eng == "pe" and not dq:
                continue
            if self.seen[eng].get(dqn, -1) >= di:
                continue
            self.seen[eng][dqn] = di
            waits.append((dqn, di))
            self.waited.setdefault(dqn, set()).add(di)
        self.ops[eng].append((fn, waits, q, idx, bool(dq)))
        if fn is None:
            return
        me = (q, idx)
        for x in w:
            self.lastw[x] = me
            self.readers[x] = []
        for x in r:
            self.readers.setdefault(x, []).append(me)

    def barrier(self):
        ext = []
        for e in ENGS:
            if self.ccount[e] > 0:
                ext.append((e, self.ccount[e] - 1))
        for q, n in self.dcount.items():
            ext.append((q, n - 1))
        for e in ENGS:
            self.add(e, None, extra=ext)

    def emit(self, nc, es):
        queues = set(self.waited.keys()) | set(self.dcount.keys())
        sem = {q: es.enter_context(nc.semaphore("s_" + q)) for q in sorted(queues)}
        val = {}
        for q, s in self.waited.items():
            if q in self.dcount:
                continue
            for rank, i in enumerate(sorted(s)):
                val[(q, i)] = rank + 1

        def value(q, i):
            if q in self.dcount:
                return 16 * (i + 1)
            return val[(q, i)]

        def runner(en):
            def f(e):
                for fn, waits, q, idx, isdma in self.ops[en]:
                    for (wq, wi) in waits:
                        e.wait_ge(sem[wq], value(wq, wi))
                    if fn is None:
                        continue
                    ins = fn(e)
                    if isdma:
                        ins.then_inc(sem[q], 16)
                    elif (q, idx) in val:
                        ins.then_inc(sem[q], 1)
            return f

        with nc.Block() as block:
            block.sync(runner("sp"))
            block.scalar(runner("act"))
            block.vector(runner("dve"))
            block.gpsimd(runner("pool"))
            block.tensor(runner("pe"))


def bc(ap, shape):
    return ap.to_broadcast(list(shape))


def build(NPREV, NOWN, phases=("mix", "mixout", "xa", "peer", "final")):
    NT = NPREV + NOWN
    nc = bass.Bass("TRN2", target_bir_lowering=False)
    dr = lambda n, s, dt, k="ExternalInput": nc.dram_tensor(n, list(s), dt, kind=k).ap()
    xs = dr("xs", [NT * 128, D], F32)
    memd = dr("mem", [256, D], F32)
    w_in_d = dr("w_in", [128, 8, 4112], F32)
    w_out_d = dr("w_out", [128, 8, 1024], F32)
    wq_d = dr("xa_wq", [128, 8, 1024], F32)
    wkv_d = dr("xa_wkv", [128, 8, 2048], F32)
    wo_d = dr("xa_wo", [128, 8, 1024], F32)
    pwq_d = dr("peer_wq", [128, 8, 2048], F32)
    subk_d = dr("subk", [128, 16, 128], F32)
    convw_d = dr("convw", [128, 80], F32)
    pvec_d = dr("pvec", [128, 40], F32)
    rvec_d = dr("rvec", [128, 2064], F32)
    cst_d = dr("cst", [128, 1040], F32)
    puv_d = dr("peer_uv", [16384, 2 * D], F32)
    outd = dr("out", [NOWN * 128, D], F32, "ExternalOutput")
    uvb_d = nc.dram_tensor("uvb", [16384, 2 * D], BF, kind="Internal").ap()

    S = Sch()
    es = ExitStack()
    with es:
        sb = lambda n, s, dt: es.enter_context(nc.sbuf_tensor("sb_" + n, list(s), dt))
        ARW = 49152
        AR = sb("arena", [128, ARW], F32)
        ARB = AR.bitcast(BF)
        ARI = AR.bitcast(I32)
        ARU = AR.bitcast(U32)
        cst = sb("cst", [128, 1040], F32)
        pvec = sb("pvec", [128, 40], F32)
        rvec = sb("rvec", [128, 2064], F32)
        identb = sb("identb", [128, 128], BF)
        small = sb("small", [128, 256], F32)
        PS = [es.enter_context(nc.psum_tensor("ps%d" % i, [128, 512], F32)) for i in range(8)]
        PSB = [p.bitcast(BF) for p in PS]

        class Bump:
            def __init__(self, lo, hi):
                self.lo, self.hi, self.p = lo, hi, lo

            def f32(self, shape):
                n = int(np.prod(shape))
                o = self.p
                self.p += n
                assert self.p <= self.hi, ("arena overflow", self.p, self.hi)
                v = AR[:, o:o + n]
                return self._shape(v, shape)

            def _shape(self, v, shape):
                if len(shape) == 1:
                    return v
                if len(shape) == 2:
                    return v.rearrange("p (a b) -> p a b", a=shape[0], b=shape[1])
                return v.rearrange("p (a b c) -> p a b c", a=shape[0], b=shape[1], c=shape[2])

            def bf(self, shape):
                n = int(np.prod(shape))
                nw = (n + 1) // 2
                o = self.p
                self.p += nw
                assert self.p <= self.hi, ("arena overflow", self.p, self.hi)
                return self._shape(ARB[:, 2 * o:2 * o + n], shape)

            def i32(self, shape, u=False):
                n = int(np.prod(shape))
                o = self.p
                self.p += n
                assert self.p <= self.hi
                return self._shape((ARU if u else ARI)[:, o:o + n], shape)

        ident = cst[:, 0:128]
        tri = cst[:, 128:256]
        rev = cst[:, 256:384]
        ci = [cst[:, 384:512], cst[:, 512:640]]
        negones = cst[:, 640:768]
        maskL = cst[:, 768:896]
        maskU = cst[:, 896:1024]
        iota16 = cst[:, 1024:1040]

        rot = {"i": 0, "banks": [0, 1, 2, 3, 4]}

        def nb():
            b = rot["banks"][rot["i"] % len(rot["banks"])]
            rot["i"] += 1
            return b

        def psn(b):
            return "ps%d" % b

        def dve(fn, r, w):
            S.add("dve", fn, r, w)

        def act(fn, r, w):
            S.add("act", fn, r, w)

        def pe(fn, r, w):
            S.add("pe", fn, r, w)

        def mm(out, lhsT, rhs, start, stop, r, w):
            pe(lambda e: e.matmul(out, lhsT=lhsT, rhs=rhs, start=start, stop=stop), r, w)

        def tr(out, in_, r, w):
            pe(lambda e: e.transpose(out=out, in_=in_, identity=identb[:, :]), list(r) + ["identb"], w)

        def tr2(out, in_, r, w):
            mm(out, in_, identb[:, :], True, True, list(r) + ["identb"], w)

        def tt(out, in0, in1, op, r, w):
            dve(lambda e: e.tensor_tensor(out=out, in0=in0, in1=in1, op=op), r, w)

        def ts(out, in0, s1, op0, r, w, s2=None, op1=None):
            if op1 is None:
                dve(lambda e: e.tensor_scalar(out=out, in0=in0, scalar1=s1, scalar2=None, op0=op0), r, w)
            else:
                dve(lambda e: e.tensor_scalar(out=out, in0=in0, scalar1=s1, scalar2=s2, op0=op0, op1=op1), r, w)

        def stt(out, in0, scalar, in1, op0, op1, r, w, accum=None):
            if accum is None:
                dve(lambda e: e.scalar_tensor_tensor(out=out, in0=in0, scalar=scalar, in1=in1, op0=op0, op1=op1), r, w)
            else:
                dve(lambda e: e.scalar_tensor_tensor(out=out, in0=in0, scalar=scalar, in1=in1, op0=op0, op1=op1,
                                                     accum_out=accum), r, w)

        def cp(out, in_, r, w, eng="dve"):
            if eng == "dve":
                dve(lambda e: e.tensor_copy(out=out, in_=in_), r, w)
            else:
                act(lambda e: e.copy(out=out, in_=in_), r, w)

        def actf(out, in_, func, r, w, bias=None, scale=None, accum=None):
            kw = {}
            if bias is not None:
                kw["bias"] = bias
            if scale is not None:
                kw["scale"] = scale
            if accum is not None:
                kw["accum_out"] = accum
            act(lambda e: e.activation(out=out, in_=in_, func=func, **kw), r, w)

        def memset(ap, v, w):
            dve(lambda e: e.memset(ap, v), [], w)

        def dma(eng, q, out, in_, r, w, chain=False):
            S.add(eng, lambda e: e.dma_start(out=out, in_=in_), r, w, dq=q, chain=chain)

        def load_w(q, dst3, src3, name, ncols):
            step = 1024
            for kc in range(8):
                for c0 in range(0, ncols, step):
                    c1 = min(ncols, c0 + step)
                    dma("pool", q, dst3[:, kc, c0:c1], src3[:, kc, c0:c1], [], [name])

        dma("sp", "ld_c", cst[:, :], cst_d[:, :], [], ["cst"])
        dma("sp", "ld_pv", pvec[:, :], pvec_d[:, :], [], ["pvec"])
        dma("sp", "ld_rv", rvec[:, :], rvec_d[:, :], [], ["rvec"])
        cp(identb[:, :], ident, ["cst"], ["identb"])
        negA = small[:, 0:4]
        ibp = small[:, 4:8]
        actf(negA, rvec[:, 0:4], AF.Exp, ["rvec"], ["negA"])
        ts(negA, negA, -1.0, ALU.mult, ["negA"], ["negA"])
        ts(ibp, rvec[:, 8:12], float(np.log(128.0 ** -0.5)), ALU.add, ["rvec"], ["ibp"])

        def cast_tables(dep=(), lo=0, hi=64):
            if "peer" in phases:
                for i in range(lo, hi):
                    dma("pool", "cvt", uvb_d[i * 256:(i + 1) * 256, :], puv_d[i * 256:(i + 1) * 256, :], list(dep), ["uvb"])

        def rms_T(xap, xname, pvcol, xn, hT, tagp, nbf=None):
            ss = small[:, 16:17]
            rstd = small[:, 17:18]
            junk = xn
            actf(junk, xap, AF.Square, [xname], ["xn", "ss"], scale=1.0 / 32.0, accum=ss)
            actf(ss, ss, AF.Sqrt, ["ss"], ["ss"], bias=EPS, scale=1.0)
            dve(lambda e: e.reciprocal(out=rstd, in_=ss), ["ss"], ["rstd"])
            ts(xn, xap, rstd, ALU.mult, [xname, "rstd"], ["xn"])
            b = (nbf or nb)()
            for kc in range(8):
                tr(PSB[b][:, kc * 128:(kc + 1) * 128], xn[:, kc * 128:(kc + 1) * 128], ["xn"], [psn(b)])
            tt(hT, PSB[b][:, :].rearrange("p (a b) -> p a b", a=8, b=128),
               bc(pvec[:, pvcol:pvcol + 8].unsqueeze(2), [128, 8, 128]), ALU.mult, [psn(b), "pvec"], ["hT"])

        MIXT_OFF = 24512
        mixTs = ARB[:, 2 * MIXT_OFF:2 * MIXT_OFF + 16 * 1024].rearrange("p (t a b) -> p t a b", t=16, a=8, b=128)
        if "mix" in phases:
            WIN_OFF = MIXT_OFF + 8192
            w_in = ARB[:, 2 * WIN_OFF:2 * WIN_OFF + 8 * 4112].rearrange("p (a b) -> p a b", a=8, b=4112)
            load_w("ld_win", w_in, w_in_d, "w_in", 4112)
            B0 = Bump(0, MIXT_OFF)
            B1 = B0
            convw = B1.f32([80])
            dma("sp", "ld_cw", convw, convw_d[:, :], [], ["convw"])
            diag = B1.bf([80, 128])
            for i in range(80):
                ts(diag[:, i, :], ident, convw[:, i:i + 1], ALU.mult, ["cst", "convw"], ["diag"])
            xin = [B0.f32([D])]
            xn = B0.bf([D])
            hT = B0.bf([8, 128])
            pre = B0.bf([20, 131])
            cvA = B0.f32([8, 128])
            cvV = B0.f32([4, 128])
            sq = B0.f32([8, 128])
            R1 = sq[:, 0:4, :]
            R2 = sq[:, 4:8, :]
            tmpM = B0.f32([4, 128])
            E1 = B0.f32([4, 128])
            wT = B0.bf([4, 128])
            qkT = B0.bf([4, 128])
            E2g = B0.f32([4, 128])
            E2m = E2g
            qkn = B0.bf([8, 128])
            kbe = B0.bf([4, 128])
            vb = B0.bf([4, 128])
            kdm = B0.bf([2, 4, 128])
            qd = B0.bf([4, 128])
            kqT = B0.bf([8, 128])
            knT = kqT[:, 0:4, :]
            qnT = kqT[:, 4:8, :]
            qdm = B0.bf([2, 4, 128])
            Pb = [B0.bf([4, 128]), B0.bf([4, 128])]
            PTb = [B0.bf([4, 128]), B0.bf([4, 128])]
            R32 = B0.f32([4, 128])
            Rbf = B0.bf([4, 128])
            u32 = B0.f32([4, 128])
            vnew = B0.bf([4, 128])
            vnew2 = [B0.bf([4, 128]), B0.bf([4, 128])]
            S32 = B0.f32([4, 128])
            Sbf2 = [B0.bf([4, 128]), B0.bf([4, 128])]
            mqk_bf = qkn
            mqF = qd
            kwm = B0.bf([2, 4, 128])
            vaug = B0.bf([4, 130])
            mqkT = kqT
            qFm = B0.bf([2, 4, 128])
            PTm = B0.bf([4, 128])
            Cm32 = B0.f32([4, 129])
            Cmbf2 = [B0.bf([4, 130]), B0.bf([4, 130])]
            hm = R32
            o32 = u32
            zs = B1.f32([4, 128])
            so = B1.f32([4, 128])
            mixed = B1.bf([8, 128])
            g16 = B1.f32([16])
            T12 = B1.f32([12])
            E12 = B1.f32([12])
            L8 = B1.f32([8])
            LA = B1.f32([8])
            beta = B1.f32([4])
            nbeta = B1.f32([4])
            bEgc = B1.f32([4])
            lip = B1.f32([4])
            gcs = B1.f32([32])
            EGC = B1.f32([8])
            EREV = B1.f32([8])
            EGL2 = [B1.f32([16]), B1.f32([16])]
            ssq = B1.f32([8])
            rinv = B1.f32([8])
            den = B1.f32([4])
            ms4 = B1.f32([4])

            for (ap, nm) in ((pre, "pre"), (kdm, "kdm"), (qdm, "qdm"), (kwm, "kwm"), (qFm, "qFm"), (vnew, "vnew"), (vnew2[0], "vnew0"), (vnew2[1], "vnew1"),
                             (S32, "S32"), (Sbf2[0], "Sbf0"), (Sbf2[1], "Sbf1"), (Cm32, "Cm32"), (Cmbf2[0], "Cmbf0"), (Cmbf2[1], "Cmbf1")):
                memset(ap, 0.0, [nm])
            memset(vaug, 1.0, ["vaug"])

            OB, HX, HY = 5, 6, 7

            def hview(b):
                return PS[b][:, :].rearrange("p (a b) -> p a b", a=4, b=128)

            def hview2(b):
                return PS[b][:, :].rearrange("p (a b) -> p a b", a=2, b=256)

            roth = {"i": 0}

            def nbh():
                b = roth["i"] % 2
                roth["i"] += 1
                return b

            rot["banks"] = [2, 3, 4]

            def Mmat(lo, dest_ps):
                cp(R1, bc(LA[:, lo:lo + 4].unsqueeze(2), [128, 4, 128]), ["LA"], ["R1"])
                tt(R2, bc(tri.unsqueeze(1), [128, 4, 128]), bc(LA[:, lo:lo + 4].unsqueeze(2), [128, 4, 128]),
                   ALU.mult, ["cst", "LA"], ["R2"])
                mm(PS[dest_ps][:, :], tri, R1[:, :, :].rearrange("p a b -> p (a b)"), True, False, ["cst", "R1"], [psn(dest_ps)])
                mm(PS[dest_ps][:, :], negones, R2[:, :, :].rearrange("p a b -> p (a b)"), False, True, ["cst", "R2"], [psn(dest_ps)])

            def head(t):
                xt = xin[0]
                xnm = "xin0"
                dma("sp", "ld_x0", xt, xs[t * 128:(t + 1) * 128, :], [], [xnm])
                rms_T(xt, xnm, 0, xn, hT, "m", nbf=nbh)
                if t > 0:
                    cp(pre[:, :, 0:3], pre[:, :, 128:131], ["pre"], ["pre"])
                groups = [0, 1, 2, 3, 4] if t >= NPREV - 1 else [1, 2, 4]
                for g in groups:
                    b = nbh()
                    for h in range(4):
                        blk = g * 4 + h
                        for kc in range(8):
                            mm(PS[b][:, h * 128:(h + 1) * 128], w_in[:, kc, blk * 128:(blk + 1) * 128], hT[:, kc, :],
                               kc == 0, kc == 7, ["w_in", "hT"], [psn(b)])
                    cp(pre[:, g * 4:(g + 1) * 4, 3:131], hview(b), [psn(b)], ["pre"], eng="act")
                full = t >= NPREV
                bg = nbh()
                for kc in range(8):
                    mm(PS[bg][:, 0:16], hT[:, kc, :], w_in[:, kc, 4096:4112], kc == 0, kc == 7, ["w_in", "hT"], [psn(bg)])
                cp(g16, PS[bg][:, 0:16], [psn(bg)], ["g16"])
                tt(T12[:, 0:4], g16[:, 0:4], rvec[:, 4:8], ALU.add, ["g16", "rvec"], ["T12"])
                stt(T12[:, 4:8], g16[:, 12:16], 1.0, rvec[:, 12:16], ALU.mult, ALU.add, ["g16", "rvec"], ["T12"])
                ts(T12[:, 4:8], T12[:, 4:8], -1.0, ALU.mult, ["T12"], ["T12"])
                ts(T12[:, 8:12], g16[:, 4:8], -1.0, ALU.mult, ["g16"], ["T12"])
                actf(E12, T12, AF.Exp, ["T12"], ["E12"])
                actf(L8, E12[:, 0:8], AF.Ln, ["E12"], ["L8"], bias=1.0)
                tt(LA[:, 0:4], L8[:, 0:4], negA, ALU.mult, ["L8", "negA"], ["LA"])
                ts(LA[:, 4:8], L8[:, 4:8], -1.0, ALU.mult, ["L8"], ["LA"])
                ts(beta, E12[:, 8:12], 1.0, ALU.add, ["E12"], ["beta"])
                dve(lambda e: e.reciprocal(out=beta, in_=beta), ["beta"], ["beta"])
                ts(nbeta, beta, -1.0, ALU.mult, ["beta"], ["nbeta"])
                tt(lip, g16[:, 8:12], ibp, ALU.add, ["g16", "ibp"], ["lip"])
                bq = nbh()
                mm(PS[bq][:, 0:8], tri, LA, True, True, ["cst", "LA"], [psn(bq)])
                mm(PS[bq][:, 8:16], rev, LA, True, True, ["cst", "LA"], [psn(bq)])
                mm(PS[bq][:, 16:24], ci[0], LA, True, True, ["cst", "LA"], [psn(bq)])
                mm(PS[bq][:, 24:32], ci[1], LA, True, True, ["cst", "LA"], [psn(bq)])
                cp(gcs, PS[bq][:, 0:32], [psn(bq)], ["gcs"])
                tt(gcs[:, 12:16], gcs[:, 12:16], lip, ALU.add, ["gcs", "lip"], ["gcs"])
                actf(EGC, gcs[:, 0:8], AF.Exp, ["gcs"], ["EGC"])
                actf(EREV, gcs[:, 8:16], AF.Exp, ["gcs"], ["EREV"])
                actf(EGL2[t % 2], gcs[:, 16:32], AF.Exp, ["gcs"], ["EGL%d" % (t % 2)])
                tt(bEgc, beta, EGC[:, 0:4], ALU.mult, ["beta", "EGC"], ["bEgc"])

                bM = nbh()
                Mmat(0, bM)
                tt(tmpM, hview(bM), bc(maskL.unsqueeze(1), [128, 4, 128]), ALU.add, [psn(bM), "cst"], ["tmpM"])
                actf(E1, tmpM, AF.Exp, ["tmpM"], ["E1"])
                if full:
                    stt(tmpM, hview(bM), -1.0, bc(maskU.unsqueeze(1), [128, 4, 128]), ALU.mult, ALU.add, [psn(bM), "cst"], ["tmpM"])
                    actf(E2g, tmpM, AF.Exp, ["tmpM"], ["E2g"])


            def tail(t, full):
                for c in range(2):
                    Sbf, Sn = Sbf2[c], "Sbf%d" % c
                    Sbo, Son = Sbf2[1 - c], "Sbf%d" % (1 - c)
                    Cbo, Con = Cmbf2[1 - c], "Cmbf%d" % (1 - c)
                    bws = nb()
                    for h in range(4):
                        mm(PS[bws][:, h * 128:(h + 1) * 128], wT[:, h, :], Sbf[:, h, :], True, True, ["E1", Sn], [psn(bws)])
                    if full and c == 0:
                        for h in range(4):
                            mm(PS[OB][:, h * 128:(h + 1) * 128], qdm[:, 0, h, :], Sbf[:, h, :], True, True, ["qdm", Sn], [psn(OB)])
                        for h in range(4):
                            hb = HX if h < 2 else HY
                            mm(hview2(hb)[:, h % 2, 0:129], qFm[:, 0, h, :], Cmbf2[0][:, h, 0:129], True, True, ["qFm", "Cmbf0"], [psn(hb)])
                    vc = vnew2[c]
                    tt(vc, u32, hview(bws), ALU.subtract, ["u32", psn(bws)], ["vnew%d" % c])
                    bds = nb()
                    for h in range(4):
                        mm(PS[bds][:, h * 128:(h + 1) * 128], kdm[:, c, h, :], vc[:, h, :], True, True, ["kdm", "vnew%d" % c], [psn(bds)])
                    tt(S32, S32, bc(EGL2[t % 2][:, c * 8:c * 8 + 4].unsqueeze(2), [128, 4, 128]), ALU.mult, ["S32", "EGL%d" % (t % 2)], ["S32"])
                    tt(S32, S32, hview(bds), ALU.add, ["S32", psn(bds)], ["S32"])
                    cp(Sbo, S32, ["S32"], [Son], eng="act")
                    bc1, bc2 = nb(), nb()
                    for h in range(4):
                        hb = bc1 if h < 2 else bc2
                        mm(hview2(hb)[:, h % 2, 0:129], kwm[:, c, h, :], vaug[:, h, 0:129], True, True, ["kwm", "vaug"], [psn(hb)])
                    tt(Cm32, Cm32, bc(EGL2[t % 2][:, c * 8 + 4:c * 8 + 8].unsqueeze(2), [128, 4, 129]), ALU.mult, ["Cm32", "EGL%d" % (t % 2)], ["Cm32"])
                    tt(Cm32[:, 0:2, :], Cm32[:, 0:2, :], hview2(bc1)[:, :, 0:129], ALU.add, ["Cm32", psn(bc1)], ["Cm32"])
                    tt(Cm32[:, 2:4, :], Cm32[:, 2:4, :], hview2(bc2)[:, :, 0:129], ALU.add, ["Cm32", psn(bc2)], ["Cm32"])
                    cp(Cbo[:, :, 0:129], Cm32, ["Cm32"], [Con], eng="act")
                if not full:
                    return
                ts(vnew, vnew2[0], ci[0][:, 0:1], ALU.mult, ["vnew0", "cst"], ["vnew"])
                stt(vnew, vnew2[1], ci[1][:, 0:1], vnew, ALU.mult, ALU.add, ["vnew1", "cst", "vnew"], ["vnew"])
                bo2 = nb()
                for h in range(4):
                    mm(PS[bo2][:, h * 128:(h + 1) * 128], qdm[:, 1, h, :], Sbf2[1][:, h, :], True, False, ["qdm", "Sbf1"], [psn(bo2)])
                    mm(PS[bo2][:, h * 128:(h + 1) * 128], qkT[:, h, :], vnew[:, h, :], False, True, ["E1", "vnew"], [psn(bo2)])
                bh2 = [nb(), nb()]
                for h in range(4):
                    hb = bh2[h // 2]
                    mm(hview2(hb)[:, h % 2, 0:129], qFm[:, 1, h, :], Cmbf2[1][:, h, 0:129], True, False, ["qFm", "Cmbf1"], [psn(hb)])
                    mm(hview2(hb)[:, h % 2, 0:129], PTm[:, h, :], vaug[:, h, 0:129], False, True, ["PTm", "vaug"], [psn(hb)])
                cp(o32, hview(OB), [psn(OB)], ["u32"], eng="act")
                tt(o32, o32, hview(bo2), ALU.add, ["u32", psn(bo2)], ["u32"])
                tt(cvA[:, 0:4, :], o32, o32, ALU.mult, ["u32"], ["cvA"])
                dve(lambda e: e.tensor_reduce(out=ms4, in_=cvA[:, 0:4, :], axis=AX.X, op=ALU.add), ["cvA"], ["ms4"])
                actf(ms4, ms4, AF.Sqrt, ["ms4"], ["ms4"], bias=EPS, scale=1.0 / 128.0)
                dve(lambda e: e.reciprocal(out=ms4, in_=ms4), ["ms4"], ["ms4"])
                tt(o32, o32, bc(ms4.unsqueeze(2), [128, 4, 128]), ALU.mult, ["u32", "ms4"], ["u32"])
                tt(mixed[:, 0:4, :], o32, zs, ALU.mult, ["u32", "zs"], ["mixed"])
                for hb, hb2, h0 in ((HX, bh2[0], 0), (HY, bh2[1], 2)):
                    cp(hm[:, h0:h0 + 2, :], hview2(hb)[:, :, 0:128], [psn(hb)], ["R32"], eng="act")
                    cp(den[:, h0:h0 + 2], hview2(hb)[:, :, 128], [psn(hb)], ["den"], eng="act")
                    tt(hm[:, h0:h0 + 2, :], hm[:, h0:h0 + 2, :], hview2(hb2)[:, :, 0:128], ALU.add, ["R32", psn(hb2)], ["R32"])
                    tt(den[:, h0:h0 + 2], den[:, h0:h0 + 2], hview2(hb2)[:, :, 128], ALU.add, ["den", psn(hb2)], ["den"])
                stt(den, den, -1.0, den, ALU.mult, ALU.max, ["den"], ["den"])
                ts(den, den, 1.0, ALU.max, ["den"], ["den"])
                dve(lambda e: e.reciprocal(out=den, in_=den), ["den"], ["den"])
                tt(hm, hm, bc(den.unsqueeze(2), [128, 4, 128]), ALU.mult, ["R32", "den"], ["R32"])
                tt(cvA[:, 0:4, :], hm, hm, ALU.mult, ["R32"], ["cvA"])
                dve(lambda e: e.tensor_reduce(out=ms4, in_=cvA[:, 0:4, :], axis=AX.X, op=ALU.add), ["cvA"], ["ms4"])
                actf(ms4, ms4, AF.Sqrt, ["ms4"], ["ms4"], bias=EPS, scale=1.0 / 128.0)
                dve(lambda e: e.reciprocal(out=ms4, in_=ms4), ["ms4"], ["ms4"])
                tt(hm, hm, bc(ms4.unsqueeze(2), [128, 4, 128]), ALU.mult, ["R32", "ms4"], ["R32"])
                tt(mixed[:, 4:8, :], hm, so, ALU.mult, ["R32", "so"], ["mixed"])
                bmx = [nb(), nb()]
                for kc in range(8):
                    tr2(PS[bmx[kc // 4]][:, (kc % 4) * 128:(kc % 4 + 1) * 128], mixed[:, kc, :], ["mixed"], [psn(bmx[kc // 4])])
                to = t - NPREV
                for g2 in range(2):
                    tt(mixTs[:, to, g2 * 4:(g2 + 1) * 4, :], hview(bmx[g2]),
                       bc(pvec[:, 8 + g2 * 4:12 + g2 * 4].unsqueeze(2), [128, 4, 128]), ALU.mult, [psn(bmx[g2]), "pvec"], ["mixT%d" % to])

            def merge(a, b):
                out, i, j = [], 0, 0
                while i < len(a) or j < len(b):
                    if j >= len(b) or (i < len(a) and i * len(b) <= j * len(a)):
                        out.append(a[i]); i += 1
                    else:
                        out.append(b[j]); j += 1
                return out

            head(0)
            for t in range(NT):
                full = t >= NPREV
                bz = bmo = None
                if full:
                    bz = nb()
                    for kc in range(8):
                        mm(PS[bz][:, :], hT[:, kc, :], w_in[:, kc, 2560:3072], kc == 0, kc == 7, ["w_in", "hT"], [psn(bz)])
                    actf(zs, hview(bz), AF.Silu, [psn(bz)], ["zs"])
                    bmo = nb()
                    for kc in range(8):
                        mm(PS[bmo][:, :], hT[:, kc, :], w_in[:, kc, 3584:4096], kc == 0, kc == 7, ["w_in", "hT"], [psn(bmo)])
                    actf(so, hview(bmo), AF.Sigmoid, [psn(bmo)], ["so"])
                bmv = nb()
                for kc in range(8):
                    mm(PS[bmv][:, :], hT[:, kc, :], w_in[:, kc, 3072:3584], kc == 0, kc == 7, ["w_in", "hT"], [psn(bmv)])
                cp(vaug[:, :, 0:128], hview(bmv), [psn(bmv)], ["vaug"], eng="act")
                def conv_group(g, dst, dname):
                    b = nb()
                    for h in range(4):
                        blk = g * 4 + h
                        for tap in range(4):
                            mm(PS[b][:, h * 128:(h + 1) * 128], pre[:, blk, tap:tap + 128], diag[:, blk * 4 + tap, :],
                               tap == 0, tap == 3, ["pre", "diag"], [psn(b)])
                    actf(dst, hview(b), AF.Silu, [psn(b)], [dname])

                if full:
                    conv_group(0, cvA[:, 0:4, :], "cvA")
                conv_group(1, cvA[:, 4:8, :], "cvA")
                conv_group(2, cvV, "cvV")
                lo = 0 if full else 4
                tt(sq[:, lo:8, :], cvA[:, lo:8, :], cvA[:, lo:8, :], ALU.mult, ["cvA"], ["R1", "R2"])
                dve(lambda e, lo=lo: e.tensor_reduce(out=ssq[:, lo:8], in_=sq[:, lo:8, :], axis=AX.X, op=ALU.add), ["R1", "R2"], ["ssq"])
                actf(ssq[:, lo:8], ssq[:, lo:8], AF.Sqrt, ["ssq"], ["ssq"], bias=EPS, scale=1.0)
                dve(lambda e, lo=lo: e.reciprocal(out=rinv[:, lo:8], in_=ssq[:, lo:8]), ["ssq"], ["rinv"])
                if full:
                    ts(rinv[:, 0:4], rinv[:, 0:4], float(128.0 ** -0.5), ALU.mult, ["rinv"], ["rinv"])
                tt(qkn[:, lo:8, :], cvA[:, lo:8, :], bc(rinv[:, lo:8].unsqueeze(2), [128, 8 - lo, 128]), ALU.mult,
                   ["cvA", "rinv"], ["qkn"])
                kn = qkn[:, 4:8, :]
                tt(kbe, kn, bc(bEgc.unsqueeze(2), [128, 4, 128]), ALU.mult, ["qkn", "bEgc"], ["kbe"])
                tt(vb, cvV, bc(beta.unsqueeze(2), [128, 4, 128]), ALU.mult, ["cvV", "beta"], ["vb"])
                for c in range(2):
                    stt(kdm[:, c, :, :], kn, ci[c][:, 0:1], bc(EREV[:, 0:4].unsqueeze(2), [128, 4, 128]), ALU.mult, ALU.mult,
                        ["qkn", "EREV", "cst"], ["kdm"])
                bT = nb()
                bTq = nb()
                for h in range(4):
                    tr2(PS[bT][:, h * 128:(h + 1) * 128], kn[:, h, :], ["qkn"], [psn(bT)])
                if full:
                    tt(qd, qkn[:, 0:4, :], bc(EGC[:, 0:4].unsqueeze(2), [128, 4, 128]), ALU.mult, ["qkn", "EGC"], ["qd"])
                if full:
                    for h in range(4):
                        tr2(PS[bTq][:, h * 128:(h + 1) * 128], qkn[:, h, :], ["qkn"], [psn(bTq)])
                cp(knT, hview(bT), [psn(bT)], ["knT"], eng="act")
                if full:
                    cp(qnT, hview(bTq), [psn(bTq)], ["qnT"], eng="act")
                    bT2 = nb()
                    for h in range(4):
                        tr2(PS[bT2][:, h * 128:(h + 1) * 128], qd[:, h, :], ["qd"], [psn(bT2)])
                    for c in range(2):
                        cp(qdm[:, c, :, c * 64:(c + 1) * 64], hview(bT2)[:, :, c * 64:(c + 1) * 64],
                           [psn(bT2)], ["qdm"], eng="act")
                bG = nb()
                for h in range(4):
                    mm(PS[bG][:, h * 128:(h + 1) * 128], knT[:, h, :], knT[:, h, :], True, True, ["knT"], [psn(bG)])
                tt(tmpM, hview(bG), E1, ALU.mult, [psn(bG), "E1"], ["tmpM"])
                tt(Pb[0], tmpM, bc(nbeta.unsqueeze(2), [128, 4, 128]), ALU.mult, ["tmpM", "nbeta"], ["P0"])
                bP = nb()
                for h in range(4):
                    tr2(PS[bP][:, h * 128:(h + 1) * 128], Pb[0][:, h, :], ["P0"], [psn(bP)])
                pT_ps = hview(bP)
                tt(R32, pT_ps, bc(ident.unsqueeze(1), [128, 4, 128]), ALU.add, [psn(bP), "cst"], ["R32"])
                cp(PTb[0], pT_ps, [psn(bP)], ["PT0"])
                cp(Rbf, R32, ["R32"], ["Rbf"], eng="act")
                for k in range(1, 6):
                    pc, pp = k % 2, (k - 1) % 2
                    b1 = nb()
                    for h in range(4):
                        mm(PS[b1][:, h * 128:(h + 1) * 128], PTb[pp][:, h, :], Pb[pp][:, h, :], True, True,
                           ["P%d" % pp, "PT%d" % pp], [psn(b1)])
                    cp(Pb[pc], hview(b1), [psn(b1)], ["P%d" % pc], eng="act")
                    if k < 5:
                        b2 = nb()
                        for h in range(4):
                            mm(PS[b2][:, h * 128:(h + 1) * 128], Pb[pp][:, h, :], PTb[pp][:, h, :], True, True,
                               ["P%d" % pp, "PT%d" % pp], [psn(b2)])
                        cp(PTb[pc], hview(b2), [psn(b2)], ["PT%d" % pc])
                    b3 = nb()
                    for h in range(4):
                        mm(PS[b3][:, h * 128:(h + 1) * 128], Pb[pc][:, h, :], Rbf[:, h, :], True, True,
                           ["P%d" % pc, "Rbf"], [psn(b3)])
                    tt(R32, R32, hview(b3), ALU.add, ["R32", psn(b3)], ["R32"])
                    cp(Rbf, R32, ["R32"], ["Rbf"], eng="act")
                bu = nb()
                for h in range(4):
                    mm(PS[bu][:, h * 128:(h + 1) * 128], Rbf[:, h, :], vb[:, h, :], True, True, ["Rbf", "vb"], [psn(bu)])
                cp(u32, hview(bu), [psn(bu)], ["u32"], eng="act")
                bw = nb()
                for h in range(4):
                    mm(PS[bw][:, h * 128:(h + 1) * 128], kbe[:, h, :], Rbf[:, h, :], True, True, ["Rbf", "kbe"], [psn(bw)])
                cp(wT, hview(bw), [psn(bw)], ["E1"])
                if full:
                    bqk = nb()
                    for h in range(4):
                        mm(PS[bqk][:, h * 128:(h + 1) * 128], knT[:, h, :], qnT[:, h, :], True, True, ["knT", "qnT"], [psn(bqk)])
                    tt(qkT, hview(bqk), E2g, ALU.mult, [psn(bqk), "E2g"], ["E1"])
                if full:
                    conv_group(3, cvA[:, 0:4, :], "cvA")
                conv_group(4, cvA[:, 4:8, :], "cvA")
                mk = cvA[:, 4:8, :]
                for c in range(2):
                    stt(kwm[:, c, :, :], mk, ci[c][:, 0:1], bc(EREV[:, 4:8].unsqueeze(2), [128, 4, 128]), ALU.mult, ALU.mult,
                        ["cvA", "EREV", "cst"], ["kwm"])
                if full:
                    cp(mqk_bf, cvA, ["cvA"], ["qkn"])
                    tt(mqF, cvA[:, 0:4, :], bc(EGC[:, 4:8].unsqueeze(2), [128, 4, 128]), ALU.mult, ["cvA", "EGC"], ["qd"])
                    bT3 = [nb(), nb()]
                    for j in range(8):
                        tr2(PS[bT3[j // 4]][:, (j % 4) * 128:(j % 4 + 1) * 128], mqk_bf[:, j, :], ["qkn"], [psn(bT3[j // 4])])
                    for g2 in range(2):
                        cp(mqkT[:, g2 * 4:(g2 + 1) * 4, :], hview(bT3[g2]), [psn(bT3[g2])], ["knT", "qnT"], eng="act")
                    bT4 = nb()
                    for h in range(4):
                        tr2(PS[bT4][:, h * 128:(h + 1) * 128], mqF[:, h, :], ["qd"], [psn(bT4)])
                    for c in range(2):
                        cp(qFm[:, c, :, c * 64:(c + 1) * 64], hview(bT4)[:, :, c * 64:(c + 1) * 64],
                           [psn(bT4)], ["qFm"], eng="act")
                    bM2 = nb()
                    Mmat(4, bM2)
                    stt(tmpM, hview(bM2), -1.0, bc(maskU.unsqueeze(1), [128, 4, 128]), ALU.mult, ALU.add, [psn(bM2), "cst"], ["tmpM"])
                    tt(tmpM, tmpM, bc(lip.unsqueeze(2), [128, 4, 128]), ALU.add, ["tmpM", "lip"], ["tmpM"])
                    actf(E2m, tmpM, AF.Exp, ["tmpM"], ["E2g"])
                    bs = nb()
                    for h in range(4):
                        mm(PS[bs][:, h * 128:(h + 1) * 128], mqkT[:, 4 + h, :], mqkT[:, h, :], True, True, ["knT", "qnT"], [psn(bs)])
                    tt(PTm, hview(bs), E2m, ALU.mult, [psn(bs), "E2g"], ["PTm"])
                tl, hd = [], []
                S.cap = tl
                tail(t, full)
                S.cap = None
                if t + 1 < NT:
                    S.cap = hd
                    head(t + 1)
                    S.cap = None
                S.replay(merge(tl, hd))
                cpt = (64 + NT - 1) // NT
                cast_tables(["S32"], min(64, t * cpt), min(64, (t + 1) * cpt))
            S.barrier()

        x_own = AR[:, 0:NOWN * D].rearrange("p (t d) -> p t d", t=NOWN, d=D)
        rot["banks"] = [0, 1, 2, 3, 4, 5, 6, 7]
        if "mixout" in phases:
            WOUT_OFF = 16384
            w_out = ARB[:, 2 * WOUT_OFF:2 * WOUT_OFF + 8192].rearrange("p (a b) -> p a b", a=8, b=1024)
            load_w("ld_wout", w_out, w_out_d, "w_out", 1024)
            for t in range(NOWN):
                dma("sp", "ld_xo", x_own[:, t, :], xs[(NPREV + t) * 128:(NPREV + t + 1) * 128, :], [], ["xo%d" % t], chain=True)
                if "mix" not in phases:
                    continue
                for n in range(2):
                    b = nb()
                    for kc in range(8):
                        mm(PS[b][:, :], mixTs[:, t, kc, :], w_out[:, kc, n * 512:(n + 1) * 512], kc == 0, kc == 7,
                           ["mixT%d" % t, "w_out"], [psn(b)])
                    tt(x_own[:, t, n * 512:(n + 1) * 512], x_own[:, t, n * 512:(n + 1) * 512], PS[b][:, :], ALU.add,
                       ["xo%d" % t, psn(b)], ["xo%d" % t])
            S.barrier()

        XEND = NOWN * D
        PWQ_OFF = ARW - 8192
        pwq = ARB[:, 2 * PWQ_OFF:2 * PWQ_OFF + 16384].rearrange("p (a b) -> p a b", a=8, b=2048)
        if "mix" not in phases:
            cast_tables()

        if "xa" in phases:
            Bx = Bump(16384, PWQ_OFF)
            wq = Bx.bf([8, 1024])
            wkv = Bx.bf([8, 2048])
            wo = Bx.bf([8, 1024])
            load_w("ld_wq", wq, wq_d, "wq", 1024)
            load_w("ld_wkv", wkv, wkv_d, "wkv", 2048)
            load_w("ld_wo", wo, wo_d, "wo", 1024)
            if "peer" in phases:
                load_w("ld_pwq", pwq, pwq_d, "pwq", 2048)
            xn = Bx.bf([D])
            hT = Bx.bf([8, 128])
            mtile = Bx.f32([D])
            mT = Bx.bf([8, 256])
            kT = Bx.bf([8, 256])
            vbf = Bx.bf([2, 1024])
            qT2 = [Bx.bf([8, 128]), Bx.bf([8, 128])]
            pexp = mtile.rearrange("p (a b) -> p a b", a=4, b=256)
            pn = Bx.bf([4, 256])
            pT = Bx.bf([8, 128])
            oT = Bx.bf([8, 128])
            mx4 = Bx.f32([4])
            sm4 = Bx.f32([4])
            for mt in range(2):
                dma("sp", "ld_mem", mtile, memd[mt * 128:(mt + 1) * 128, :], [], ["mtile"], chain=True)
                rms_T(mtile, "mtile", 24, xn, hT, "mem")
                cp(mT[:, :, mt * 128:(mt + 1) * 128], hT, ["hT"], ["mT"])
            for blk in range(8):
                b = nb()
                for kc in range(8):
                    mm(PS[b][:, 0:256], wkv[:, kc, blk * 128:(blk + 1) * 128], mT[:, kc, :], kc == 0, kc == 7, ["wkv", "mT"], [psn(b)])
                cp(kT[:, blk, :], PS[b][:, 0:256], [psn(b)], ["kT"], eng="act")
            for mt in range(2):
                for n in range(2):
                    b = nb()
                    for kc in range(8):
                        mm(PS[b][:, :], mT[:, kc, mt * 128:(mt + 1) * 128], wkv[:, kc, 1024 + n * 512:1024 + (n + 1) * 512],
                           kc == 0, kc == 7, ["wkv", "mT"], [psn(b)])
                    cp(vbf[:, mt, n * 512:(n + 1) * 512], PS[b][:, :], [psn(b)], ["vbf"], eng="act")
            rotx = {"i": 0}

            def nbx():
                b = rotx["i"] % 2
                rotx["i"] += 1
                return b

            rot["banks"] = [2, 3, 4, 5, 6, 7]

            def xhead(t):
                xnm = "xo%d" % t
                rms_T(x_own[:, t, :], xnm, 16, xn, hT, "xa", nbf=nbx)
                for g in range(2):
                    b = nbx()
                    for j in range(4):
                        blk = g * 4 + j
                        for kc in range(8):
                            mm(PS[b][:, j * 128:(j + 1) * 128], wq[:, kc, blk * 128:(blk + 1) * 128], hT[:, kc, :], kc == 0, kc == 7,
                               ["wq", "hT"], [psn(b)])
                    cp(qT2[t % 2][:, g * 4:(g + 1) * 4, :], PS[b][:, :].rearrange("p (a b) -> p a b", a=4, b=128), [psn(b)], ["qT%d" % (t % 2)], eng="act")

            def xtail(t):
                xnm = "xo%d" % t
                sb_ = [nb(), nb()]
                for h in range(4):
                    b = sb_[h // 2]
                    for dc in range(2):
                        mm(PS[b][:, (h % 2) * 256:(h % 2 + 1) * 256], qT2[t % 2][:, h * 2 + dc, :], kT[:, h * 2 + dc, :], dc == 0, dc == 1,
                           ["qT%d" % (t % 2), "kT"], [psn(b)])
                for g in range(2):
                    b = sb_[g]
                    dve(lambda e, b=b, g=g: e.tensor_reduce(out=mx4[:, g * 2:g * 2 + 2],
                                                             in_=PS[b][:, :].rearrange("p (a b) -> p a b", a=2, b=256),
                                                             axis=AX.X, op=ALU.max), [psn(b)], ["mx4"])
                ts(mx4, mx4, -1.0 / 16.0, ALU.mult, ["mx4"], ["mx4"])
                for h in range(4):
                    b = sb_[h // 2]
                    actf(pexp[:, h, :], PS[b][:, (h % 2) * 256:(h % 2 + 1) * 256], AF.Exp, [psn(b), "mx4"], ["pexp", "sm4"],
                         bias=mx4[:, h:h + 1], scale=1.0 / 16.0, accum=sm4[:, h:h + 1])
                dve(lambda e: e.reciprocal(out=sm4, in_=sm4), ["sm4"], ["sm4"])
                tt(pn, pexp, bc(sm4.unsqueeze(2), [128, 4, 256]), ALU.mult, ["pexp", "sm4"], ["pn"])
                b = nb()
                for h in range(4):
                    for mt in range(2):
                        tr(PSB[b][:, (h * 2 + mt) * 128:(h * 2 + mt + 1) * 128], pn[:, h, mt * 128:(mt + 1) * 128], ["pn"], [psn(b)])
                cp(pT, PSB[b][:, :].rearrange("p (a b) -> p a b", a=8, b=128), [psn(b)], ["pT"], eng="act")
                for g in range(2):
                    b = nb()
                    for j in range(4):
                        blk = g * 4 + j
                        h, dc = blk // 2, blk % 2
                        for mt in range(2):
                            mm(PS[b][:, j * 128:(j + 1) * 128], vbf[:, mt, h * 256 + dc * 128:h * 256 + (dc + 1) * 128],
                               pT[:, h * 2 + mt, :], mt == 0, mt == 1, ["vbf", "pT"], [psn(b)])
                    cp(oT[:, g * 4:(g + 1) * 4, :], PS[b][:, :].rearrange("p (a b) -> p a b", a=4, b=128), [psn(b)], ["oT"], eng="act")
                for n in range(2):
                    b = nb()
                    for kc in range(8):
                        mm(PS[b][:, :], oT[:, kc, :], wo[:, kc, n * 512:(n + 1) * 512], kc == 0, kc == 7, ["oT", "wo"], [psn(b)])
                    tt(x_own[:, t, n * 512:(n + 1) * 512], x_own[:, t, n * 512:(n + 1) * 512], PS[b][:, :], ALU.add,
                       [xnm, psn(b)], [xnm])

            def xmerge(a, b):
                out, i, j = [], 0, 0
                while i < len(a) or j < len(b):
                    if j >= len(b) or (i < len(a) and i * len(b) <= j * len(a)):
                        out.append(a[i]); i += 1
                    else:
                        out.append(b[j]); j += 1
                return out

            xhead(0)
            for t in range(NOWN):
                tl, hd = [], []
                S.cap = tl
                xtail(t)
                S.cap = None
                if t + 1 < NOWN:
                    S.cap = hd
                    xhead(t + 1)
                    S.cap = None
                S.replay(xmerge(tl, hd))
            S.barrier()

        if "peer" in phases:
            Bp = Bump(16384, PWQ_OFF)
            subk = Bp.bf([16, 128])
            if "xa" not in phases:
                load_w("ld_pwq", pwq, pwq_d, "pwq", 2048)
            dma("pool", "ld_sk", subk[:, 0:8, :], subk_d[:, 0:8, :], [], ["subk"])
            dma("pool", "ld_sk", subk[:, 8:16, :], subk_d[:, 8:16, :], [], ["subk"])
            xn = Bp.bf([D])
            hT = Bp.bf([8, 128])
            htok2 = [Bp.f32([D]), Bp.f32([D])]
            qT = Bp.bf([16, 128])
            sc = Bp.f32([16, 128])
            t2k = Bp.f32([2048])
            sc2 = t2k.rearrange("p (a b) -> p a b", a=16, b=128)
            topv = Bp.f32([16, 16])
            topi = Bp.i32([16, 16], u=True)
            topif = Bp.f32([16, 16])
            cand = sc[:, :, :].rearrange("p a b -> p (a b)").rearrange("p (a b) -> p a b", a=8, b=256)
            cand2 = t2k.rearrange("p (a b) -> p a b", a=8, b=256)
            bestv = Bp.f32([8, 16])
            pos = Bp.i32([8, 16], u=True)
            pa = Bp.i32([8, 16], u=True)
            pb_ = Bp.i32([8, 16], u=True)
            paf = Bp.f32([8, 16])
            pbf = Bp.f32([8, 16])
            oh = t2k.rearrange("p (a b c) -> p a b c", a=8, b=16, c=16)
            isel = Bp.f32([8, 16])
            jsel = Bp.f32([8, 16])
            eidf = Bp.f32([128])
            eid2 = [Bp.i32([128]), Bp.i32([128])]
            gate2 = [Bp.f32([8, 16]), Bp.f32([8, 16])]
            gsum = Bp.f32([8])
            actv = Bp.f32([128])
            wgt = Bp.f32([128])
            junk = Bp.bf([D])
            NG = 8
            dgs = [Bp.bf([128]) for _ in range(4)]
            rot["banks"] = [0, 1, 2, 3, 4, 5]
            gbuf = [Bp.bf([2 * D]) for _ in range(NG)]
            def prologue(t):
                par = t % 2
                eid, gate, htok = eid2[par], gate2[par], htok2[par]
                EIDN, GATEN, HTOKN = "eid%d" % par, "gate%d" % par, "htok%d" % par
                xnm = "xo%d" % t
                rms_T(x_own[:, t, :], xnm, 32, xn, hT, "pf")
                stt(htok, x_own[:, t, :], small[:, 17:18], rvec[:, 16:1040], ALU.mult, ALU.mult, [xnm, "rstd", "rvec"], [HTOKN])
                for g in range(4):
                    b = nb()
                    for j in range(4):
                        blk = g * 4 + j
                        for kc in range(8):
                            mm(PS[b][:, j * 128:(j + 1) * 128], pwq[:, kc, blk * 128:(blk + 1) * 128], hT[:, kc, :], kc == 0, kc == 7,
                               ["pwq", "hT"], [psn(b)])
                    cp(qT[:, g * 4:(g + 1) * 4, :], PS[b][:, :].rearrange("p (a b) -> p a b", a=4, b=128), [psn(b)], ["qT"], eng="act")
                for g in range(4):
                    b = nb()
                    for j in range(4):
                        blk = g * 4 + j
                        mm(PS[b][:, j * 128:(j + 1) * 128], qT[:, blk, :], subk[:, blk, :], True, True, ["qT", "subk"], [psn(b)])
                    cp(sc[:, g * 4:(g + 1) * 4, :], PS[b][:, :].rearrange("p (a b) -> p a b", a=4, b=128), [psn(b)], ["sc"], eng="act")
                for blk in range(16):
                    dve(lambda e, blk=blk: e.max(out=topv[:, blk, 0:8], in_=sc[:, blk, :]), ["sc"], ["topv"])
                    dve(lambda e, blk=blk: e.match_replace(out=sc2[:, blk, :], in_to_replace=topv[:, blk, 0:8],
                                                             in_values=sc[:, blk, :], imm_value=-1e30), ["sc", "topv"], ["t2k"])
                    dve(lambda e, blk=blk: e.max(out=topv[:, blk, 8:16], in_=sc2[:, blk, :]), ["t2k"], ["topv"])
                    dve(lambda e, blk=blk: e.max_index(out=topi[:, blk, 0:8], in_max=topv[:, blk, 0:8], in_values=sc[:, blk, :]),
                        ["sc", "topv"], ["topi"])
                    dve(lambda e, blk=blk: e.max_index(out=topi[:, blk, 8:16], in_max=topv[:, blk, 8:16], in_values=sc[:, blk, :]),
                        ["sc", "topv"], ["topi"])
                cp(topif, topi, ["topi"], ["topif"])
                tv = topv[:, :, :].rearrange("p (h two) k -> p h two k", h=8, two=2)
                tif = topif[:, :, :].rearrange("p (h two) k -> p h two k", h=8, two=2)
                candv = cand[:, :, :].rearrange("p h (a b) -> p h a b", a=16, b=16)
                tt(candv, bc(tv[:, :, 0, :].unsqueeze(3), [128, 8, 16, 16]), bc(tv[:, :, 1, :].unsqueeze(2), [128, 8, 16, 16]),
                   ALU.add, ["topv"], ["sc"])
                for h in range(8):
                    dve(lambda e, h=h: e.max(out=bestv[:, h, 0:8], in_=cand[:, h, :]), ["sc"], ["bestv"])
                    dve(lambda e, h=h: e.match_replace(out=cand2[:, h, :], in_to_replace=bestv[:, h, 0:8],
                                                         in_values=cand[:, h, :], imm_value=-1e30), ["sc", "bestv"], ["t2k"])
                    dve(lambda e, h=h: e.max(out=bestv[:, h, 8:16], in_=cand2[:, h, :]), ["t2k"], ["bestv"])
                    dve(lambda e, h=h: e.max_index(out=pos[:, h, 0:8], in_max=bestv[:, h, 0:8], in_values=cand[:, h, :]),
                        ["sc", "bestv"], ["pos"])
                    dve(lambda e, h=h: e.max_index(out=pos[:, h, 8:16], in_max=bestv[:, h, 8:16], in_values=cand[:, h, :]),
                        ["sc", "bestv"], ["pos"])
                dve(lambda e: e.tensor_single_scalar(out=pa, in_=pos, scalar=4, op=ALU.arith_shift_right), ["pos"], ["pa"])
                dve(lambda e: e.tensor_single_scalar(out=pb_, in_=pos, scalar=15, op=ALU.bitwise_and), ["pos"], ["pb"])
                cp(paf, pa, ["pa"], ["paf"])
                cp(pbf, pb_, ["pb"], ["pbf"])
                io4 = bc(iota16.unsqueeze(1).unsqueeze(1), [128, 8, 16, 16])
                tt(oh, io4, bc(paf.unsqueeze(3), [128, 8, 16, 16]), ALU.is_equal, ["cst", "paf"], ["t2k"])
                tt(oh, oh, bc(tif[:, :, 0, :].unsqueeze(2), [128, 8, 16, 16]), ALU.mult, ["t2k", "topif"], ["t2k"])
                dve(lambda e: e.tensor_reduce(out=isel, in_=oh, axis=AX.X, op=ALU.add), ["t2k"], ["isel"])
                tt(oh, io4, bc(pbf.unsqueeze(3), [128, 8, 16, 16]), ALU.is_equal, ["cst", "pbf"], ["t2k"])
                tt(oh, oh, bc(tif[:, :, 1, :].unsqueeze(2), [128, 8, 16, 16]), ALU.mult, ["t2k", "topif"], ["t2k"])
                dve(lambda e: e.tensor_reduce(out=jsel, in_=oh, axis=AX.X, op=ALU.add), ["t2k"], ["jsel"])
                stt(eidf.rearrange("p (h k) -> p h k", h=8, k=16), isel, 128.0, jsel, ALU.mult, ALU.add, ["isel", "jsel"], ["eidf"])
                cp(eid, eidf, ["eidf"], [EIDN])
                tt(gate, bestv, bc(bestv[:, :, 0:1], [128, 8, 16]), ALU.subtract, ["bestv"], [GATEN])
                actf(gate, gate, AF.Exp, [GATEN], [GATEN])
                dve(lambda e: e.tensor_reduce(out=gsum, in_=gate, axis=AX.X, op=ALU.add), [GATEN], ["gsum"])
                dve(lambda e: e.reciprocal(out=gsum, in_=gsum), ["gsum"], ["gsum"])
                tt(gate, gate, bc(gsum.unsqueeze(2), [128, 8, 16]), ALU.mult, [GATEN, "gsum"], [GATEN])

            def slotloop(t, inj):
                par = t % 2
                xnm = "xo%d" % t
                eid, gate, htok = eid2[par], gate2[par], htok2[par]
                EIDN, GATEN, HTOKN = "eid%d" % par, "gate%d" % par, "htok%d" % par
                gflat = gate[:, :, :].rearrange("p h k -> p (h k)")

                def fin(s_):
                    gb_ = gbuf[s_ % NG]
                    gn_ = "gb%d" % (s_ % NG)
                    dg = dgs[s_ % 4]
                    dn = "dg%d" % (s_ % 4)
                    actf(wgt[:, s_:s_ + 1], wgt[:, s_:s_ + 1], AF.Copy, ["wgt%d" % s_, GATEN], ["wgt%d" % s_], scale=gflat[:, s_:s_ + 1])
                    actf(dg, ident, AF.Copy, ["cst", "wgt%d" % s_], [dn], scale=wgt[:, s_:s_ + 1])
                    for n_ in range(2):
                        mm(PS[6 + n_][:, :], dg, gb_[:, D + n_ * 512:D + (n_ + 1) * 512], s_ == 0, s_ == 127, [dn, gn_], [psn(6 + n_)])

                for s in range(128):
                    gb = gbuf[s % NG]
                    gn = "gb%d" % (s % NG)
                    S.add("pool", lambda e, gb=gb, s=s: e.indirect_dma_start(
                        out=gb, out_offset=None, in_=uvb_d[:, :],
                        in_offset=bass.IndirectOffsetOnAxis(ap=eid[:, s:s + 1], axis=0)), [EIDN, "uvb"], [gn], dq="g%d" % (s % NG))
                    stt(junk, gb[:, 0:D], 1.0, htok, ALU.mult, ALU.mult, [gn, HTOKN], ["junk", "actv%d" % s], accum=actv[:, s:s + 1])
                    actf(wgt[:, s:s + 1], actv[:, s:s + 1], AF.Gelu, ["actv%d" % s], ["wgt%d" % s])
                    if s >= 1:
                        fin(s - 1)
                    if inj and s < 120:
                        S.replay(inj[len(inj) * s // 120:len(inj) * (s + 1) // 120])
                fin(127)
                for n_ in range(2):
                    tt(x_own[:, t, n_ * 512:(n_ + 1) * 512], x_own[:, t, n_ * 512:(n_ + 1) * 512], PS[6 + n_][:, :], ALU.add,
                       [xnm, psn(6 + n_)], [xnm])

            prologue(0)
            for t in range(NOWN):
                inj = []
                if t + 1 < NOWN:
                    S.cap = inj
                    prologue(t + 1)
                    S.cap = None
                slotloop(t, inj)
            S.barrier()

        Bf = Bump(16384, ARW)
        obuf = [Bf.f32([D]), Bf.f32([D])]
        junkb = Bf.bf([D])
        outs = []
        for t in range(NOWN):
            xnm = "xo%d" % t
            ob = obuf[t % 2]
            on = "ob%d" % (t % 2)
            if "final" in phases:
                ss = small[:, 16:17]
                rstd = small[:, 17:18]
                actf(junkb, x_own[:, t, :], AF.Square, [xnm], ["junkb", "ss"], scale=1.0 / 32.0, accum=ss)
                actf(ss, ss, AF.Sqrt, ["ss"], ["ss"], bias=EPS, scale=1.0)
                dve(lambda e: e.reciprocal(out=rstd, in_=ss), ["ss"], ["rstd"])
                stt(ob, x_own[:, t, :], rstd, rvec[:, 1040:2064], ALU.mult, ALU.mult, [xnm, "rstd", "rvec"], [on])
            else:
                cp(ob, x_own[:, t, :], [xnm], [on])
            dma("sp", "st%d" % (t % 2), outd[t * 128:(t + 1) * 128, :], ob, [on], ["out%d" % t])
            outs.append("out%d" % t)
        S.add("sp", None, r=outs)
        S.add("sp", None, extra=[(q, n - 1) for q, n in S.dcount.items() if q.startswith("st")])
        S.emit(nc, es)
    return nc


def _consts():
    p = np.arange(128)[:, None]
    f = np.arange(128)[None, :]
    same = (p // 64) == (f // 64)
    c = np.zeros((128, 1040), np.float32)
    c[:, 0:128] = np.eye(128)
    c[:, 128:256] = ((p <= f) & same)
    c[:, 256:384] = ((p > f) & same)
    c[:, 384:512] = (p // 64 == 0) * np.ones((1, 128))
    c[:, 512:640] = (p // 64 == 1) * np.ones((1, 128))
    c[:, 640:768] = -1.0
    c[:, 768:896] = np.where((p > f) & same, 0.0, NEG)
    c[:, 896:1024] = np.where((f >= p) & same, 0.0, NEG)
    c[:, 1024:1040] = np.arange(16)[None, :]
    return c


def _kc(w):
    return np.ascontiguousarray(w.reshape(8, 128, -1).transpose(1, 0, 2))


def _pv(v):
    return v.reshape(8, 128).T


_NC_CACHE = {}


def run(inputs, npre, nown, phases=("mix", "mixout", "xa", "peer", "final")):
    f = lambda k: np.asarray(inputs[k], dtype=np.float32)
    x = f("x")
    B, SEQ, _ = x.shape
    half = SEQ // 2
    assert half == nown * 128 and npre == nown
    perm = np.r_[0:1536, 2056:3080, 1536:2048, 3080:3592, 3592:4104, 2048:2052, 2052:2056, 4104:4108, 4108:4112]
    w_in = _kc(f("w_in")[0][:, perm])
    convw = np.concatenate([f("gdn_conv_w")[0], f("mlstm_conv_w")[0]], axis=1)
    convw = np.ascontiguousarray(convw.reshape(4, 20, 128).transpose(2, 1, 0).reshape(128, 80))
    pvec = np.zeros((128, 40), np.float32)
    pvec[:, 0:8] = _pv(f("norm_mix_w")[0])
    pvec[:, 8:12] = f("gdn_norm_w")[0][:, None]
    pvec[:, 12:16] = f("mlstm_norm_w")[0].reshape(4, 128).T
    pvec[:, 16:24] = _pv(f("norm_xa_w")[0])
    pvec[:, 24:32] = _pv(f("norm_mem_w")[0])
    pvec[:, 32:40] = _pv(f("norm_ffn_w")[0])
    rv = np.concatenate([f("gdn_a_log")[0], f("gdn_dt_bias")[0], f("mlstm_i_bias")[0], f("mlstm_f_bias")[0],
                         f("norm_ffn_w")[0], f("norm_final_w")])
    rvec = np.ascontiguousarray(np.broadcast_to(rv[None, :], (128, 2064)))
    subk = f("peer_sub_keys")[0].reshape(16, 128, 128)
    subk = np.ascontiguousarray(subk.transpose(2, 0, 1))
    common = {
        "w_in": w_in, "w_out": _kc(f("w_out")[0]), "xa_wq": _kc(f("xa_wq")[0]), "xa_wkv": _kc(f("xa_wkv")[0]),
        "xa_wo": _kc(f("xa_wo")[0]), "peer_wq": _kc(f("peer_wq")[0]), "subk": subk, "convw": convw, "pvec": pvec,
        "rvec": rvec, "cst": _consts(),
        "peer_uv": np.ascontiguousarray(np.concatenate([f("peer_u")[0], f("peer_v")[0]], axis=1)),
    }
    mem = f("mem")
    in_maps = []
    ncores = 2 * B
    for c in range(ncores):
        b, hf = c // 2, c % 2
        own = x[b, hf * half:(hf + 1) * half]
        prev = x[b, 0:half] if hf == 1 else np.zeros_like(own)
        m = dict(common)
        m["xs"] = np.ascontiguousarray(np.concatenate([prev, own], axis=0))
        m["mem"] = np.ascontiguousarray(mem[b])
        in_maps.append(m)
    key = (npre, nown, tuple(phases))
    if key not in _NC_CACHE:
        _NC_CACHE[key] = build(npre, nown, phases)
    nc = _NC_CACHE[key]
    res = run_bass_kernel_spmd(nc, in_maps, core_ids=list(range(ncores)))
    out = np.zeros((B, SEQ, D), np.float32)
    for c in range(ncores):
        b, hf = c // 2, c % 2
        out[b, hf * half:(hf + 1) * half] = res.results[c]["out"]
    return out


def kernel(**inputs):
    return run(inputs, 16, 16)
```

```python
import numpy as np
from contextlib import ExitStack
import concourse.bass as bass
import concourse.mybir as mybir
from concourse.bass_utils import run_bass_kernel_spmd

F32 = mybir.dt.float32
BF = mybir.dt.bfloat16
U32 = mybir.dt.uint32
I32 = mybir.dt.int32
AF = mybir.ActivationFunctionType
ALU = mybir.AluOpType
AX = mybir.AxisListType

ENGS = ("sp", "act", "dve", "pool", "pe")
D = 1024
NEG = -30000.0
EPS = 1e-6


class Sch:
    def __init__(self):
        self.ops = {e: [] for e in ENGS}
        self.ccount = {e: 0 for e in ENGS}
        self.dcount = {}
        self.seen = {e: {} for e in ENGS}
        self.waited = {}
        self.lastw = {}
        self.readers = {}
        self.cap = None

    def replay(self, items):
        for it in items:
            self.add(*it)

    def add(self, eng, fn, r=(), w=(), dq=None, chain=False, extra=()):
        if self.cap is not None:
            self.cap.append((eng, fn, list(r), list(w), dq, chain, list(extra)))
            return
        r = list(r)
        w = list(w)
        if dq and chain:
            w.append("__chain_" + dq)
        if dq:
            idx = self.dcount.get(dq, 0)
            self.dcount[dq] = idx + 1
            q = dq
        else:
            idx = self.ccount[eng]
            if fn is not None:
                self.ccount[eng] += 1
            q = eng
        deps = {}

        def need(d):
            if d is None:
                return
            if deps.get(d[0], -1) < d[1]:
                deps[d[0]] = d[1]

        for x in r:
            need(self.lastw.get(x))
            if isinstance(x, str) and x.startswith("ps"):
                for d in self.readers.get(x, ()):
                    if d[0] != q:
                        need(d)
        for x in w:
            need(self.lastw.get(x))
            for d in self.readers.get(x, ()):
                need(d)
        for d in extra:
            need(d)
        waits = []
        for dqn, di in deps.items():
            if dqn == "pe" and eng == "pe" and not dq:
                continue
            if self.seen[eng].get(dqn, -1) >= di:
                continue
            self.seen[eng][dqn] = di
            waits.append((dqn, di))
            self.waited.setdefault(dqn, set()).add(di)
        self.ops[eng].append((fn, waits, q, idx, bool(dq)))
        if fn is None:
            return
        me = (q, idx)
        for x in w:
            self.lastw[x] = me
            self.readers[x] = []
        for x in r:
            self.readers.setdefault(x, []).append(me)

    def barrier(self):
        ext = []
        for e in ENGS:
            if self.ccount[e] > 0:
                ext.append((e, self.ccount[e] - 1))
        for q, n in self.dcount.items():
            ext.append((q, n - 1))
        for e in ENGS:
            self.add(e, None, extra=ext)

    def emit(self, nc, es):
        queues = set(self.waited.keys()) | set(self.dcount.keys())
        sem = {q: es.enter_context(nc.semaphore("s_" + q)) for q in sorted(queues)}
        val = {}
        for q, s in self.waited.items():
            if q in self.dcount:
                continue
            for rank, i in enumerate(sorted(s)):
                val[(q, i)] = rank + 1

        def value(q, i):
            if q in self.dcount:
                return 16 * (i + 1)
            return val[(q, i)]

        def runner(en):
            def f(e):
                for fn, waits, q, idx, isdma in self.ops[en]:
                    for (wq, wi) in waits:
                        e.wait_ge(sem[wq], value(wq, wi))
                    if fn is None:
                        continue
                    ins = fn(e)
                    if isdma:
                        ins.then_inc(sem[q], 16)
                    elif (q, idx) in val:
                        ins.then_inc(sem[q], 1)
            return f

        with nc.Block() as block:
            block.sync(runner("sp"))
            block.scalar(runner("act"))
            block.vector(runner("dve"))
            block.gpsimd(runner("pool"))
            block.tensor(runner("pe"))


def bc(ap, shape):
    return ap.to_broadcast(list(shape))


def build(NPREV, NOWN, phases=("mix", "mixout", "xa", "peer", "final")):
    NT = NPREV + NOWN
    nc = bass.Bass("TRN2", target_bir_lowering=False)
    dr = lambda n, s, dt, k="ExternalInput": nc.dram_tensor(n, list(s), dt, kind=k).ap()
    xs = dr("xs", [NT * 128, D], F32)
    memd = dr("mem", [256, D], F32)
    w_in_d = dr("w_in", [128, 8, 4112], F32)
    w_out_d = dr("w_out", [128, 8, 1024], F32)
    wq_d = dr("xa_wq", [128, 8, 1024], F32)
    wkv_d = dr("xa_wkv", [128, 8, 2048], F32)
    wo_d = dr("xa_wo", [128, 8, 1024], F32)
    pwq_d = dr("peer_wq", [128, 8, 2048], F32)
    subk_d = dr("subk", [128, 16, 128], F32)
    convw_d = dr("convw", [128, 80], F32)
    pvec_d = dr("pvec", [128, 40], F32)
    rvec_d = dr("rvec", [128, 2064], F32)
    cst_d = dr("cst", [128, 1040], F32)
    puv_d = dr("peer_uv", [16384, 2 * D], F32)
    outd = dr("out", [NOWN * 128, D], F32, "ExternalOutput")
    uvb_d = nc.dram_tensor("uvb", [16384, 2 * D], BF, kind="Internal").ap()

    S = Sch()
    es = ExitStack()
    with es:
        sb = lambda n, s, dt: es.enter_context(nc.sbuf_tensor("sb_" + n, list(s), dt))
        ARW = 49152
        AR = sb("arena", [128, ARW], F32)
        ARB = AR.bitcast(BF)
        ARI = AR.bitcast(I32)
        ARU = AR.bitcast(U32)
        cst = sb("cst", [128, 1040], F32)
        pvec = sb("pvec", [128, 40], F32)
        rvec = sb("rvec", [128, 2064], F32)
        identb = sb("identb", [128, 128], BF)
        small = sb("small", [128, 256], F32)
        PS = [es.enter_context(nc.psum_tensor("ps%d" % i, [128, 512], F32)) for i in range(8)]
        PSB = [p.bitcast(BF) for p in PS]

        class Bump:
            def __init__(self, lo, hi):
                self.lo, self.hi, self.p = lo, hi, lo

            def f32(self, shape):
                n = int(np.prod(shape))
                o = self.p
                self.p += n
                assert self.p <= self.hi, ("arena overflow", self.p, self.hi)
                v = AR[:, o:o + n]
                return self._shape(v, shape)

            def _shape(self, v, shape):
                if len(shape) == 1:
                    return v
                if len(shape) == 2:
                    return v.rearrange("p (a b) -> p a b", a=shape[0], b=shape[1])
                return v.rearrange("p (a b c) -> p a b c", a=shape[0], b=shape[1], c=shape[2])

            def bf(self, shape):
                n = int(np.prod(shape))
                nw = (n + 1) // 2
                o = self.p
                self.p += nw
                assert self.p <= self.hi, ("arena overflow", self.p, self.hi)
                return self._shape(ARB[:, 2 * o:2 * o + n], shape)

            def i32(self, shape, u=False):
                n = int(np.prod(shape))
                o = self.p
                self.p += n
                assert self.p <= self.hi
                return self._shape((ARU if u else ARI)[:, o:o + n], shape)

        ident = cst[:, 0:128]
        tri = cst[:, 128:256]
        rev = cst[:, 256:384]
        ci = [cst[:, 384:512], cst[:, 512:640]]
        negones = cst[:, 640:768]
        maskL = cst[:, 768:896]
        maskU = cst[:, 896:1024]
        iota16 = cst[:, 1024:1040]

        rot = {"i": 0, "banks": [0, 1, 2, 3, 4]}

        def nb():
            b = rot["banks"][rot["i"] % len(rot["banks"])]
            rot["i"] += 1
            return b

        def psn(b):
            return "ps%d" % b

        def dve(fn, r, w):
            S.add("dve", fn, r, w)

        def act(fn, r, w):
            S.add("act", fn, r, w)

        def pe(fn, r, w):
            S.add("pe", fn, r, w)

        def mm(out, lhsT, rhs, start, stop, r, w):
            pe(lambda e: e.matmul(out, lhsT=lhsT, rhs=rhs, start=start, stop=stop), r, w)

        def tr(out, in_, r, w):
            pe(lambda e: e.transpose(out=out, in_=in_, identity=identb[:, :]), list(r) + ["identb"], w)

        def tr2(out, in_, r, w):
            mm(out, in_, identb[:, :], True, True, list(r) + ["identb"], w)

        def tt(out, in0, in1, op, r, w):
            dve(lambda e: e.tensor_tensor(out=out, in0=in0, in1=in1, op=op), r, w)

        def ts(out, in0, s1, op0, r, w, s2=None, op1=None):
            if op1 is None:
                dve(lambda e: e.tensor_scalar(out=out, in0=in0, scalar1=s1, scalar2=None, op0=op0), r, w)
            else:
                dve(lambda e: e.tensor_scalar(out=out, in0=in0, scalar1=s1, scalar2=s2, op0=op0, op1=op1), r, w)

        def stt(out, in0, scalar, in1, op0, op1, r, w, accum=None):
            if accum is None:
                dve(lambda e: e.scalar_tensor_tensor(out=out, in0=in0, scalar=scalar, in1=in1, op0=op0, op1=op1), r, w)
            else:
                dve(lambda e: e.scalar_tensor_tensor(out=out, in0=in0, scalar=scalar, in1=in1, op0=op0, op1=op1,
                                                     accum_out=accum), r, w)

        def cp(out, in_, r, w, eng="dve"):
            if eng == "dve":
                dve(lambda e: e.tensor_copy(out=out, in_=in_), r, w)
            else:
                act(lambda e: e.copy(out=out, in_=in_), r, w)

        def actf(out, in_, func, r, w, bias=None, scale=None, accum=None):
            kw = {}
            if bias is not None:
                kw["bias"] = bias
            if scale is not None:
                kw["scale"] = scale
            if accum is not None:
                kw["accum_out"] = accum
            act(lambda e: e.activation(out=out, in_=in_, func=func, **kw), r, w)

        def memset(ap, v, w):
            dve(lambda e: e.memset(ap, v), [], w)

        def dma(eng, q, out, in_, r, w, chain=False):
            S.add(eng, lambda e: e.dma_start(out=out, in_=in_), r, w, dq=q, chain=chain)

        def load_w(q, dst3, src3, name, ncols):
            step = 1024
            for kc in range(8):
                for c0 in range(0, ncols, step):
                    c1 = min(ncols, c0 + step)
                    dma("pool", q, dst3[:, kc, c0:c1], src3[:, kc, c0:c1], [], [name])

        dma("sp", "ld_c", cst[:, :], cst_d[:, :], [], ["cst"])
        dma("sp", "ld_pv", pvec[:, :], pvec_d[:, :], [], ["pvec"])
        dma("sp", "ld_rv", rvec[:, :], rvec_d[:, :], [], ["rvec"])
        cp(identb[:, :], ident, ["cst"], ["identb"])
        negA = small[:, 0:4]
        ibp = small[:, 4:8]
        actf(negA, rvec[:, 0:4], AF.Exp, ["rvec"], ["negA"])
        ts(negA, negA, -1.0, ALU.mult, ["negA"], ["negA"])
        ts(ibp, rvec[:, 8:12], float(np.log(128.0 ** -0.5)), ALU.add, ["rvec"], ["ibp"])

        def cast_tables(dep=(), lo=0, hi=64):
            if "peer" in phases:
                for i in range(lo, hi):
                    dma("pool", "cvt", uvb_d[i * 256:(i + 1) * 256, :], puv_d[i * 256:(i + 1) * 256, :], list(dep), ["uvb"])

        def rms_T(xap, xname, pvcol, xn, hT, tagp, nbf=None):
            ss = small[:, 16:17]
            rstd = small[:, 17:18]
            junk = xn
            actf(junk, xap, AF.Square, [xname], ["xn", "ss"], scale=1.0 / 32.0, accum=ss)
            actf(ss, ss, AF.Sqrt, ["ss"], ["ss"], bias=EPS, scale=1.0)
            dve(lambda e: e.reciprocal(out=rstd, in_=ss), ["ss"], ["rstd"])
            ts(xn, xap, rstd, ALU.mult, [xname, "rstd"], ["xn"])
            b = (nbf or nb)()
            for kc in range(8):
                tr(PSB[b][:, kc * 128:(kc + 1) * 128], xn[:, kc * 128:(kc + 1) * 128], ["xn"], [psn(b)])
            tt(hT, PSB[b][:, :].rearrange("p (a b) -> p a b", a=8, b=128),
               bc(pvec[:, pvcol:pvcol + 8].unsqueeze(2), [128, 8, 128]), ALU.mult, [psn(b), "pvec"], ["hT"])

        MIXT_OFF = 24512
        mixTs = ARB[:, 2 * MIXT_OFF:2 * MIXT_OFF + 16 * 1024].rearrange("p (t a b) -> p t a b", t=16, a=8, b=128)
        if "mix" in phases:
            WIN_OFF = MIXT_OFF + 8192
            w_in = ARB[:, 2 * WIN_OFF:2 * WIN_OFF + 8 * 4112].rearrange("p (a b) -> p a b", a=8, b=4112)
            for (q_, nm_, rngs_) in (("ld_win", "w_in1", ((512, 1536), (2048, 2560), (3072, 3584), (4096, 4112))),
                                     ("ld_win2", "w_in2", ((0, 512), (1536, 2048), (2560, 3072), (3584, 4096)))):
                for kc in range(8):
                    for (c0, c1) in rngs_:
                        dma("pool", q_, w_in[:, kc, c0:c1], w_in_d[:, kc, c0:c1], [], [nm_])
            B0 = Bump(0, MIXT_OFF)
            B1 = B0
            convw = B1.f32([80])
            dma("sp", "ld_cw", convw, convw_d[:, :], [], ["convw"])
            diag = B1.bf([80, 128])
            for i in range(80):
                ts(diag[:, i, :], ident, convw[:, i:i + 1], ALU.mult, ["cst", "convw"], ["diag"])
            xin = [B0.f32([D])]
            xn = B0.bf([D])
            hT = B0.bf([8, 128])
            pre = B0.bf([20, 131])
            cvA = B0.f32([8, 128])
            cvV = B0.f32([4, 128])
            sq = B0.f32([8, 128])
            R1 = sq[:, 0:4, :]
            R2 = sq[:, 4:8, :]
            tmpM = B0.f32([4, 128])
            E1 = B0.f32([4, 128])
            wT = B0.bf([4, 128])
            qkT = B0.bf([4, 128])
            E2g = B0.f32([4, 128])
            E2m = E2g
            qkn = B0.bf([8, 128])
            kbe = B0.bf([4, 128])
            vb = B0.bf([4, 128])
            kdm = B0.bf([2, 4, 128])
            qd = B0.bf([4, 128])
            kqT = B0.bf([8, 128])
            knT = kqT[:, 0:4, :]
            qnT = kqT[:, 4:8, :]
            qdm = B0.bf([2, 4, 128])
            Pb = [B0.bf([4, 128]), B0.bf([4, 128])]
            PTb = [B0.bf([4, 128]), B0.bf([4, 128])]
            R32 = B0.f32([4, 128])
            Rbf = B0.bf([4, 128])
            u32 = B0.f32([4, 128])
            vnew = B0.bf([4, 128])
            vnew2 = [B0.bf([4, 128]), B0.bf([4, 128])]
            S32 = B0.f32([4, 128])
            Sbf2 = [B0.bf([4, 128]), B0.bf([4, 128])]
            mqk_bf = qkn
            mqF = qd
            kwm = B0.bf([2, 4, 128])
            vaug = B0.bf([4, 130])
            mqkT = kqT
            qFm = B0.bf([2, 4, 128])
            PTm = B0.bf([4, 128])
            Cm32 = B0.f32([4, 129])
            Cmbf2 = [B0.bf([4, 130]), B0.bf([4, 130])]
            hm = R32
            o32 = u32
            zs = B1.f32([4, 128])
            so = B1.f32([4, 128])
            mixed = B1.bf([8, 128])
            g16 = B1.f32([16])
            T12 = B1.f32([12])
            E12 = B1.f32([12])
            L8 = B1.f32([8])
            LA = B1.f32([8])
            beta = B1.f32([4])
            nbeta = B1.f32([4])
            bEgc = B1.f32([4])
            lip = B1.f32([4])
            gcs = B1.f32([32])
            EGC = B1.f32([8])
            EREV = B1.f32([8])
            EGL2 = [B1.f32([16]), B1.f32([16])]
            ssq = B1.f32([8])
            rinv = B1.f32([8])
            den = B1.f32([4])
            ms4 = B1.f32([4])

            for (ap, nm) in ((pre, "pre"), (kdm, "kdm"), (qdm, "qdm"), (kwm, "kwm"), (qFm, "qFm"), (vnew, "vnew"), (vnew2[0], "vnew0"), (vnew2[1], "vnew1"),
                             (S32, "S32"), (Sbf2[0], "Sbf0"), (Sbf2[1], "Sbf1"), (Cm32, "Cm32"), (Cmbf2[0], "Cmbf0"), (Cmbf2[1], "Cmbf1")):
                memset(ap, 0.0, [nm])
            memset(vaug, 1.0, ["vaug"])

            OB, HX, HY = 5, 6, 7

            def hview(b):
                return PS[b][:, :].rearrange("p (a b) -> p a b", a=4, b=128)

            def hview2(b):
                return PS[b][:, :].rearrange("p (a b) -> p a b", a=2, b=256)

            roth = {"i": 0}

            def nbh():
                b = roth["i"] % 2
                roth["i"] += 1
                return b

            rot["banks"] = [2, 3, 4]

            def Mmat(lo, dest_ps):
                cp(R1, bc(LA[:, lo:lo + 4].unsqueeze(2), [128, 4, 128]), ["LA"], ["R1"])
                tt(R2, bc(tri.unsqueeze(1), [128, 4, 128]), bc(LA[:, lo:lo + 4].unsqueeze(2), [128, 4, 128]),
                   ALU.mult, ["cst", "LA"], ["R2"])
                mm(PS[dest_ps][:, :], tri, R1[:, :, :].rearrange("p a b -> p (a b)"), True, False, ["cst", "R1"], [psn(dest_ps)])
                mm(PS[dest_ps][:, :], negones, R2[:, :, :].rearrange("p a b -> p (a b)"), False, True, ["cst", "R2"], [psn(dest_ps)])

            def head(t):
                xt = xin[0]
                xnm = "xin0"
                dma("sp", "ld_x0", xt, xs[t * 128:(t + 1) * 128, :], [], [xnm])
                rms_T(xt, xnm, 0, xn, hT, "m", nbf=nbh)
                if t > 0:
                    cp(pre[:, :, 0:3], pre[:, :, 128:131], ["pre"], ["pre"])
                groups = [0, 1, 2, 3, 4] if t >= NPREV - 1 else [1, 2, 4]
                for g in groups:
                    b = nbh()
                    for h in range(4):
                        blk = g * 4 + h
                        for kc in range(8):
                            mm(PS[b][:, h * 128:(h + 1) * 128], w_in[:, kc, blk * 128:(blk + 1) * 128], hT[:, kc, :],
                               kc == 0, kc == 7, ["w_in1" if g in (1, 2, 4) else "w_in2", "hT"], [psn(b)])
                    cp(pre[:, g * 4:(g + 1) * 4, 3:131], hview(b), [psn(b)], ["pre"], eng="act")
                full = t >= NPREV
                bg = nbh()
                for kc in range(8):
                    mm(PS[bg][:, 0:16], hT[:, kc, :], w_in[:, kc, 4096:4112], kc == 0, kc == 7, ["w_in1", "hT"], [psn(bg)])
                cp(g16, PS[bg][:, 0:16], [psn(bg)], ["g16"])
                tt(T12[:, 0:4], g16[:, 0:4], rvec[:, 4:8], ALU.add, ["g16", "rvec"], ["T12"])
                stt(T12[:, 4:8], g16[:, 12:16], 1.0, rvec[:, 12:16], ALU.mult, ALU.add, ["g16", "rvec"], ["T12"])
                ts(T12[:, 4:8], T12[:, 4:8], -1.0, ALU.mult, ["T12"], ["T12"])
                ts(T12[:, 8:12], g16[:, 4:8], -1.0, ALU.mult, ["g16"], ["T12"])
                actf(E12, T12, AF.Exp, ["T12"], ["E12"])
                actf(L8, E12[:, 0:8], AF.Ln, ["E12"], ["L8"], bias=1.0)
                tt(LA[:, 0:4], L8[:, 0:4], negA, ALU.mult, ["L8", "negA"], ["LA"])
                ts(LA[:, 4:8], L8[:, 4:8], -1.0, ALU.mult, ["L8"], ["LA"])
                ts(beta, E12[:, 8:12], 1.0, ALU.add, ["E12"], ["beta"])
                dve(lambda e: e.reciprocal(out=beta, in_=beta), ["beta"], ["beta"])
                ts(nbeta, beta, -1.0, ALU.mult, ["beta"], ["nbeta"])
                tt(lip, g16[:, 8:12], ibp, ALU.add, ["g16", "ibp"], ["lip"])
                bq = nbh()
                mm(PS[bq][:, 0:8], tri, LA, True, True, ["cst", "LA"], [psn(bq)])
                mm(PS[bq][:, 8:16], rev, LA, True, True, ["cst", "LA"], [psn(bq)])
                mm(PS[bq][:, 16:24], ci[0], LA, True, True, ["cst", "LA"], [psn(bq)])
                mm(PS[bq][:, 24:32], ci[1], LA, True, True, ["cst", "LA"], [psn(bq)])
                cp(gcs, PS[bq][:, 0:32], [psn(bq)], ["gcs"])
                tt(gcs[:, 12:16], gcs[:, 12:16], lip, ALU.add, ["gcs", "lip"], ["gcs"])
                actf(EGC, gcs[:, 0:8], AF.Exp, ["gcs"], ["EGC"])
                actf(EREV, gcs[:, 8:16], AF.Exp, ["gcs"], ["EREV"])
                actf(EGL2[t % 2], gcs[:, 16:32], AF.Exp, ["gcs"], ["EGL%d" % (t % 2)])
                tt(bEgc, beta, EGC[:, 0:4], ALU.mult, ["beta", "EGC"], ["bEgc"])

                bM = nbh()
                Mmat(0, bM)
                tt(tmpM, hview(bM), bc(maskL.unsqueeze(1), [128, 4, 128]), ALU.add, [psn(bM), "cst"], ["tmpM"])
                actf(E1, tmpM, AF.Exp, ["tmpM"], ["E1"])
                if full:
                    stt(tmpM, hview(bM), -1.0, bc(maskU.unsqueeze(1), [128, 4, 128]), ALU.mult, ALU.add, [psn(bM), "cst"], ["tmpM"])
                    actf(E2g, tmpM, AF.Exp, ["tmpM"], ["E2g"])


            def tail(t, full):
                for c in range(2):
                    Sbf, Sn = Sbf2[c], "Sbf%d" % c
                    Sbo, Son = Sbf2[1 - c], "Sbf%d" % (1 - c)
                    Cbo, Con = Cmbf2[1 - c], "Cmbf%d" % (1 - c)
                    bws = nb()
                    for h in range(4):
                        mm(PS[bws][:, h * 128:(h + 1) * 128], wT[:, h, :], Sbf[:, h, :], True, True, ["E1", Sn], [psn(bws)])
                    if full and c == 0:
                        for h in range(4):
                            mm(PS[OB][:, h * 128:(h + 1) * 128], qdm[:, 0, h, :], Sbf[:, h, :], True, True, ["qdm", Sn], [psn(OB)])
                        for h in range(4):
                            hb = HX if h < 2 else HY
                            mm(hview2(hb)[:, h % 2, 0:129], qFm[:, 0, h, :], Cmbf2[0][:, h, 0:129], True, True, ["qFm", "Cmbf0"], [psn(hb)])
                    vc = vnew2[c]
                    tt(vc, u32, hview(bws), ALU.subtract, ["u32", psn(bws)], ["vnew%d" % c])
                    bds = nb()
                    for h in range(4):
                        mm(PS[bds][:, h * 128:(h + 1) * 128], kdm[:, c, h, :], vc[:, h, :], True, True, ["kdm", "vnew%d" % c], [psn(bds)])
                    tt(S32, S32, bc(EGL2[t % 2][:, c * 8:c * 8 + 4].unsqueeze(2), [128, 4, 128]), ALU.mult, ["S32", "EGL%d" % (t % 2)], ["S32"])
                    tt(S32, S32, hview(bds), ALU.add, ["S32", psn(bds)], ["S32"])
                    cp(Sbo, S32, ["S32"], [Son], eng="act")
                    bc1, bc2 = nb(), nb()
                    for h in range(4):
                        hb = bc1 if h < 2 else bc2
                        mm(hview2(hb)[:, h % 2, 0:129], kwm[:, c, h, :], vaug[:, h, 0:129], True, True, ["kwm", "vaug"], [psn(hb)])
                    tt(Cm32, Cm32, bc(EGL2[t % 2][:, c * 8 + 4:c * 8 + 8].unsqueeze(2), [128, 4, 129]), ALU.mult, ["Cm32", "EGL%d" % (t % 2)], ["Cm32"])
                    tt(Cm32[:, 0:2, :], Cm32[:, 0:2, :], hview2(bc1)[:, :, 0:129], ALU.add, ["Cm32", psn(bc1)], ["Cm32"])
                    tt(Cm32[:, 2:4, :], Cm32[:, 2:4, :], hview2(bc2)[:, :, 0:129], ALU.add, ["Cm32", psn(bc2)], ["Cm32"])
                    cp(Cbo[:, :, 0:129], Cm32, ["Cm32"], [Con], eng="act")
                if not full:
                    return
                ts(vnew, vnew2[0], ci[0][:, 0:1], ALU.mult, ["vnew0", "cst"], ["vnew"])
                stt(vnew, vnew2[1], ci[1][:, 0:1], vnew, ALU.mult, ALU.add, ["vnew1", "cst", "vnew"], ["vnew"])
                bo2 = nb()
                for h in range(4):
                    mm(PS[bo2][:, h * 128:(h + 1) * 128], qdm[:, 1, h, :], Sbf2[1][:, h, :], True, False, ["qdm", "Sbf1"], [psn(bo2)])
                    mm(PS[bo2][:, h * 128:(h + 1) * 128], qkT[:, h, :], vnew[:, h, :], False, True, ["E1", "vnew"], [psn(bo2)])
                bh2 = [nb(), nb()]
                for h in range(4):
                    hb = bh2[h // 2]
                    mm(hview2(hb)[:, h % 2, 0:129], qFm[:, 1, h, :], Cmbf2[1][:, h, 0:129], True, False, ["qFm", "Cmbf1"], [psn(hb)])
                    mm(hview2(hb)[:, h % 2, 0:129], PTm[:, h, :], vaug[:, h, 0:129], False, True, ["PTm", "vaug"], [psn(hb)])
                cp(o32, hview(OB), [psn(OB)], ["u32"], eng="act")
                tt(o32, o32, hview(bo2), ALU.add, ["u32", psn(bo2)], ["u32"])
                tt(cvA[:, 0:4, :], o32, o32, ALU.mult, ["u32"], ["cvA"])
                dve(lambda e: e.tensor_reduce(out=ms4, in_=cvA[:, 0:4, :], axis=AX.X, op=ALU.add), ["cvA"], ["ms4"])
                actf(ms4, ms4, AF.Sqrt, ["ms4"], ["ms4"], bias=EPS, scale=1.0 / 128.0)
                dve(lambda e: e.reciprocal(out=ms4, in_=ms4), ["ms4"], ["ms4"])
                tt(o32, o32, bc(ms4.unsqueeze(2), [128, 4, 128]), ALU.mult, ["u32", "ms4"], ["u32"])
                tt(mixed[:, 0:4, :], o32, zs, ALU.mult, ["u32", "zs"], ["mixed"])
                for hb, hb2, h0 in ((HX, bh2[0], 0), (HY, bh2[1], 2)):
                    cp(hm[:, h0:h0 + 2, :], hview2(hb)[:, :, 0:128], [psn(hb)], ["R32"], eng="act")
                    cp(den[:, h0:h0 + 2], hview2(hb)[:, :, 128], [psn(hb)], ["den"], eng="act")
                    tt(hm[:, h0:h0 + 2, :], hm[:, h0:h0 + 2, :], hview2(hb2)[:, :, 0:128], ALU.add, ["R32", psn(hb2)], ["R32"])
                    tt(den[:, h0:h0 + 2], den[:, h0:h0 + 2], hview2(hb2)[:, :, 128], ALU.add, ["den", psn(hb2)], ["den"])
                stt(den, den, -1.0, den, ALU.mult, ALU.max, ["den"], ["den"])
                ts(den, den, 1.0, ALU.max, ["den"], ["den"])
                dve(lambda e: e.reciprocal(out=den, in_=den), ["den"], ["den"])
                tt(hm, hm, bc(den.unsqueeze(2), [128, 4, 128]), ALU.mult, ["R32", "den"], ["R32"])
                tt(cvA[:, 0:4, :], hm, hm, ALU.mult, ["R32"], ["cvA"])
                dve(lambda e: e.tensor_reduce(out=ms4, in_=cvA[:, 0:4, :], axis=AX.X, op=ALU.add), ["cvA"], ["ms4"])
                actf(ms4, ms4, AF.Sqrt, ["ms4"], ["ms4"], bias=EPS, scale=1.0 / 128.0)
                dve(lambda e: e.reciprocal(out=ms4, in_=ms4), ["ms4"], ["ms4"])
                tt(hm, hm, bc(ms4.unsqueeze(2), [128, 4, 128]), ALU.mult, ["R32", "ms4"], ["R32"])
                tt(mixed[:, 4:8, :], hm, so, ALU.mult, ["R32", "so"], ["mixed"])
                bmx = [nb(), nb()]
                for kc in range(8):
                    tr2(PS[bmx[kc // 4]][:, (kc % 4) * 128:(kc % 4 + 1) * 128], mixed[:, kc, :], ["mixed"], [psn(bmx[kc // 4])])
                to = t - NPREV
                for g2 in range(2):
                    tt(mixTs[:, to, g2 * 4:(g2 + 1) * 4, :], hview(bmx[g2]),
                       bc(pvec[:, 8 + g2 * 4:12 + g2 * 4].unsqueeze(2), [128, 4, 128]), ALU.mult, [psn(bmx[g2]), "pvec"], ["mixT%d" % to])

            def merge(a, b):
                out, i, j = [], 0, 0
                while i < len(a) or j < len(b):
                    if j >= len(b) or (i < len(a) and i * len(b) <= j * len(a)):
                        out.append(a[i]); i += 1
                    else:
                        out.append(b[j]); j += 1
                return out

            head(0)
            for t in range(NT):
                full = t >= NPREV
                bz = bmo = None
                if full:
                    bz = nb()
                    for kc in range(8):
                        mm(PS[bz][:, :], hT[:, kc, :], w_in[:, kc, 2560:3072], kc == 0, kc == 7, ["w_in2", "hT"], [psn(bz)])
                    actf(zs, hview(bz), AF.Silu, [psn(bz)], ["zs"])
                    bmo = nb()
                    for kc in range(8):
                        mm(PS[bmo][:, :], hT[:, kc, :], w_in[:, kc, 3584:4096], kc == 0, kc == 7, ["w_in2", "hT"], [psn(bmo)])
                    actf(so, hview(bmo), AF.Sigmoid, [psn(bmo)], ["so"])
                bmv = nb()
                for kc in range(8):
                    mm(PS[bmv][:, :], hT[:, kc, :], w_in[:, kc, 3072:3584], kc == 0, kc == 7, ["w_in1", "hT"], [psn(bmv)])
                cp(vaug[:, :, 0:128], hview(bmv), [psn(bmv)], ["vaug"], eng="act")
                def conv_group(g, dst, dname):
                    b = nb()
                    for h in range(4):
                        blk = g * 4 + h
                        for tap in range(4):
                            mm(PS[b][:, h * 128:(h + 1) * 128], pre[:, blk, tap:tap + 128], diag[:, blk * 4 + tap, :],
                               tap == 0, tap == 3, ["pre", "diag"], [psn(b)])
                    actf(dst, hview(b), AF.Silu, [psn(b)], [dname])

                if full:
                    conv_group(0, cvA[:, 0:4, :], "cvA")
                conv_group(1, cvA[:, 4:8, :], "cvA")
                conv_group(2, cvV, "cvV")
                lo = 0 if full else 4
                tt(sq[:, lo:8, :], cvA[:, lo:8, :], cvA[:, lo:8, :], ALU.mult, ["cvA"], ["R1", "R2"])
                dve(lambda e, lo=lo: e.tensor_reduce(out=ssq[:, lo:8], in_=sq[:, lo:8, :], axis=AX.X, op=ALU.add), ["R1", "R2"], ["ssq"])
                actf(ssq[:, lo:8], ssq[:, lo:8], AF.Sqrt, ["ssq"], ["ssq"], bias=EPS, scale=1.0)
                dve(lambda e, lo=lo: e.reciprocal(out=rinv[:, lo:8], in_=ssq[:, lo:8]), ["ssq"], ["rinv"])
                if full:
                    ts(rinv[:, 0:4], rinv[:, 0:4], float(128.0 ** -0.5), ALU.mult, ["rinv"], ["rinv"])
                tt(qkn[:, lo:8, :], cvA[:, lo:8, :], bc(rinv[:, lo:8].unsqueeze(2), [128, 8 - lo, 128]), ALU.mult,
                   ["cvA", "rinv"], ["qkn"])
                kn = qkn[:, 4:8, :]
                tt(kbe, kn, bc(bEgc.unsqueeze(2), [128, 4, 128]), ALU.mult, ["qkn", "bEgc"], ["kbe"])
                tt(vb, cvV, bc(beta.unsqueeze(2), [128, 4, 128]), ALU.mult, ["cvV", "beta"], ["vb"])
                for c in range(2):
                    stt(kdm[:, c, :, :], kn, ci[c][:, 0:1], bc(EREV[:, 0:4].unsqueeze(2), [128, 4, 128]), ALU.mult, ALU.mult,
                        ["qkn", "EREV", "cst"], ["kdm"])
                bT = nb()
                bTq = nb()
                for h in range(4):
                    tr2(PS[bT][:, h * 128:(h + 1) * 128], kn[:, h, :], ["qkn"], [psn(bT)])
                if full:
                    tt(qd, qkn[:, 0:4, :], bc(EGC[:, 0:4].unsqueeze(2), [128, 4, 128]), ALU.mult, ["qkn", "EGC"], ["qd"])
                if full:
                    for h in range(4):
                        tr2(PS[bTq][:, h * 128:(h + 1) * 128], qkn[:, h, :], ["qkn"], [psn(bTq)])
                cp(knT, hview(bT), [psn(bT)], ["knT"], eng="act")
                if full:
                    cp(qnT, hview(bTq), [psn(bTq)], ["qnT"], eng="act")
                    bT2 = nb()
                    for h in range(4):
                        tr2(PS[bT2][:, h * 128:(h + 1) * 128], qd[:, h, :], ["qd"], [psn(bT2)])
                    for c in range(2):
                        cp(qdm[:, c, :, c * 64:(c + 1) * 64], hview(bT2)[:, :, c * 64:(c + 1) * 64],
                           [psn(bT2)], ["qdm"], eng="act")
                bG = nb()
                for h in range(4):
                    mm(PS[bG][:, h * 128:(h + 1) * 128], knT[:, h, :], knT[:, h, :], True, True, ["knT"], [psn(bG)])
                tt(tmpM, hview(bG), E1, ALU.mult, [psn(bG), "E1"], ["tmpM"])
                tt(Pb[0], tmpM, bc(nbeta.unsqueeze(2), [128, 4, 128]), ALU.mult, ["tmpM", "nbeta"], ["P0"])
                bP = nb()
                for h in range(4):
                    tr2(PS[bP][:, h * 128:(h + 1) * 128], Pb[0][:, h, :], ["P0"], [psn(bP)])
                pT_ps = hview(bP)
                tt(R32, pT_ps, bc(ident.unsqueeze(1), [128, 4, 128]), ALU.add, [psn(bP), "cst"], ["R32"])
                cp(PTb[0], pT_ps, [psn(bP)], ["PT0"])
                cp(Rbf, R32, ["R32"], ["Rbf"], eng="act")
                for k in range(1, 6):
                    pc, pp = k % 2, (k - 1) % 2
                    b1 = nb()
                    for h in range(4):
                        mm(PS[b1][:, h * 128:(h + 1) * 128], PTb[pp][:, h, :], Pb[pp][:, h, :], True, True,
                           ["P%d" % pp, "PT%d" % pp], [psn(b1)])
                    cp(Pb[pc], hview(b1), [psn(b1)], ["P%d" % pc], eng="act")
                    if k < 5:
                        b2 = nb()
                        for h in range(4):
                            mm(PS[b2][:, h * 128:(h + 1) * 128], Pb[pp][:, h, :], PTb[pp][:, h, :], True, True,
                               ["P%d" % pp, "PT%d" % pp], [psn(b2)])
                        cp(PTb[pc], hview(b2), [psn(b2)], ["PT%d" % pc])
                    b3 = nb()
                    for h in range(4):
                        mm(PS[b3][:, h * 128:(h + 1) * 128], Pb[pc][:, h, :], Rbf[:, h, :], True, True,
                           ["P%d" % pc, "Rbf"], [psn(b3)])
                    tt(R32, R32, hview(b3), ALU.add, ["R32", psn(b3)], ["R32"])
                    cp(Rbf, R32, ["R32"], ["Rbf"], eng="act")
                bu = nb()
                for h in range(4):
                    mm(PS[bu][:, h * 128:(h + 1) * 128], Rbf[:, h, :], vb[:, h, :], True, True, ["Rbf", "vb"], [psn(bu)])
                cp(u32, hview(bu), [psn(bu)], ["u32"], eng="act")
                bw = nb()
                for h in range(4):
                    mm(PS[bw][:, h * 128:(h + 1) * 128], kbe[:, h, :], Rbf[:, h, :], True, True, ["Rbf", "kbe"], [psn(bw)])
                cp(wT, hview(bw), [psn(bw)], ["E1"])
                if full:
                    bqk = nb()
                    for h in range(4):
                        mm(PS[bqk][:, h * 128:(h + 1) * 128], knT[:, h, :], qnT[:, h, :], True, True, ["knT", "qnT"], [psn(bqk)])
                    tt(qkT, hview(bqk), E2g, ALU.mult, [psn(bqk), "E2g"], ["E1"])
                if full:
                    conv_group(3, cvA[:, 0:4, :], "cvA")
                conv_group(4, cvA[:, 4:8, :], "cvA")
                mk = cvA[:, 4:8, :]
                for c in range(2):
                    stt(kwm[:, c, :, :], mk, ci[c][:, 0:1], bc(EREV[:, 4:8].unsqueeze(2), [128, 4, 128]), ALU.mult, ALU.mult,
                        ["cvA", "EREV", "cst"], ["kwm"])
                if full:
                    cp(mqk_bf, cvA, ["cvA"], ["qkn"])
                    tt(mqF, cvA[:, 0:4, :], bc(EGC[:, 4:8].unsqueeze(2), [128, 4, 128]), ALU.mult, ["cvA", "EGC"], ["qd"])
                    bT3 = [nb(), nb()]
                    for j in range(8):
                        tr2(PS[bT3[j // 4]][:, (j % 4) * 128:(j % 4 + 1) * 128], mqk_bf[:, j, :], ["qkn"], [psn(bT3[j // 4])])
                    for g2 in range(2):
                        cp(mqkT[:, g2 * 4:(g2 + 1) * 4, :], hview(bT3[g2]), [psn(bT3[g2])], ["knT", "qnT"], eng="act")
                    bT4 = nb()
                    for h in range(4):
                        tr2(PS[bT4][:, h * 128:(h + 1) * 128], mqF[:, h, :], ["qd"], [psn(bT4)])
                    for c in range(2):
                        cp(qFm[:, c, :, c * 64:(c + 1) * 64], hview(bT4)[:, :, c * 64:(c + 1) * 64],
                           [psn(bT4)], ["qFm"], eng="act")
                    bM2 = nb()
                    Mmat(4, bM2)
                    stt(tmpM, hview(bM2), -1.0, bc(maskU.unsqueeze(1), [128, 4, 128]), ALU.mult, ALU.add, [psn(bM2), "cst"], ["tmpM"])
                    tt(tmpM, tmpM, bc(lip.unsqueeze(2), [128, 4, 128]), ALU.add, ["tmpM", "lip"], ["tmpM"])
                    actf(E2m, tmpM, AF.Exp, ["tmpM"], ["E2g"])
                    bs = nb()
                    for h in range(4):
                        mm(PS[bs][:, h * 128:(h + 1) * 128], mqkT[:, 4 + h, :], mqkT[:, h, :], True, True, ["knT", "qnT"], [psn(bs)])
                    tt(PTm, hview(bs), E2m, ALU.mult, [psn(bs), "E2g"], ["PTm"])
                tl, hd = [], []
                S.cap = tl
                tail(t, full)
                S.cap = None
                if t + 1 < NT:
                    S.cap = hd
                    head(t + 1)
                    S.cap = None
                S.replay(merge(tl, hd))
                cpt = (64 + NT - 1) // NT
                cast_tables(["S32"], min(64, t * cpt), min(64, (t + 1) * cpt))
            S.barrier()

        x_own = AR[:, 0:NOWN * D].rearrange("p (t d) -> p t d", t=NOWN, d=D)
        rot["banks"] = [0, 1, 2, 3, 4, 5, 6, 7]
        if "mixout" in phases:
            WOUT_OFF = 16384
            w_out = ARB[:, 2 * WOUT_OFF:2 * WOUT_OFF + 8192].rearrange("p (a b) -> p a b", a=8, b=1024)
            load_w("ld_wout", w_out, w_out_d, "w_out", 1024)
            for t in range(NOWN):
                dma("sp", "ld_xo", x_own[:, t, :], xs[(NPREV + t) * 128:(NPREV + t + 1) * 128, :], [], ["xo%d" % t], chain=True)
                if "mix" not in phases:
                    continue
                for n in range(2):
                    b = nb()
                    for kc in range(8):
                        mm(PS[b][:, :], mixTs[:, t, kc, :], w_out[:, kc, n * 512:(n + 1) * 512], kc == 0, kc == 7,
                           ["mixT%d" % t, "w_out"], [psn(b)])
                    tt(x_own[:, t, n * 512:(n + 1) * 512], x_own[:, t, n * 512:(n + 1) * 512], PS[b][:, :], ALU.add,
                       ["xo%d" % t, psn(b)], ["xo%d" % t])
            S.barrier()

        XEND = NOWN * D
        PWQ_OFF = ARW - 8192
        pwq = ARB[:, 2 * PWQ_OFF:2 * PWQ_OFF + 16384].rearrange("p (a b) -> p a b", a=8, b=2048)
        if "mix" not in phases:
            cast_tables()

        if "xa" in phases:
            Bx = Bump(16384, PWQ_OFF)
            wq = Bx.bf([8, 1024])
            wkv = Bx.bf([8, 2048])
            wo = Bx.bf([8, 1024])
            load_w("ld_wq", wq, wq_d, "wq", 1024)
            load_w("ld_wkv", wkv, wkv_d, "wkv", 2048)
            load_w("ld_wo", wo, wo_d, "wo", 1024)
            if "peer" in phases:
                load_w("ld_pwq", pwq, pwq_d, "pwq", 2048)
            xn = Bx.bf([D])
            hT = Bx.bf([8, 128])
            mtile = Bx.f32([D])
            mT = Bx.bf([8, 256])
            kT = Bx.bf([8, 256])
            vbf = Bx.bf([2, 1024])
            qT2 = [Bx.bf([8, 128]), Bx.bf([8, 128])]
            pexp = mtile.rearrange("p (a b) -> p a b", a=4, b=256)
            pn = Bx.bf([4, 256])
            pT = Bx.bf([8, 128])
            oT = Bx.bf([8, 128])
            mx4 = Bx.f32([4])
            sm4 = Bx.f32([4])
            for mt in range(2):
                dma("sp", "ld_mem", mtile, memd[mt * 128:(mt + 1) * 128, :], [], ["mtile"], chain=True)
                rms_T(mtile, "mtile", 24, xn, hT, "mem")
                cp(mT[:, :, mt * 128:(mt + 1) * 128], hT, ["hT"], ["mT"])
            for blk in range(8):
                b = nb()
                for kc in range(8):
                    mm(PS[b][:, 0:256], wkv[:, kc, blk * 128:(blk + 1) * 128], mT[:, kc, :], kc == 0, kc == 7, ["wkv", "mT"], [psn(b)])
                cp(kT[:, blk, :], PS[b][:, 0:256], [psn(b)], ["kT"], eng="act")
            for mt in range(2):
                for n in range(2):
                    b = nb()
                    for kc in range(8):
                        mm(PS[b][:, :], mT[:, kc, mt * 128:(mt + 1) * 128], wkv[:, kc, 1024 + n * 512:1024 + (n + 1) * 512],
                           kc == 0, kc == 7, ["wkv", "mT"], [psn(b)])
                    cp(vbf[:, mt, n * 512:(n + 1) * 512], PS[b][:, :], [psn(b)], ["vbf"], eng="act")
            rotx = {"i": 0}

            def nbx():
                b = rotx["i"] % 2
                rotx["i"] += 1
                return b

            rot["banks"] = [2, 3, 4, 5, 6, 7]

            def xhead(t):
                xnm = "xo%d" % t
                rms_T(x_own[:, t, :], xnm, 16, xn, hT, "xa", nbf=nbx)
                for g in range(2):
                    b = nbx()
                    for j in range(4):
                        blk = g * 4 + j
                        for kc in range(8):
                            mm(PS[b][:, j * 128:(j + 1) * 128], wq[:, kc, blk * 128:(blk + 1) * 128], hT[:, kc, :], kc == 0, kc == 7,
                               ["wq", "hT"], [psn(b)])
                    cp(qT2[t % 2][:, g * 4:(g + 1) * 4, :], PS[b][:, :].rearrange("p (a b) -> p a b", a=4, b=128), [psn(b)], ["qT%d" % (t % 2)], eng="act")

            def xtail(t):
                xnm = "xo%d" % t
                sb_ = [nb(), nb()]
                for h in range(4):
                    b = sb_[h // 2]
                    for dc in range(2):
                        mm(PS[b][:, (h % 2) * 256:(h % 2 + 1) * 256], qT2[t % 2][:, h * 2 + dc, :], kT[:, h * 2 + dc, :], dc == 0, dc == 1,
                           ["qT%d" % (t % 2), "kT"], [psn(b)])
                for g in range(2):
                    b = sb_[g]
                    dve(lambda e, b=b, g=g: e.tensor_reduce(out=mx4[:, g * 2:g * 2 + 2],
                                                             in_=PS[b][:, :].rearrange("p (a b) -> p a b", a=2, b=256),
                                                             axis=AX.X, op=ALU.max), [psn(b)], ["mx4"])
                ts(mx4, mx4, -1.0 / 16.0, ALU.mult, ["mx4"], ["mx4"])
                for h in range(4):
                    b = sb_[h // 2]
                    actf(pexp[:, h, :], PS[b][:, (h % 2) * 256:(h % 2 + 1) * 256], AF.Exp, [psn(b), "mx4"], ["pexp", "sm4"],
                         bias=mx4[:, h:h + 1], scale=1.0 / 16.0, accum=sm4[:, h:h + 1])
                dve(lambda e: e.reciprocal(out=sm4, in_=sm4), ["sm4"], ["sm4"])
                tt(pn, pexp, bc(sm4.unsqueeze(2), [128, 4, 256]), ALU.mult, ["pexp", "sm4"], ["pn"])
                b = nb()
                for h in range(4):
                    for mt in range(2):
                        tr(PSB[b][:, (h * 2 + mt) * 128:(h * 2 + mt + 1) * 128], pn[:, h, mt * 128:(mt + 1) * 128], ["pn"], [psn(b)])
                cp(pT, PSB[b][:, :].rearrange("p (a b) -> p a b", a=8, b=128), [psn(b)], ["pT"], eng="act")
                for g in range(2):
                    b = nb()
                    for j in range(4):
                        blk = g * 4 + j
                        h, dc = blk // 2, blk % 2
                        for mt in range(2):
                            mm(PS[b][:, j * 128:(j + 1) * 128], vbf[:, mt, h * 256 + dc * 128:h * 256 + (dc + 1) * 128],
                               pT[:, h * 2 + mt, :], mt == 0, mt == 1, ["vbf", "pT"], [psn(b)])
                    cp(oT[:, g * 4:(g + 1) * 4, :], PS[b][:, :].rearrange("p (a b) -> p a b", a=4, b=128), [psn(b)], ["oT"], eng="act")
                for n in range(2):
                    b = nb()
                    for kc in range(8):
                        mm(PS[b][:, :], oT[:, kc, :], wo[:, kc, n * 512:(n + 1) * 512], kc == 0, kc == 7, ["oT", "wo"], [psn(b)])
                    tt(x_own[:, t, n * 512:(n + 1) * 512], x_own[:, t, n * 512:(n + 1) * 512], PS[b][:, :], ALU.add,
                       [xnm, psn(b)], [xnm])

            def xmerge(a, b):
                out, i, j = [], 0, 0
                while i < len(a) or j < len(b):
                    if j >= len(b) or (i < len(a) and i * len(b) <= j * len(a)):
                        out.append(a[i]); i += 1
                    else:
                        out.append(b[j]); j += 1
                return out

            xhead(0)
            for t in range(NOWN):
                tl, hd = [], []
                S.cap = tl
                xtail(t)
                S.cap = None
                if t + 1 < NOWN:
                    S.cap = hd
                    xhead(t + 1)
                    S.cap = None
                S.replay(xmerge(tl, hd))
            S.barrier()

        if "peer" in phases:
            Bp = Bump(16384, PWQ_OFF)
            subk = Bp.bf([16, 128])
            if "xa" not in phases:
                load_w("ld_pwq", pwq, pwq_d, "pwq", 2048)
            dma("pool", "ld_sk", subk[:, 0:8, :], subk_d[:, 0:8, :], [], ["subk"])
            dma("pool", "ld_sk", subk[:, 8:16, :], subk_d[:, 8:16, :], [], ["subk"])
            xn = Bp.bf([D])
            hT = Bp.bf([8, 128])
            htok2 = [Bp.f32([D]), Bp.f32([D])]
            qT = Bp.bf([16, 128])
            sc = Bp.f32([16, 128])
            t2k = Bp.f32([2048])
            sc2 = t2k.rearrange("p (a b) -> p a b", a=16, b=128)
            topv = Bp.f32([16, 16])
            topi = Bp.i32([16, 16], u=True)
            topif = Bp.f32([16, 16])
            cand = sc[:, :, :].rearrange("p a b -> p (a b)").rearrange("p (a b) -> p a b", a=8, b=256)
            cand2 = t2k.rearrange("p (a b) -> p a b", a=8, b=256)
            bestv = Bp.f32([8, 16])
            pos = Bp.i32([8, 16], u=True)
            pa = Bp.i32([8, 16], u=True)
            pb_ = Bp.i32([8, 16], u=True)
            paf = Bp.f32([8, 16])
            pbf = Bp.f32([8, 16])
            oh = t2k.rearrange("p (a b c) -> p a b c", a=8, b=16, c=16)
            isel = Bp.f32([8, 16])
            jsel = Bp.f32([8, 16])
            eidf = Bp.f32([128])
            eid2 = [Bp.i32([128]), Bp.i32([128])]
            gate2 = [Bp.f32([8, 16]), Bp.f32([8, 16])]
            gsum = Bp.f32([8])
            actv = Bp.f32([128])
            wgt = Bp.f32([128])
            junk = Bp.bf([D])
            NG = 8
            dgs = [Bp.bf([128]) for _ in range(4)]
            rot["banks"] = [0, 1, 2, 3, 4, 5]
            gbuf = [Bp.bf([2 * D]) for _ in range(NG)]
            def prologue(t):
                par = t % 2
                eid, gate, htok = eid2[par], gate2[par], htok2[par]
                EIDN, GATEN, HTOKN = "eid%d" % par, "gate%d" % par, "htok%d" % par
                xnm = "xo%d" % t
                rms_T(x_own[:, t, :], xnm, 32, xn, hT, "pf")
                stt(htok, x_own[:, t, :], small[:, 17:18], rvec[:, 16:1040], ALU.mult, ALU.mult, [xnm, "rstd", "rvec"], [HTOKN])
                for g in range(4):
                    b = nb()
                    for j in range(4):
                        blk = g * 4 + j
                        for kc in range(8):
                            mm(PS[b][:, j * 128:(j + 1) * 128], pwq[:, kc, blk * 128:(blk + 1) * 128], hT[:, kc, :], kc == 0, kc == 7,
                               ["pwq", "hT"], [psn(b)])
                    cp(qT[:, g * 4:(g + 1) * 4, :], PS[b][:, :].rearrange("p (a b) -> p a b", a=4, b=128), [psn(b)], ["qT"], eng="act")
                for g in range(4):
                    b = nb()
                    for j in range(4):
                        blk = g * 4 + j
                        mm(PS[b][:, j * 128:(j + 1) * 128], qT[:, blk, :], subk[:, blk, :], True, True, ["qT", "subk"], [psn(b)])
                    cp(sc[:, g * 4:(g + 1) * 4, :], PS[b][:, :].rearrange("p (a b) -> p a b", a=4, b=128), [psn(b)], ["sc"], eng="act")
                for blk in range(16):
                    dve(lambda e, blk=blk: e.max(out=topv[:, blk, 0:8], in_=sc[:, blk, :]), ["sc"], ["topv"])
                    dve(lambda e, blk=blk: e.match_replace(out=sc2[:, blk, :], in_to_replace=topv[:, blk, 0:8],
                                                             in_values=sc[:, blk, :], imm_value=-1e30), ["sc", "topv"], ["t2k"])
                    dve(lambda e, blk=blk: e.max(out=topv[:, blk, 8:16], in_=sc2[:, blk, :]), ["t2k"], ["topv"])
                    dve(lambda e, blk=blk: e.max_index(out=topi[:, blk, 0:8], in_max=topv[:, blk, 0:8], in_values=sc[:, blk, :]),
                        ["sc", "topv"], ["topi"])
                    dve(lambda e, blk=blk: e.max_index(out=topi[:, blk, 8:16], in_max=topv[:, blk, 8:16], in_values=sc[:, blk, :]),
                        ["sc", "topv"], ["topi"])
                cp(topif, topi, ["topi"], ["topif"])
                tv = topv[:, :, :].rearrange("p (h two) k -> p h two k", h=8, two=2)
                tif = topif[:, :, :].rearrange("p (h two) k -> p h two k", h=8, two=2)
                candv = cand[:, :, :].rearrange("p h (a b) -> p h a b", a=16, b=16)
                tt(candv, bc(tv[:, :, 0, :].unsqueeze(3), [128, 8, 16, 16]), bc(tv[:, :, 1, :].unsqueeze(2), [128, 8, 16, 16]),
                   ALU.add, ["topv"], ["sc"])
                for h in range(8):
                    dve(lambda e, h=h: e.max(out=bestv[:, h, 0:8], in_=cand[:, h, :]), ["sc"], ["bestv"])
                    dve(lambda e, h=h: e.match_replace(out=cand2[:, h, :], in_to_replace=bestv[:, h, 0:8],
                                                         in_values=cand[:, h, :], imm_value=-1e30), ["sc", "bestv"], ["t2k"])
                    dve(lambda e, h=h: e.max(out=bestv[:, h, 8:16], in_=cand2[:, h, :]), ["t2k"], ["bestv"])
                    dve(lambda e, h=h: e.max_index(out=pos[:, h, 0:8], in_max=bestv[:, h, 0:8], in_values=cand[:, h, :]),
                        ["sc", "bestv"], ["pos"])
                    dve(lambda e, h=h: e.max_index(out=pos[:, h, 8:16], in_max=bestv[:, h, 8:16], in_values=cand[:, h, :]),
                        ["sc", "bestv"], ["pos"])
                dve(lambda e: e.tensor_single_scalar(out=pa, in_=pos, scalar=4, op=ALU.arith_shift_right), ["pos"], ["pa"])
                dve(lambda e: e.tensor_single_scalar(out=pb_, in_=pos, scalar=15, op=ALU.bitwise_and), ["pos"], ["pb"])
                cp(paf, pa, ["pa"], ["paf"])
                cp(pbf, pb_, ["pb"], ["pbf"])
                io4 = bc(iota16.unsqueeze(1).unsqueeze(1), [128, 8, 16, 16])
                tt(oh, io4, bc(paf.unsqueeze(3), [128, 8, 16, 16]), ALU.is_equal, ["cst", "paf"], ["t2k"])
                tt(oh, oh, bc(tif[:, :, 0, :].unsqueeze(2), [128, 8, 16, 16]), ALU.mult, ["t2k", "topif"], ["t2k"])
                dve(lambda e: e.tensor_reduce(out=isel, in_=oh, axis=AX.X, op=ALU.add), ["t2k"], ["isel"])
                tt(oh, io4, bc(pbf.unsqueeze(3), [128, 8, 16, 16]), ALU.is_equal, ["cst", "pbf"], ["t2k"])
                tt(oh, oh, bc(tif[:, :, 1, :].unsqueeze(2), [128, 8, 16, 16]), ALU.mult, ["t2k", "topif"], ["t2k"])
                dve(lambda e: e.tensor_reduce(out=jsel, in_=oh, axis=AX.X, op=ALU.add), ["t2k"], ["jsel"])
                stt(eidf.rearrange("p (h k) -> p h k", h=8, k=16), isel, 128.0, jsel, ALU.mult, ALU.add, ["isel", "jsel"], ["eidf"])
                cp(eid, eidf, ["eidf"], [EIDN])
                tt(gate, bestv, bc(bestv[:, :, 0:1], [128, 8, 16]), ALU.subtract, ["bestv"], [GATEN])
                actf(gate, gate, AF.Exp, [GATEN], [GATEN])
                dve(lambda e: e.tensor_reduce(out=gsum, in_=gate, axis=AX.X, op=ALU.add), [GATEN], ["gsum"])
                dve(lambda e: e.reciprocal(out=gsum, in_=gsum), ["gsum"], ["gsum"])
                tt(gate, gate, bc(gsum.unsqueeze(2), [128, 8, 16]), ALU.mult, [GATEN, "gsum"], [GATEN])

            def slotloop(t, inj):
                par = t % 2
                xnm = "xo%d" % t
                eid, gate, htok = eid2[par], gate2[par], htok2[par]
                EIDN, GATEN, HTOKN = "eid%d" % par, "gate%d" % par, "htok%d" % par
                gflat = gate[:, :, :].rearrange("p h k -> p (h k)")

                def fin(s_):
                    gb_ = gbuf[s_ % NG]
                    gn_ = "gb%d" % (s_ % NG)
                    dg = dgs[s_ % 4]
                    dn = "dg%d" % (s_ % 4)
                    actf(wgt[:, s_:s_ + 1], wgt[:, s_:s_ + 1], AF.Copy, ["wgt%d" % s_, GATEN], ["wgt%d" % s_], scale=gflat[:, s_:s_ + 1])
                    actf(dg, ident, AF.Copy, ["cst", "wgt%d" % s_], [dn], scale=wgt[:, s_:s_ + 1])
                    for n_ in range(2):
                        mm(PS[6 + n_][:, :], dg, gb_[:, D + n_ * 512:D + (n_ + 1) * 512], s_ == 0, s_ == 127, [dn, gn_], [psn(6 + n_)])

                for s in range(128):
                    gb = gbuf[s % NG]
                    gn = "gb%d" % (s % NG)
                    S.add("pool", lambda e, gb=gb, s=s: e.indirect_dma_start(
                        out=gb, out_offset=None, in_=uvb_d[:, :],
                        in_offset=bass.IndirectOffsetOnAxis(ap=eid[:, s:s + 1], axis=0)), [EIDN, "uvb"], [gn], dq="g%d" % (s % NG))
                    stt(junk, gb[:, 0:D], 1.0, htok, ALU.mult, ALU.mult, [gn, HTOKN], ["junk", "actv%d" % s], accum=actv[:, s:s + 1])
                    actf(wgt[:, s:s + 1], actv[:, s:s + 1], AF.Gelu, ["actv%d" % s], ["wgt%d" % s])
                    if s >= 1:
                        fin(s - 1)
                    if inj and s < 120:
                        S.replay(inj[len(inj) * s // 120:len(inj) * (s + 1) // 120])
                fin(127)
                for n_ in range(2):
                    tt(x_own[:, t, n_ * 512:(n_ + 1) * 512], x_own[:, t, n_ * 512:(n_ + 1) * 512], PS[6 + n_][:, :], ALU.add,
                       [xnm, psn(6 + n_)], [xnm])

            prologue(0)
            for t in range(NOWN):
                inj = []
                if t + 1 < NOWN:
                    S.cap = inj
                    prologue(t + 1)
                    S.cap = None
                slotloop(t, inj)
            S.barrier()

        Bf = Bump(16384, ARW)
        obuf = [Bf.f32([D]), Bf.f32([D])]
        junkb = Bf.bf([D])
        outs = []
        for t in range(NOWN):
            xnm = "xo%d" % t
            ob = obuf[t % 2]
            on = "ob%d" % (t % 2)
            if "final" in phases:
                ss = small[:, 16:17]
                rstd = small[:, 17:18]
                actf(junkb, x_own[:, t, :], AF.Square, [xnm], ["junkb", "ss"], scale=1.0 / 32.0, accum=ss)
                actf(ss, ss, AF.Sqrt, ["ss"], ["ss"], bias=EPS, scale=1.0)
                dve(lambda e: e.reciprocal(out=rstd, in_=ss), ["ss"], ["rstd"])
                stt(ob, x_own[:, t, :], rstd, rvec[:, 1040:2064], ALU.mult, ALU.mult, [xnm, "rstd", "rvec"], [on])
            else:
                cp(ob, x_own[:, t, :], [xnm], [on])
            dma("sp", "st%d" % (t % 2), outd[t * 128:(t + 1) * 128, :], ob, [on], ["out%d" % t])
            outs.append("out%d" % t)
        S.add("sp", None, r=outs)
        S.add("sp", None, extra=[(q, n - 1) for q, n in S.dcount.items() if q.startswith("st")])
        S.emit(nc, es)
    return nc


def _consts():
    p = np.arange(128)[:, None]
    f = np.arange(128)[None, :]
    same = (p // 64) == (f // 64)
    c = np.zeros((128, 1040), np.float32)
    c[:, 0:128] = np.eye(128)
    c[:, 128:256] = ((p <= f) & same)
    c[:, 256:384] = ((p > f) & same)
    c[:, 384:512] = (p // 64 == 0) * np.ones((1, 128))
    c[:, 512:640] = (p // 64 == 1) * np.ones((1, 128))
    c[:, 640:768] = -1.0
    c[:, 768:896] = np.where((p > f) & same, 0.0, NEG)
    c[:, 896:1024] = np.where((f >= p) & same, 0.0, NEG)
    c[:, 1024:1040] = np.arange(16)[None, :]
    return c


def _kc(w):
    return np.ascontiguousarray(w.reshape(8, 128, -1).transpose(1, 0, 2))


def _pv(v):
    return v.reshape(8, 128).T


_NC_CACHE = {}


def run(inputs, npre, nown, phases=("mix", "mixout", "xa", "peer", "final")):
    f = lambda k: np.asarray(inputs[k], dtype=np.float32)
    x = f("x")
    B, SEQ, _ = x.shape
    half = SEQ // 2
    assert half == nown * 128 and npre == nown
    perm = np.r_[0:1536, 2056:3080, 1536:2048, 3080:3592, 3592:4104, 2048:2052, 2052:2056, 4104:4108, 4108:4112]
    w_in = _kc(f("w_in")[0][:, perm])
    convw = np.concatenate([f("gdn_conv_w")[0], f("mlstm_conv_w")[0]], axis=1)
    convw = np.ascontiguousarray(convw.reshape(4, 20, 128).transpose(2, 1, 0).reshape(128, 80))
    pvec = np.zeros((128, 40), np.float32)
    pvec[:, 0:8] = _pv(f("norm_mix_w")[0])
    pvec[:, 8:12] = f("gdn_norm_w")[0][:, None]
    pvec[:, 12:16] = f("mlstm_norm_w")[0].reshape(4, 128).T
    pvec[:, 16:24] = _pv(f("norm_xa_w")[0])
    pvec[:, 24:32] = _pv(f("norm_mem_w")[0])
    pvec[:, 32:40] = _pv(f("norm_ffn_w")[0])
    rv = np.concatenate([f("gdn_a_log")[0], f("gdn_dt_bias")[0], f("mlstm_i_bias")[0], f("mlstm_f_bias")[0],
                         f("norm_ffn_w")[0], f("norm_final_w")])
    rvec = np.ascontiguousarray(np.broadcast_to(rv[None, :], (128, 2064)))
    subk = f("peer_sub_keys")[0].reshape(16, 128, 128)
    subk = np.ascontiguousarray(subk.transpose(2, 0, 1))
    common = {
        "w_in": w_in, "w_out": _kc(f("w_out")[0]), "xa_wq": _kc(f("xa_wq")[0]), "xa_wkv": _kc(f("xa_wkv")[0]),
        "xa_wo": _kc(f("xa_wo")[0]), "peer_wq": _kc(f("peer_wq")[0]), "subk": subk, "convw": convw, "pvec": pvec,
        "rvec": rvec, "cst": _consts(),
        "peer_uv": np.ascontiguousarray(np.concatenate([f("peer_u")[0], f("peer_v")[0]], axis=1)),
    }
    mem = f("mem")
    in_maps = []
    ncores = 2 * B
    for c in range(ncores):
        b, hf = c // 2, c % 2
        own = x[b, hf * half:(hf + 1) * half]
        prev = x[b, 0:half] if hf == 1 else np.zeros_like(own)
        m = dict(common)
        m["xs"] = np.ascontiguousarray(np.concatenate([prev, own], axis=0))
        m["mem"] = np.ascontiguousarray(mem[b])
        in_maps.append(m)
    key = (npre, nown, tuple(phases))
    if key not in _NC_CACHE:
        _NC_CACHE[key] = build(npre, nown, phases)
    nc = _NC_CACHE[key]
    res = run_bass_kernel_spmd(nc, in_maps, core_ids=list(range(ncores)))
    out = np.zeros((B, SEQ, D), np.float32)
    for c in range(ncores):
        b, hf = c // 2, c % 2
        out[b, hf * half:(hf + 1) * half] = res.results[c]["out"]
    return out


def kernel(**inputs):
    return run(inputs, 16, 16)
```
